# Optimizing a Trainium2 kernel written in Bass

```python
import math
import jax, jax.numpy as jnp
from jax import lax
import numpy as np

D_MODEL = 1024
BATCH = 32
SEQ = 256
DEPTH = 2
DEC_BATCH = 2
DEC_SEQ = 2048
PAST_LEN = 256

GRID_W = 64
HEAD_DIM = 64
ROPE_BASE = 10000.0
N_EVEN = (DEPTH + 1) // 2
N_ODD = DEPTH // 2
MIX_WIDTH = D_MODEL
A_HEADS = D_MODEL // 128
A_KV_HEADS = A_HEADS // 4
A_GROUP = A_HEADS // A_KV_HEADS
A_WIDTH = A_HEADS * HEAD_DIM
A_KV_WIDTH = A_KV_HEADS * HEAD_DIM
WINDOW = 128
BLOCK = 128
B_HEADS = D_MODEL // 128
B_WIDTH = B_HEADS * HEAD_DIM
DECAY_LORA = 64
AAA_LORA = 64
B_SHIFT_WIDTH = 3 * B_WIDTH + DECAY_LORA + AAA_LORA
C_HEADS = D_MODEL // 128
C_QK_DIM = 64
C_V_DIM = 2 * C_QK_DIM
C_WIDTH = C_HEADS * C_V_DIM
EVEN_IN = A_WIDTH + 2 * A_KV_WIDTH + A_WIDTH + B_SHIFT_WIDTH + B_WIDTH
ODD_IN = 2 * (C_HEADS * 2 * C_QK_DIM) + C_WIDTH + C_WIDTH
NORM_EPS = 1e-6
RWKV_LN_EPS = 64e-5
SUBLN_EPS = 1e-5
NEG_INF = -1e30

kernel_name = "hybrid_diffusion_prefix_step"

F32 = jnp.float32


def split_cols(p, sizes):
    idx = np.cumsum(sizes)[:-1].tolist()
    return jnp.split(p, idx, axis=-1)


def rms_norm(x, gain, eps=NORM_EPS):
    xf = x.astype(F32)
    y = xf * lax.rsqrt(jnp.mean(xf * xf, -1, keepdims=True) + eps)
    return (y * gain.astype(F32)).astype(x.dtype)


def modulation(cvec, ada_w, ada_b):
    m = jax.nn.silu(cvec) @ ada_w + ada_b
    return split_cols(m, [D_MODEL] * 3)


def axial_rope_tables(T, d):
    nf = d // 4
    inv = 1.0 / (ROPE_BASE ** (jnp.arange(nf, dtype=F32) / nf))
    t = jnp.arange(T)
    row = (t // GRID_W).astype(F32)
    col = (t % GRID_W).astype(F32)
    ang_r = row[:, None] * inv[None]
    ang_c = col[:, None] * inv[None]
    ang = jnp.concatenate([ang_r, ang_r, ang_c, ang_c], -1)
    return jnp.cos(ang), jnp.sin(ang)


def apply_rope(x, cos, sin):
    x1, x2, x3, x4 = jnp.split(x, 4, axis=-1)
    rot = jnp.concatenate([-x2, x1, -x4, x3], -1)
    return x * cos[None, :, None, :].astype(x.dtype) + rot * sin[None, :, None, :].astype(x.dtype)


def sink_softmax(s, sink):
    m = jnp.maximum(jnp.max(s, -1, keepdims=True), sink)
    e = jnp.exp(s - m)
    return e / (jnp.sum(e, -1, keepdims=True) + jnp.exp(sink - m))


def gqa_context(q, k, v, sink):
    Bn, C = q.shape[:2]
    qg = q.reshape(Bn, C, A_KV_HEADS, A_GROUP, HEAD_DIM)
    s = jnp.einsum('bqgrd,bkgd->bgrqk', qg, k, preferred_element_type=F32) * (HEAD_DIM ** -0.5)
    p = sink_softmax(s, sink.astype(F32).reshape(1, A_KV_HEADS, A_GROUP, 1, 1))
    o = jnp.einsum('bgrqk,bkgd->bqgrd', p.astype(v.dtype), v)
    return o.reshape(Bn, C, A_WIDTH)


def gqa_window_latent(q, k, v, k_ctx, v_ctx, sink):
    Bn, T = q.shape[:2]
    nb = T // BLOCK
    C = k_ctx.shape[1]
    qb = q.reshape(Bn, nb, BLOCK, A_KV_HEADS, A_GROUP, HEAD_DIM)

    def band(t):
        tp = jnp.pad(t, ((0, 0), (BLOCK, BLOCK), (0, 0), (0, 0)))
        tp = tp.reshape(Bn, nb + 2, BLOCK, A_KV_HEADS, HEAD_DIM)
        return jnp.concatenate([tp[:, :-2], tp[:, 1:-1], tp[:, 2:]], axis=2)

    kb, vb = band(k), band(v)
    blk = jnp.arange(nb)[:, None] * BLOCK
    qpos = blk + jnp.arange(BLOCK)[None]
    kpos = blk + jnp.arange(3 * BLOCK)[None] - BLOCK
    valid = ((jnp.abs(qpos[:, :, None] - kpos[:, None, :]) <= WINDOW)
             & (kpos[:, None, :] >= 0) & (kpos[:, None, :] < T))
    scale = HEAD_DIM ** -0.5
    s_loc = jnp.einsum('bnigrd,bnjgd->bngrij', qb, kb, preferred_element_type=F32) * scale
    s_loc = jnp.where(valid[None, :, None, None], s_loc, NEG_INF)
    s_ctx = jnp.einsum('bnigrd,bcgd->bngric', qb, k_ctx, preferred_element_type=F32) * scale
    s = jnp.concatenate([s_ctx, s_loc], -1)
    p = sink_softmax(s, sink.astype(F32).reshape(1, 1, A_KV_HEADS, A_GROUP, 1, 1)).astype(v.dtype)
    o = (jnp.einsum('bngric,bcgd->bnigrd', p[..., :C], v_ctx)
         + jnp.einsum('bngrij,bnjgd->bnigrd', p[..., C:], vb))
    return o.reshape(Bn, T, A_WIDTH)


def centred_shift(p):
    z = jnp.zeros_like(p[:, :1])
    prev = jnp.concatenate([z, p[:, :-1]], 1)
    nxt = jnp.concatenate([p[:, 1:], z], 1)
    return 0.5 * (prev + nxt)


def heads_b(t):
    return t.reshape(t.shape[:-1] + (B_HEADS, HEAD_DIM))


def rwkv_scan(S0, r, decay, kk, a, k, v, reverse):
    def step(S, inp):
        r_t, w_t, kk_t, a_t, k_t, v_t = inp
        sa = jnp.einsum('bhvk,bhk->bhv', S, -kk_t)
        S = (S * w_t[:, :, None, :] + sa[..., None] * (kk_t * a_t)[:, :, None, :]
             + v_t[..., None] * k_t[:, :, None, :])
        return S, jnp.einsum('bhvk,bhk->bhv', S, r_t)

    xs = tuple(jnp.moveaxis(t, 1, 0) for t in (r, decay, kk, a, k, v))
    S, ys = lax.scan(step, S0, xs, reverse=reverse)
    return S, jnp.moveaxis(ys, 0, 1)


def rwkv_mix(p, S0_fwd, S0_bwd, mu, w0, w2, a0, a2, k_k, k_a, r_k, ln_w, ln_b):
    pf = p.astype(F32)
    xf = pf + (centred_shift(pf) - pf) * mu.astype(F32)
    r, k, v, wlo, alo = split_cols(xf, [B_WIDTH] * 3 + [DECAY_LORA, AAA_LORA])
    w = -jax.nn.softplus(-(w0.astype(F32)[:, None, None, :]
                           + jnp.einsum('btl,zlc->zbtc', jnp.tanh(wlo), w2.astype(F32)))) - 0.5
    decay = jnp.exp(-jnp.exp(w))
    a = jax.nn.sigmoid(a0.astype(F32)[:, None, None, :]
                       + jnp.einsum('btl,zlc->zbtc', alo, a2.astype(F32)))
    kk = heads_b(k * k_k.astype(F32))
    kk = kk / jnp.maximum(jnp.sqrt(jnp.sum(kk * kk, -1, keepdims=True)), 1e-12)
    k_eff = k[None] * (1.0 + (a - 1.0) * k_a.astype(F32))
    rh, vh = heads_b(r), heads_b(v)
    S_f, y_f = rwkv_scan(S0_fwd.astype(F32), rh, heads_b(decay[0]), kk, heads_b(a[0]),
                         heads_b(k_eff[0]), vh, False)
    S_b, y_b = rwkv_scan(S0_bwd.astype(F32), rh, heads_b(decay[1]), kk, heads_b(a[1]),
                         heads_b(k_eff[1]), vh, True)
    y = y_f + y_b
    mean = jnp.mean(y, -1, keepdims=True)
    var = jnp.mean(jnp.square(y - mean), -1, keepdims=True)
    y = (y - mean) * lax.rsqrt(var + RWKV_LN_EPS)
    Bn, T = p.shape[:2]
    y = y.reshape(Bn, T, B_WIDTH) * ln_w.astype(F32) + ln_b.astype(F32)
    k_bonus = heads_b(0.5 * (k_eff[0] + k_eff[1]))
    bonus = jnp.sum(rh * k_bonus * heads_b(r_k.astype(F32)), -1, keepdims=True) * vh
    y = y + bonus.reshape(Bn, T, B_WIDTH)
    return y.astype(p.dtype), jnp.stack([S_f, S_b], axis=1)


def diff_attn_context(q, k, v, lam):
    s = jnp.einsum('bqhzd,bkhzd->bhzqk', q, k, preferred_element_type=F32) * (C_QK_DIM ** -0.5)
    p = jax.nn.softmax(s, -1)
    att = p[:, :, 0] - lam * p[:, :, 1]
    return jnp.einsum('bhqk,bkhd->bqhd', att.astype(v.dtype), v)


def diff_attn_latent(q, k, v, k_ctx, v_ctx, lam):
    Bn, T = q.shape[:2]
    nb = T // BLOCK
    k_all = jnp.concatenate([k_ctx, k], 1)
    v_all = jnp.concatenate([v_ctx, v], 1)
    qb = jnp.moveaxis(q.reshape(Bn, nb, BLOCK, C_HEADS, 2, C_QK_DIM), 1, 0)

    def one_block(qblk):
        s = jnp.einsum('bqhzd,bkhzd->bhzqk', qblk, k_all, preferred_element_type=F32) * (C_QK_DIM ** -0.5)
        p = jax.nn.softmax(s, -1)
        att = p[:, :, 0] - lam * p[:, :, 1]
        return jnp.einsum('bhqk,bkhd->bqhd', att.astype(v.dtype), v_all)

    o = lax.map(one_block, qb)
    return jnp.moveaxis(o, 0, 1).reshape(Bn, T, C_HEADS, C_V_DIM)


def even_layer(x, shift, scale, gate, norm_pre, norm_post, w_out, w_in, sink, mu, w0, w2, a0, a2,
               k_k, k_a, r_k, ln_w, ln_b, ctx):
    Bn, T = x.shape[:2]
    h = rms_norm(x, norm_pre) * (1.0 + scale) + shift
    qa, ka, va, ga, pb, gb = split_cols(h @ w_in, [A_WIDTH, A_KV_WIDTH, A_KV_WIDTH, A_WIDTH,
                                                    B_SHIFT_WIDTH, B_WIDTH])
    qa = qa.reshape(Bn, T, A_HEADS, HEAD_DIM)
    ka = ka.reshape(Bn, T, A_KV_HEADS, HEAD_DIM)
    va = va.reshape(Bn, T, A_KV_HEADS, HEAD_DIM)
    if ctx is None:
        ya = gqa_context(qa, ka, va, sink)
        zeros = jnp.zeros((Bn, B_HEADS, HEAD_DIM, HEAD_DIM), F32)
        yb, st = rwkv_mix(pb, zeros, zeros, mu, w0, w2, a0, a2, k_k, k_a, r_k, ln_w, ln_b)
        new = (ka, va, st)
    else:
        k_ctx, v_ctx, st0 = ctx
        cos, sin = axial_rope_tables(T, HEAD_DIM)
        ya = gqa_window_latent(apply_rope(qa, cos, sin), apply_rope(ka, cos, sin), va,
                               k_ctx, v_ctx, sink)
        yb, _ = rwkv_mix(pb, st0[:, 0], st0[:, 1], mu, w0, w2, a0, a2, k_k, k_a, r_k, ln_w, ln_b)
        new = None
    y = jnp.concatenate([ya * jax.nn.silu(ga), yb * jax.nn.silu(gb)], -1) @ w_out
    return x + gate * rms_norm(y, norm_post), new


def odd_layer(x, shift, scale, gate, norm_pre, norm_post, w_out, w_in, lq1, lk1, lq2, lk2, subln,
              lam_init, ctx):
    Bn, T = x.shape[:2]
    h = rms_norm(x, norm_pre) * (1.0 + scale) + shift
    qc, kc, vc, gc = split_cols(h @ w_in, [C_HEADS * 2 * C_QK_DIM, C_HEADS * 2 * C_QK_DIM,
                                           C_WIDTH, C_WIDTH])
    qc = qc.reshape(Bn, T, C_HEADS, 2, C_QK_DIM)
    kc = kc.reshape(Bn, T, C_HEADS, 2, C_QK_DIM)
    vc = vc.reshape(Bn, T, C_HEADS, C_V_DIM)
    lam = (jnp.exp(jnp.sum((lq1 * lk1).astype(F32))) - jnp.exp(jnp.sum((lq2 * lk2).astype(F32)))
           + lam_init)
    if ctx is None:
        o = diff_attn_context(qc, kc, vc, lam)
        new = (kc.reshape(Bn, T, C_HEADS, 2 * C_QK_DIM), vc)
    else:
        k_ctx, v_ctx = ctx
        C = k_ctx.shape[1]
        cos, sin = axial_rope_tables(T, C_QK_DIM)
        qr = apply_rope(qc.reshape(Bn, T, C_HEADS * 2, C_QK_DIM), cos, sin).reshape(qc.shape)
        kr = apply_rope(kc.reshape(Bn, T, C_HEADS * 2, C_QK_DIM), cos, sin).reshape(kc.shape)
        o = diff_attn_latent(qr, kr, vc, k_ctx.reshape(Bn, C, C_HEADS, 2, C_QK_DIM), v_ctx, lam)
        new = None
    o = rms_norm(o, subln, SUBLN_EPS) * (1.0 - lam_init)
    y = (o.reshape(Bn, T, C_WIDTH) * jax.nn.silu(gc)) @ w_out
    return x + gate * rms_norm(y, norm_post), new


def setup_inputs(seed: int = 0) -> dict:
    key = jax.random.key(seed)
    ks = jax.random.split(key, 40)

    def nrm(k, shape, s):
        return jax.random.normal(k, shape, F32) * s

    D = D_MODEL
    return {
        "x_prompt": nrm(ks[0], (BATCH, SEQ, D), 1.0),
        "x_sample": nrm(ks[1], (DEC_BATCH, DEC_SEQ, D), 1.0),
        "cache_a_k": nrm(ks[2], (DEC_BATCH, N_EVEN, PAST_LEN, A_KV_HEADS, HEAD_DIM), 1.0),
        "cache_a_v": nrm(ks[3], (DEC_BATCH, N_EVEN, PAST_LEN, A_KV_HEADS, HEAD_DIM), 1.0),
        "state_rwkv": nrm(ks[4], (DEC_BATCH, N_EVEN, 2, B_HEADS, HEAD_DIM, HEAD_DIM), 0.5),
        "cache_c_k": nrm(ks[5], (DEC_BATCH, N_ODD, PAST_LEN, C_HEADS, 2 * C_QK_DIM), 1.0),
        "cache_c_v": nrm(ks[6], (DEC_BATCH, N_ODD, PAST_LEN, C_HEADS, C_V_DIM), 1.0),
        "c": nrm(ks[7], (DEC_BATCH, D), 1.0),
        "c_ctx": nrm(ks[8], (D,), 1.0),
        "ada_w": nrm(ks[9], (DEPTH, D, 3 * D), 0.5 * D ** -0.5),
        "ada_b": nrm(ks[10], (DEPTH, 3 * D), 0.02),
        "norm_pre": 1.0 + nrm(ks[11], (DEPTH, D), 0.05),
        "norm_post": 1.0 + nrm(ks[12], (DEPTH, D), 0.05),
        "w_out": nrm(ks[13], (DEPTH, MIX_WIDTH, D), MIX_WIDTH ** -0.5),
        "even_w_in": nrm(ks[14], (N_EVEN, D, EVEN_IN), D ** -0.5),
        "a_sink": nrm(ks[15], (N_EVEN, A_HEADS), 0.5),
        "b_mu": jax.random.uniform(ks[16], (N_EVEN, B_SHIFT_WIDTH), F32),
        "b_w0": -2.0 + nrm(ks[17], (N_EVEN, 2, B_WIDTH), 0.5),
        "b_w2": nrm(ks[18], (N_EVEN, 2, DECAY_LORA, B_WIDTH), 0.1 * DECAY_LORA ** -0.5),
        "b_a0": nrm(ks[19], (N_EVEN, 2, B_WIDTH), 0.1),
        "b_a2": nrm(ks[20], (N_EVEN, 2, AAA_LORA, B_WIDTH), AAA_LORA ** -0.5),
        "b_kk": 0.85 + nrm(ks[21], (N_EVEN, B_WIDTH), 0.05),
        "b_ka": 1.0 + nrm(ks[22], (N_EVEN, B_WIDTH), 0.05),
        "b_rk": nrm(ks[23], (N_EVEN, B_WIDTH), 0.1),
        "b_ln_w": 1.0 + nrm(ks[24], (N_EVEN, B_WIDTH), 0.05),
        "b_ln_b": nrm(ks[25], (N_EVEN, B_WIDTH), 0.02),
        "odd_w_in": nrm(ks[26], (N_ODD, D, ODD_IN), D ** -0.5),
        "c_lq1": nrm(ks[27], (N_ODD, C_QK_DIM), 0.1),
        "c_lk1": nrm(ks[28], (N_ODD, C_QK_DIM), 0.1),
        "c_lq2": nrm(ks[29], (N_ODD, C_QK_DIM), 0.1),
        "c_lk2": nrm(ks[30], (N_ODD, C_QK_DIM), 0.1),
        "c_subln": 1.0 + nrm(ks[31], (N_ODD, C_V_DIM), 0.05),
    }


def reference(x_prompt, x_sample, cache_a_k, cache_a_v, state_rwkv, cache_c_k, cache_c_v, c, c_ctx,
              ada_w, ada_b, norm_pre, norm_post, w_out, even_w_in, a_sink, b_mu, b_w0, b_w2, b_a0,
              b_a2, b_kk, b_ka, b_rk, b_ln_w, b_ln_b, odd_w_in, c_lq1, c_lk1, c_lq2, c_lk2, c_subln):
    y_p, y_s = x_prompt, x_sample
    new_ak, new_av, new_st, new_ck, new_cv = [], [], [], [], []
    for layer in range(DEPTH):
        sh_c, sc_c, g_c = modulation(c_ctx, ada_w[layer], ada_b[layer])
        sh_s, sc_s, g_s = modulation(c[:, None, :], ada_w[layer], ada_b[layer])
        common = (norm_pre[layer], norm_post[layer], w_out[layer])
        if layer % 2 == 0:
            e = layer // 2
            ep = (even_w_in[e], a_sink[e], b_mu[e], b_w0[e], b_w2[e], b_a0[e], b_a2[e], b_kk[e],
                  b_ka[e], b_rk[e], b_ln_w[e], b_ln_b[e])
            y_p, (k_a, v_a, st) = even_layer(y_p, sh_c, sc_c, g_c, *common, *ep, None)
            y_s, _ = even_layer(y_s, sh_s, sc_s, g_s, *common, *ep,
                                (cache_a_k[:, e], cache_a_v[:, e], state_rwkv[:, e]))
            new_ak.append(k_a)
            new_av.append(v_a)
            new_st.append(st)
        else:
            o = layer // 2
            lam_init = 0.8 - 0.6 * math.exp(-0.3 * layer)
            op = (odd_w_in[o], c_lq1[o], c_lk1[o], c_lq2[o], c_lk2[o], c_subln[o], lam_init)
            y_p, (k_c, v_c) = odd_layer(y_p, sh_c, sc_c, g_c, *common, *op, None)
            y_s, _ = odd_layer(y_s, sh_s, sc_s, g_s, *common, *op, (cache_c_k[:, o], cache_c_v[:, o]))
            new_ck.append(k_c)
            new_cv.append(v_c)
    new_cache_a_k = jnp.stack(new_ak, axis=1)
    new_cache_a_v = jnp.stack(new_av, axis=1)
    new_state_rwkv = jnp.stack(new_st, axis=1)
    new_cache_c_k = jnp.stack(new_ck, axis=1)
    new_cache_c_v = jnp.stack(new_cv, axis=1)
    return (y_p, y_s, new_cache_a_k, new_cache_a_v, new_state_rwkv, new_cache_c_k, new_cache_c_v)
```

```python
import math
from contextlib import ExitStack, contextmanager
import numpy as np
import concourse.bass as bass
import concourse.mybir as mybir
from concourse.bass_utils import run_bass_kernel_spmd

F32, BF16, F16 = mybir.dt.float32, mybir.dt.bfloat16, mybir.dt.float16
ALU, AF, AX = mybir.AluOpType, mybir.ActivationFunctionType, mybir.AxisListType

D = 1024
NPS = 4
PT = 2
ST = 16
MYT = 4
EVEN_IN = 3456
ODD_IN = 4096
CW = -math.exp(-0.5)
LAM_INIT = 0.8 - 0.6 * math.exp(-0.3 * 1)
NDMA = 12
NDMA_Q = {"sp": 12, "pool": 4}


class T:
    __slots__ = ("ap", "w", "r", "x")

    def __init__(self, ap, x=False):
        self.ap = ap
        self.w = None
        self.r = {}
        self.x = x

    def __getitem__(self, k):
        return self.ap[k]


class Kern:
    def __init__(self, plan=None):
        self.plan = plan
        self.needed = set()
        self.rank = {}
        self.val = {}
        nc = self.nc = bass.Bass("TRN2", target_bir_lowering=False)
        self.h = {"pe": nc.tensor, "act": nc.scalar, "dve": nc.vector, "pool": nc.gpsimd, "sp": nc.sync}
        self.sems = {k: nc.alloc_semaphore("s_" + k) for k in self.h}
        self.cnt = {k: 0 for k in self.h}
        self.seen = {k: {} for k in self.h}
        self.dval = {}
        self.dnext = {"sp": 0, "pool": 0}
        for q in ("sp", "pool"):
            for i in range(NDMA):
                n = f"d{q}{i}"
                self.sems[n] = nc.alloc_semaphore(n)
                self.dval[n] = 0
        self.banks = [T(nc.alloc_psum_tensor(f"bank{i}", [128, 512], F32).ap(), x=True) for i in range(8)]
        self.ring = list(range(8))
        self.rpos = 0
        self.nsb = 0
        self.stacks = []
        self.order = {}
        self.nev = 0
        self.snap = {}
        self.pool_compute = False

    def sb(self, shape, dt=F32, name=None):
        self.nsb += 1
        nm = f"{name or 't'}_{self.nsb}"
        if self.stacks:
            h = self.stacks[-1].enter_context(self.nc.sbuf_tensor(nm, list(shape), dt))
        else:
            h = self.nc.alloc_sbuf_tensor(nm, list(shape), dt)
        return T(h.ap())

    @contextmanager
    def scope(self):
        st = ExitStack()
        self.stacks.append(st)
        try:
            yield
        finally:
            self.barrier()
            self.stacks.pop()
            st.close()

    def ps(self):
        n = len(self.ring)
        for _ in range(n):
            b = self.banks[self.ring[self.rpos % n]]
            self.rpos += 1
            if not (b.w is not None and not b.r):
                return b
        raise RuntimeError("no free PSUM bank")

    def pin(self, n):
        free = [i for i in self.ring if not (self.banks[i].w is not None and not self.banks[i].r)]
        assert len(free) >= n, "not enough free PSUM banks to pin"
        out = free[-n:]
        for i in out:
            self.ring.remove(i)
        return [self.banks[i] for i in out], out

    def unpin(self, idx):
        self.ring.extend(idx)

    def _deps(self, R, W, en=None):
        deps = {}

        def add(s, v):
            if deps.get(s, 0) < v:
                deps[s] = v
        for t in R:
            if t.w:
                add(*t.w)
            if t.x:
                for s, v in t.r.items():
                    if s != en:
                        add(s, v)
        for t in W:
            if t.w:
                add(*t.w)
            for s, v in t.r.items():
                add(s, v)
        return deps

    def _wait(self, en, deps):
        seen = self.seen[en]
        need = []
        for s, v in sorted(deps.items(), key=lambda kv: -self.order.get(kv, 0)):
            if en == "pe" and s == "pe":
                continue
            if seen.get(s, 0) < v:
                need.append((s, v))
                seen[s] = v
                for s2, v2 in self.snap.get((s, v), {}).items():
                    if seen.get(s2, 0) < v2:
                        seen[s2] = v2
        for s, v in need:
            self.needed.add((s, v))
        for s, v in need[:-1]:
            self.h[en].wait_ge(self.sems[s], self.sv(s, v))
        return need[-1] if need else None

    def sv(self, s, v):
        if self.plan is None or s not in self.h:
            return v
        return self.val[(s, v)]

    def _mark(self, ev, R, W):
        for t in R:
            if t.r.get(ev[0], 0) < ev[1]:
                t.r[ev[0]] = ev[1]
        for t in W:
            t.w = ev
            t.r = {}

    def op(self, en, fn, R=(), W=()):
        if en == "gp":
            en = "pool"
        elif en == "pool" and not self.pool_compute:
            en = "dve"
        fw = self._wait(en, self._deps(R, W, en))
        inst = fn(self.h[en])
        if fw is not None:
            inst._wait_ge(self.sems[fw[0]], self.sv(*fw))
        self.cnt[en] += 1
        ev = (en, self.cnt[en])
        if self.plan is None or ev in self.plan:
            inst.then_inc(self.sems[en], 1)
            self.rank[en] = self.rank.get(en, 0) + 1
            self.val[ev] = self.rank[en]
        sn_ = dict(self.seen[en])
        if en != "pe":
            sn_[en] = max(sn_.get(en, 0), 0)
        self.snap[ev] = sn_
        self.nev += 1
        self.order[ev] = self.nev
        self._mark(ev, R, W)

    def dma(self, q, out, in_, R=(), W=(), **kw):
        slot = self.dnext[q]
        self.dnext[q] = (slot + 1) % NDMA_Q[q]
        sn = f"d{q}{slot}"
        deps = self._deps(R, W)
        if self.dval[sn] > 0 and deps.get(sn, 0) < self.dval[sn]:
            deps[sn] = self.dval[sn]
        fw = self._wait(q, deps)
        inst = self.h[q].dma_start(out=out, in_=in_, **kw)
        if fw is not None:
            inst._wait_ge(self.sems[fw[0]], self.sv(*fw))
        self.dval[sn] += 16
        inst.then_inc(self.sems[sn], 16)
        ev = (sn, self.dval[sn])
        self.snap[ev] = dict(self.seen[q])
        self.nev += 1
        self.order[ev] = self.nev
        self._mark(ev, R, W)

    def gather(self, out_t, out_ap, in_ap, idx_t, idx_ap, R=()):
        q = "pool"
        slot = self.dnext[q]
        self.dnext[q] = (slot + 1) % NDMA_Q[q]
        sn = f"d{q}{slot}"
        deps = self._deps(list(R) + [idx_t], [out_t], q)
        if self.dval[sn] > 0 and deps.get(sn, 0) < self.dval[sn]:
            deps[sn] = self.dval[sn]
        fw = self._wait(q, deps)
        if fw is not None:
            self.h[q].wait_ge(self.sems[fw[0]], self.sv(*fw))
        inst = self.h[q].indirect_dma_start(out=out_ap, out_offset=None, in_=in_ap,
                                            in_offset=bass.IndirectOffsetOnAxis(ap=idx_ap, axis=0))
        self.dval[sn] += 16
        inst.then_inc(self.sems[sn], 16)
        ev = (sn, self.dval[sn])
        self.snap[ev] = dict(self.seen[q])
        self.nev += 1
        self.order[ev] = self.nev
        self._mark(ev, list(R) + [idx_t], [out_t])

    def load(self, out_t, out_ap, in_ap, R=(), **kw):
        self.dma("sp", out_ap, in_ap, R=R, W=[out_t], **kw)

    def store(self, out_ap, in_t, in_ap, W=(), **kw):
        self.dma("pool", out_ap, in_ap, R=[in_t], W=W, **kw)

    def barrier(self):
        deps = {k: self.cnt[k] for k in ("pe", "act", "dve", "pool") if self.cnt[k] > 0}
        for n, v in self.dval.items():
            if v > 0:
                deps[n] = v
        for en in self.h:
            seen = self.seen[en]
            for s, v in deps.items():
                if seen.get(s, 0) < v:
                    self.needed.add((s, v))
                    self.h[en].wait_ge(self.sems[s], self.sv(s, v))
                    seen[s] = v

    def mm(self, out_t, out_ap, lhsT, rhs, start=True, stop=True, R=()):
        self.op("pe", lambda e: e.matmul(out_ap, lhsT=lhsT, rhs=rhs, start=start, stop=stop), R=R, W=[out_t])

    def tr(self, out_t, out_ap, in_ap, ident_ap, R=()):
        self.op("pe", lambda e: e.transpose(out_ap, in_ap, ident_ap), R=R, W=[out_t])

    def tt(self, en, out_t, out_ap, a, b, op, R=()):
        self.op(en, lambda e: e.tensor_tensor(out=out_ap, in0=a, in1=b, op=op), R=R, W=[out_t])

    def stt(self, en, out_t, out_ap, a, s, b, op0, op1, R=()):
        en = "dve"
        self.op(en, lambda e: e.scalar_tensor_tensor(out=out_ap, in0=a, scalar=s, in1=b, op0=op0, op1=op1), R=R, W=[out_t])

    def ts(self, en, out_t, out_ap, a, s1, s2, op0, op1=None, R=()):
        if op1 is None:
            self.op(en, lambda e: e.tensor_scalar(out=out_ap, in0=a, scalar1=s1, scalar2=None, op0=op0), R=R, W=[out_t])
        else:
            self.op(en, lambda e: e.tensor_scalar(out=out_ap, in0=a, scalar1=s1, scalar2=s2, op0=op0, op1=op1), R=R, W=[out_t])

    def cp(self, en, out_t, out_ap, a, R=()):
        if en == "act":
            self.op(en, lambda e: e.copy(out=out_ap, in_=a), R=R, W=[out_t])
        else:
            self.op(en, lambda e: e.tensor_copy(out=out_ap, in_=a), R=R, W=[out_t])

    def act(self, out_t, out_ap, a, func, R=(), Wx=(), **kw):
        self.op("act", lambda e: e.activation(out=out_ap, in_=a, func=func, **kw), R=R, W=[out_t] + list(Wx))

    def red(self, en, out_t, out_ap, a, R=(), op=ALU.add):
        self.op(en, lambda e: e.tensor_reduce(out=out_ap, in_=a, axis=AX.X, op=op), R=R, W=[out_t])

    def rcp(self, out_t, out_ap, a, R=()):
        self.op("dve", lambda e: e.reciprocal(out=out_ap, in_=a), R=R, W=[out_t])

    def ms(self, en, out_t, out_ap, v):
        self.op(en, lambda e: e.memset(out_ap, v), W=[out_t])

    def finish(self):
        self.barrier()


class Ring:
    def __init__(self, k, n, shape, dt=F32, name=None):
        self.t = [k.sb(shape, dt, name=(f"{name}{i}" if name else None)) for i in range(n)]
        self.i = 0

    def next(self):
        t = self.t[self.i % len(self.t)]
        self.i += 1
        return t


def v3(ap, a):
    return ap.rearrange("p (a b) -> p a b", a=a)


def pump(gen, n=1):
    if gen is None:
        return
    for _ in range(n):
        try:
            next(gen)
        except StopIteration:
            return


def exhaust(gen):
    if gen is None:
        return
    for _ in gen:
        pass


def _consts():
    i = np.arange(128)
    r, c = i[:, None], i[None, :]
    ident = (r == c)
    U = (r < c)
    Lw = (r > c)
    Ui = (r <= c)
    Li = (r >= c)
    Pm = np.zeros((64, 64), np.float32)
    for a in range(16):
        Pm[a, a + 16] = -1.0
        Pm[a + 16, a] = 1.0
        Pm[a + 32, a + 48] = -1.0
        Pm[a + 48, a + 32] = 1.0
    PmT = np.zeros((128, 128), np.float32)
    PmT[:64, :64] = Pm.T
    PmT[64:, 64:] = Pm.T
    ones = np.ones((128, 128), np.float32)
    sel = np.zeros((128, 256), np.float32)
    sel[0, 0:128] = 1.0
    sel[1, 128:256] = 1.0
    cst = np.concatenate([x.astype(np.float32) for x in (ident, U, Lw, Ui, Li, PmT, ones, sel)], axis=1)
    nf = 16
    inv = 1.0 / (10000.0 ** (np.arange(nf, dtype=np.float32) / nf))
    t = np.arange(2048)
    row = (t // 64).astype(np.float32)
    col = (t % 64).astype(np.float32)
    ang_r = row[:, None] * inv[None]
    ang_c = col[:, None] * inv[None]
    ang = np.concatenate([ang_r, ang_r, ang_c, ang_c], -1).astype(np.float32)
    cos, sin = np.cos(ang).T, np.sin(ang).T
    rope = np.zeros((128, 2, 2048), np.float32)
    rope[:64, 0], rope[64:, 0] = cos, cos
    rope[:64, 1], rope[64:, 1] = sin, sin
    return np.ascontiguousarray(cst), rope


C_ID, C_U, C_LW, C_UI, C_LI, C_PM, C_ONE, C_SEL = [128 * i for i in range(8)]
NCST = 128 * 9


def build(debug=False, stop_after=None, plan=None):
    k = Kern(plan)
    nc = k.nc

    def din(name, shape):
        return nc.dram_tensor(name, list(shape), F32, kind="ExternalInput").ap()

    def dout(name, shape):
        return nc.dram_tensor(name, list(shape), F32, kind="ExternalOutput").ap()

    def dscr(name, shape, dt=F32, dbg=False):
        if dbg and debug:
            return nc.dram_tensor(name, list(shape), dt, kind="ExternalOutput").ap()
        return nc.dram_tensor(name, list(shape), dt).ap()

    I = {}
    for name, shape in [
        ("xp", (NPS * 256, D)), ("xs", (2048, D)), ("cak", (256, 128)), ("cav", (256, 128)),
        ("st0", (2, 8, 64, 64)), ("cck", (256, 1024)), ("ccv", (256, 1024)), ("cvec", (2, D)),
        ("ada_w", (2, D, 3 * D)), ("ada_b", (2, 3 * D)), ("norm_pre", (2, D)), ("norm_post", (2, D)),
        ("w_out", (2, D, D)), ("w_in0", (D, EVEN_IN)), ("a_sink", (1, 8)), ("b_mu", (1, 1664)),
        ("b_w0", (2, 512)), ("b_w2", (2, 64, 512)), ("b_a0", (2, 512)), ("b_a2", (2, 64, 512)),
        ("b_kk", (1, 512)), ("b_ka", (1, 512)), ("b_rk", (1, 512)), ("b_ln_w", (1, 512)), ("b_ln_b", (1, 512)),
        ("w_in1", (D, ODD_IN)), ("c_lq1", (1, 64)), ("c_lk1", (1, 64)), ("c_lq2", (1, 64)), ("c_lk2", (1, 64)),
        ("c_subln", (1, 128)), ("cst", (128, NCST)), ("rope", (128, 2, 2048)), ("rope_my", (128, 2, 512)),
    ]:
        I[name] = din(name, shape)
    I["myrows"] = nc.dram_tensor("myrows", [128, 4], mybir.dt.int32, kind="ExternalInput").ap()
    O = {}
    for name, shape in [("yp", (NPS * 256, D)), ("ys", (MYT * 128, D)), ("nak", (NPS, 256, 128)), ("nav", (NPS, 256, 128)),
                        ("nst", (NPS, 2, 8, 64, 64)), ("nck", (NPS, 256, 1024)), ("ncv", (NPS, 256, 1024))]:
        O[name] = dout(name, shape)

    NT = NPS * PT + ST
    hT_s = dscr("hT_s", (NT, 128, 1024), BF16)
    hsT_s = dscr("hsT_s", (NT, 128, 1024), BF16)
    xf_s = dscr("xf_s", (NT, 128, 1536), F32)
    lora_s = dscr("lora_s", (NT, 128, 128), F32)
    yf_s = dscr("yf_s", (NT, 128, 512), F32, dbg=True)
    yb_s = dscr("yb_s", (NT, 128, 512), F32)
    x1_s = dscr("x1_s", (NT * 128, D), F32, dbg=True)
    DT = {}

    def dt_(name, i):
        key = (name, i)
        if key not in DT:
            DT[key] = T(None)
        return DT[key]

    seqs = []
    for s in range(NPS):
        seqs.append(dict(n=PT, x=I["xp"][s * 256:(s + 1) * 256, :], t0=s * PT, g=0, kc0=2304 + 256 * s, sample=False, s=s))
    seqs.append(dict(n=ST, x=I["xs"], t0=NPS * PT, g=1, kc0=256, sample=True, s=None))

    cst = k.sb([128, NCST], F32, "cst")
    k.load(cst, cst.ap, I["cst"])
    cws = k.sb([128, 5 * 128], F32, "cws")
    for i, off in enumerate((C_UI, C_LI, C_U, C_LW, C_ONE)):
        k.ts("dve", cws, cws[:, i * 128:(i + 1) * 128], cst[:, off:off + 128], CW, None, ALU.mult, R=[cst])
    W_UI, W_LI, W_U, W_LW, W_ONE = [cws[:, i * 128:(i + 1) * 128] for i in range(5)]
    cI = [W_UI, W_LI]
    cE = [W_U, W_LW]
    cR = [W_LW, W_U]
    M4 = [k.sb([128, 512], F32, f"M4_{z}") for z in range(2)]
    for z, (a, b) in enumerate(((C_U, C_UI), (C_LW, C_LI))):
        for q in range(2):
            k.cp("dve", M4[z], M4[z][:, q * 256:q * 256 + 128], cst[:, a:a + 128], R=[cst])
            k.cp("dve", M4[z], M4[z][:, q * 256 + 128:q * 256 + 256], cst[:, b:b + 128], R=[cst])
    mS = [cst[:, C_LW:C_LW + 128], cst[:, C_U:C_U + 128]]
    identb = k.sb([128, 128], BF16, "identb")
    k.cp("dve", identb, identb.ap, cst[:, C_ID:C_ID + 128], R=[cst])
    pmtb = k.sb([128, 128], BF16, "pmtb")
    k.cp("dve", pmtb, pmtb.ap, cst[:, C_PM:C_PM + 128], R=[cst])
    identf = cst[:, C_ID:C_ID + 128]
    mhalf = k.sb([128, 16], F32, "mhalf")
    k.ms("dve", mhalf, mhalf.ap, -0.5)

    def bank_bf(b):
        return b.ap.bitcast(BF16)

    def load_bc(dst, src_row_ap, n=128):
        k.load(dst, dst.ap, src_row_ap.partition_broadcast(n))

    def modulation(l, Sh, A, G):
        with k.scope():
            cT = k.sb([128, 8, 2], F32, "cT")
            for g in range(2):
                cg = k.sb([8, 128], F32, "cg")
                k.load(cg, cg.ap, I["cvec"][g].rearrange("(c p) -> c p", p=128))
                psc = k.ps()
                k.mm(psc, psc[:, 0:8], cg.ap, cst[0:8, C_ID:C_ID + 8], R=[cg, cst])
                k.cp("dve", cT, cT[:, :, g], psc[:, 0:8], R=[psc])
            cTs = k.sb([128, 8, 2], F32, "cTs")
            k.act(cTs, cTs.ap, cT.ap, AF.Silu, R=[cT])
            stg = Ring(k, 2, [128, 3 * D], F32, "adastg")
            adab = k.sb([2, 3 * D], F32, "adab")
            load_bc(adab, I["ada_b"][l:l + 1, :], 2)
            npre = k.sb([128, D], F32, "npre")
            npost = k.sb([128, D], F32, "npost")
            load_bc(npre, I["norm_pre"][l:l + 1, :])
            load_bc(npost, I["norm_post"][l:l + 1, :])
            banks, idx = k.pin(6)
            for kc in range(8):
                st = stg.next()
                k.load(st, st.ap, I["ada_w"][l, kc * 128:(kc + 1) * 128, :])
                for nb in range(6):
                    k.mm(banks[nb], banks[nb][0:2, :], cTs[:, kc, :], st[:, nb * 512:(nb + 1) * 512],
                         start=(kc == 0), stop=(kc == 7), R=[cTs, st])
            mod2 = k.sb([2, 3 * D], F32, "mod2")
            for nb in range(6):
                k.tt("dve", mod2, mod2[0:2, nb * 512:(nb + 1) * 512], banks[nb][0:2, :], adab[0:2, nb * 512:(nb + 1) * 512],
                     ALU.add, R=[banks[nb], adab])
            k.unpin(idx)
            for g in range(2):
                for nb in range(6):
                    ps = k.ps()
                    k.mm(ps, ps.ap, cst[0:2, C_SEL + g * 128:C_SEL + (g + 1) * 128], mod2[0:2, nb * 512:(nb + 1) * 512], R=[cst, mod2])
                    cs = slice((nb % 2) * 512, (nb % 2) * 512 + 512)
                    if nb < 2:
                        k.cp("act", Sh[g], Sh[g][:, cs], ps.ap, R=[ps])
                    elif nb < 4:
                        k.stt("dve", A[g], A[g][:, cs], ps.ap, 1.0, npre[:, cs], ALU.add, ALU.mult, R=[ps, npre])
                    else:
                        k.tt("dve", G[g], G[g][:, cs], ps.ap, npost[:, cs], ALU.mult, R=[ps, npost])

    def load_cols(dsts, w_ap, c0, ncols, stg):
        for kc in range(8):
            st = stg.next()
            k.load(st, st[:, 0:ncols], w_ap[kc * 128:(kc + 1) * 128, c0:c0 + ncols])
            for i, (dt, fn, sc) in enumerate(dsts):
                en = "dve" if (kc + i) % 2 == 0 else "pool"
                if sc is None:
                    k.cp(en, dt, fn(kc), st[:, 0:ncols], R=[st])
                else:
                    k.tt(en, dt, fn(kc), st[:, 0:ncols], sc[:, 0:ncols], ALU.mult, R=[st, sc])


    mod_s = dscr("mod_s", (2, 2, 3, 128, D), F32)
    mod_t = T(None)

    def modulation_to_dram(l):
        with k.scope():
            Sh = [k.sb([128, D], F32, "Sh") for _ in range(2)]
            A = [k.sb([128, D], F32, "A") for _ in range(2)]
            G = [k.sb([128, D], F32, "G") for _ in range(2)]
            modulation(l, Sh, A, G)
            for g in range(2):
                for i, t in enumerate((Sh[g], A[g], G[g])):
                    k.store(mod_s[l, g, i], t, t.ap, W=[mod_t])

    def load_cols2(dsts, w_ap, c0, ncols, stg):
        for kc in range(8):
            st = stg.next()
            k.load(st, st[:, 0:ncols], w_ap[kc * 128:(kc + 1) * 128, c0:c0 + ncols])
            for i, (dt, fn, sc, (a, b)) in enumerate(dsts):
                en = "dve" if (kc + i) % 2 == 0 else "act"
                if sc is None:
                    k.cp(en, dt, fn(kc), st[:, a:b], R=[st])
                else:
                    k.tt("dve", dt, fn(kc), st[:, a:b], sc[:, a:b], ALU.mult, R=[st, sc])

    def run2(gens, width=2):
        active = []
        it = iter(gens)
        while True:
            while len(active) < width:
                g = next(it, None)
                if g is None:
                    break
                active.append(g)
            if not active:
                break
            for g in list(active):
                try:
                    next(g)
                except StopIteration:
                    active.remove(g)


    def rstd_of(ss, junk, parts, n_el, eps):
        k.ms("dve", ss, ss.ap, 0.0)
        for i, (t, ap) in enumerate(parts):
            k.act(junk, junk[:, 0:ap.shape[1]], ap, AF.Square, R=[t, ss], Wx=[ss], scale=float(n_el) ** -0.5,
                  accum_out=ss[:, i:i + 1])
        if len(parts) == 2:
            k.tt("dve", ss, ss[:, 0:1], ss[:, 0:1], ss[:, 1:2], ALU.add, R=[ss])
        k.ts("dve", ss, ss[:, 4:5], ss[:, 0:1], eps, None, ALU.add, R=[ss])
        k.tt("gp", ss, ss[:, 6:7], ss[:, 4:5], mhalf[:, 0:1], ALU.pow, R=[ss, mhalf])
        return ss[:, 6:7]

    def layer0():
        with k.scope():
            K2T = k.sb([128, 2, 3328], BF16, "K2T")
            Vall = k.sb([128, 26 * 2 * 65], BF16, "Vall")
            Vv = Vall.ap.rearrange("p (b g e) -> p b g e", b=26, g=2)
            k.ms("pool", Vall, Vall.ap, 1.0)
            with k.scope():
                for blk in range(2):
                    cvt = k.sb([128, 128], F32, "cvt")
                    k.load(cvt, cvt.ap, I["cav"][blk * 128:(blk + 1) * 128, :])
                    k.cp("dve", Vall, Vv[:, blk, :, 0:64], v3(cvt.ap, 2), R=[cvt])
                    ckt = k.sb([128, 128], F32, "ckt")
                    k.load(ckt, ckt.ap, I["cak"][blk * 128:(blk + 1) * 128, :])
                    kd = k.sb([128, 256], BF16, "kd")
                    for g in range(2):
                        for r in range(2):
                            k.cp("dve", kd, kd[:, g * 128 + r * 64:g * 128 + r * 64 + 64], ckt[:, g * 64:(g + 1) * 64], R=[ckt])
                    ps = k.ps()
                    pb = bank_bf(ps)
                    for g in range(2):
                        k.tr(ps, pb[:, g * 128:(g + 1) * 128], kd[:, g * 128:(g + 1) * 128], identb.ap, R=[kd, identb])
                    for g in range(2):
                        k.cp("act", K2T, K2T[:, g, blk * 128:(blk + 1) * 128], pb[:, g * 128:(g + 1) * 128], R=[ps])
            phaseF(K2T, Vall, Vv)
            if stop_after in ("F", "F1", "F2a"):
                return
            phaseB(K2T, Vall, Vv)

    def rw_consts(names):
        out = {}
        for nm in names:
            if nm in ("kkv", "kav", "rkv", "lnw", "lnb"):
                src = {"kkv": "b_kk", "kav": "b_ka", "rkv": "b_rk", "lnw": "b_ln_w", "lnb": "b_ln_b"}[nm]
                t = k.sb([128, 512], F32, nm)
                load_bc(t, I[src][0:1, :])
            elif nm[:2] in ("w0", "a0"):
                z = int(nm[2])
                t = k.sb([128, 512], F32, nm)
                load_bc(t, I["b_" + nm[:2]][z:z + 1, :])
            elif nm[:2] == "w2":
                z = int(nm[2])
                t = k.sb([128, 512], F32, nm)
                k.load(t, t[0:64, :], I["b_w2"][z])
            elif nm[:2] == "a2":
                z = int(nm[2])
                t = k.sb([128, 512], F32, nm)
                k.load(t, t[64:128, :], I["b_a2"][z])
            out[nm] = t
        return out

    def rw_temps(dbl=False):
        tm = {}
        for nm in ("kkr", "kk", "sg", "Ei", "Einv", "Ee", "Er", "beta"):
            tm[nm] = k.sb([128, 512], F32, nm)
        for nm in ("a0t", "keff0", "a1t", "keff1"):
            tm[nm] = k.sb([128, 512], F32, nm)
        tm["sq"], tm["wpre"], tm["apre0"], tm["apre1"] = tm["Ee"], tm["Er"], tm["Ei"], tm["Einv"]
        for nm in ("vb", "At", "Rt", "Bh", "Kh", "Bt", "Kt"):
            tm[nm] = k.sb([128, 512], BF16, nm)
        tm["s8"] = k.sb([128, 32], F32, "s8")
        tm["lTt"] = k.sb([128, 128], F32, "lTt")
        tm["FAR"] = k.sb([128, 1024], BF16, "FAR")
        tm["FBK"] = k.sb([128, 1024], BF16, "FBK")
        tm["G1"] = Ring(k, 4, [128, 512], BF16, "G1")
        tm["XX"] = Ring(k, 8, [128, 256], F32, "XX")
        tm["ZZ"] = Ring(k, 8, [128, 128], F32, "ZZ")
        tm["Zb"] = Ring(k, 4, [128, 128], BF16, "Zb")
        tm["OmT"] = Ring(k, 4, [64, 128], BF16, "OmT")
        tm["PP"] = k.sb([64, 8 * 128], F32, "PP")
        tm["dW"] = k.sb([64, 512], F32, "dW")
        tm["Wt"] = k.sb([64, 512], F32, "Wt")
        tm["bon"] = k.sb([128, 512], F32, "bon")
        if not dbl:
            return tm
        tm2 = dict(tm)
        for nm in ("vb", "At", "Rt", "Bh", "Kh", "Bt", "Kt"):
            tm2[nm] = k.sb([128, 512], BF16, nm + "2")
        tm2["FAR"] = k.sb([128, 1024], BF16, "FAR2")
        tm2["FBK"] = k.sb([128, 1024], BF16, "FBK2")
        tm2["dW"] = k.sb([64, 512], F32, "dW2")
        tm2["Wt"] = k.sb([64, 512], F32, "Wt2")
        tm2["bon"] = k.sb([128, 512], F32, "bon2")
        return tm, tm2

    def rw_feats(C, tm, xf, lT, zdec, za):
        r, kx, v = xf[:, 0:512], xf[:, 512:1024], xf[:, 1024:1536]
        k.cp("act", tm["vb"], tm["vb"].ap, v, R=[xf])
        yield
        k.tt("dve", tm["kkr"], tm["kkr"].ap, kx, C["kkv"].ap, ALU.mult, R=[xf, C["kkv"]])
        k.tt("pool", tm["sq"], tm["sq"].ap, tm["kkr"].ap, tm["kkr"].ap, ALU.mult, R=[tm["kkr"]])
        s8 = tm["s8"]
        k.red("dve", s8, s8[:, 0:8], v3(tm["sq"].ap, 8), R=[tm["sq"]])
        k.ts("dve", s8, s8[:, 8:16], s8[:, 0:8], 1e-24, None, ALU.max, R=[s8])
        k.tt("gp", s8, s8[:, 24:32], s8[:, 8:16], mhalf[:, 0:8], ALU.pow, R=[s8, mhalf])
        k.tt("dve", tm["kk"], v3(tm["kk"].ap, 8), v3(tm["kkr"].ap, 8), s8[:, 24:32].unsqueeze(2).to_broadcast([128, 8, 64]),
             ALU.mult, R=[tm["kkr"], s8])
        yield
        lTt = tm["lTt"]
        k.act(lTt, lTt[0:64, :], lT[0:64, :], AF.Tanh, R=[lT])
        k.cp("act", lTt, lTt[64:128, :], lT[64:128, :], R=[lT])
        yield
        ps = k.ps()
        k.mm(ps, ps.ap, lTt[0:64, :], C[f"w2{zdec}"][0:64, :], R=[lTt, C[f"w2{zdec}"]])
        k.tt("dve", tm["wpre"], tm["wpre"].ap, ps.ap, C[f"w0{zdec}"].ap, ALU.add, R=[ps, C[f"w0{zdec}"]])
        k.act(tm["sg"], tm["sg"].ap, tm["wpre"].ap, AF.Sigmoid, R=[tm["wpre"]])
        yield
        for z in za:
            ps = k.ps()
            k.mm(ps, ps.ap, lTt[64:128, :], C[f"a2{z}"][64:128, :], R=[lTt, C[f"a2{z}"]])
            ap_, a_, ke_ = tm[f"apre{z}"], tm[f"a{z}t"], tm[f"keff{z}"]
            k.tt("dve", ap_, ap_.ap, ps.ap, C[f"a0{z}"].ap, ALU.add, R=[ps, C[f"a0{z}"]])
            k.act(a_, a_.ap, ap_.ap, AF.Sigmoid, R=[ap_])
            k.stt("pool", ap_, ap_.ap, a_.ap, -1.0, C["kav"].ap, ALU.add, ALU.mult, R=[a_, C["kav"]])
            k.stt("dve", ke_, ke_.ap, ap_.ap, 1.0, kx, ALU.add, ALU.mult, R=[ap_, xf])
            yield
        if len(za) == 2:
            bt, s8b = tm["beta"], tm["s8"]
            k.tt("dve", bt, bt.ap, tm["keff0"].ap, tm["keff1"].ap, ALU.add, R=[tm["keff0"], tm["keff1"]])
            k.tt("dve", bt, bt.ap, bt.ap, r, ALU.mult, R=[bt, xf])
            k.tt("dve", bt, bt.ap, bt.ap, C["rkv"].ap, ALU.mult, R=[bt, C["rkv"]])
            k.red("dve", s8b, s8b[:, 0:8], v3(bt.ap, 8), R=[bt])
            k.ts("dve", s8b, s8b[:, 0:8], s8b[:, 0:8], 0.5, None, ALU.mult, R=[s8b])
            k.tt("dve", tm["bon"], v3(tm["bon"].ap, 8), v3(v, 8), s8b[:, 0:8].unsqueeze(2).to_broadcast([128, 8, 64]), ALU.mult,
                 R=[xf, s8b])
            yield
        z = zdec
        sg = tm["sg"]
        psI, psT = k.ps(), k.ps()
        k.mm(psI, psI.ap, cI[z], sg.ap, R=[cws, sg])
        k.mm(psT, psT.ap, W_ONE, sg.ap, R=[cws, sg])
        k.act(tm["Ei"], tm["Ei"].ap, psI.ap, AF.Exp, R=[psI])
        k.act(tm["Einv"], tm["Einv"].ap, psI.ap, AF.Exp, R=[psI], scale=-1.0)
        k.act(tm["Er"], tm["Er"].ap, psT.ap, AF.Exp, R=[psT])
        k.act(tm["Ee"], tm["Ee"].ap, sg.ap, AF.Exp, R=[sg], scale=-CW)
        yield
        k.tt("dve", tm["dW"], v3(tm["dW"].ap, 8), cst[0:64, C_ID:C_ID + 64].unsqueeze(1).to_broadcast([64, 8, 64]),
             v3(tm["Er"][0:64, :], 8), ALU.mult, R=[cst, tm["Er"]])
        k.tt("dve", tm["Er"], tm["Er"].ap, tm["Er"].ap, tm["Einv"].ap, ALU.mult, R=[tm["Er"], tm["Einv"]])
        k.tt("dve", tm["Ee"], tm["Ee"].ap, tm["Ee"].ap, tm["Ei"].ap, ALU.mult, R=[tm["Ee"], tm["Ei"]])
        a_, ke_ = tm[f"a{z}t"], tm[f"keff{z}"]
        k.stt("dve", tm["At"], tm["At"].ap, tm["kk"].ap, -1.0, tm["Ee"].ap, ALU.mult, ALU.mult, R=[tm["kk"], tm["Ee"]])
        k.tt("pool", tm["Rt"], tm["Rt"].ap, r, tm["Ei"].ap, ALU.mult, R=[xf, tm["Ei"]])
        yield
        k.tt("dve", tm["beta"], tm["beta"].ap, tm["kk"].ap, a_.ap, ALU.mult, R=[tm["kk"], a_])
        k.tt("dve", tm["Bh"], tm["Bh"].ap, tm["beta"].ap, tm["Einv"].ap, ALU.mult, R=[tm["beta"], tm["Einv"]])
        k.tt("pool", tm["Kh"], tm["Kh"].ap, ke_.ap, tm["Einv"].ap, ALU.mult, R=[ke_, tm["Einv"]])
        yield
        k.tt("dve", tm["Bt"], tm["Bt"].ap, tm["beta"].ap, tm["Er"].ap, ALU.mult, R=[tm["beta"], tm["Er"]])
        k.tt("pool", tm["Kt"], tm["Kt"].ap, ke_.ap, tm["Er"].ap, ALU.mult, R=[ke_, tm["Er"]])
        yield
        for (dst, srcs) in ((tm["FAR"], (tm["At"], tm["Rt"])), (tm["FBK"], (tm["Bh"], tm["Kh"]))):
            ps = k.ps()
            pb = bank_bf(ps)
            for j in range(4):
                for x_, src in enumerate(srcs):
                    o = (j * 2 + x_) * 128
                    k.tr(ps, pb[:, o:o + 128], src[:, j * 128:(j + 1) * 128], identb.ap, R=[src, identb])
            k.cp("act", dst, dst.ap, pb, R=[ps])
            yield

    def rw_chunk(tm, z, S, S0b, bg=None):
        (psY, psP0, psP1), pidx = k.pin(3)
        psPP = (psP0, psP1)
        FAR, FBK, vb = tm["FAR"], tm["FBK"], tm["vb"]
        At, Rt, Bt, Kt = tm["At"], tm["Rt"], tm["Bt"], tm["Kt"]
        for grp in range(2):
            heads = [grp * 4 + i for i in range(4)]
            G1, XZ, XT = {}, {}, {}
            for pr2 in range(2):
                hp = heads[2 * pr2:2 * pr2 + 2]
                ops_ = {}
                for h in hp:
                    j, p = h // 2, h % 2
                    P_ = slice(64 * p, 64 * p + 64)
                    ops_[h] = dict(far=FAR[P_, j * 256:(j + 1) * 256], bT=FBK[P_, j * 256:j * 256 + 128],
                                   kT=FBK[P_, j * 256 + 128:j * 256 + 256], aT=FAR[P_, j * 256:j * 256 + 128],
                                   ps1=k.ps(), ps3=k.ps())
                for h in hp:
                    o_ = ops_[h]
                    k.mm(o_["ps1"], o_["ps1"][:, 0:256], o_["bT"], o_["far"], R=[FBK, FAR])
                for h in hp:
                    o_ = ops_[h]
                    k.mm(o_["ps1"], o_["ps1"][:, 256:512], o_["kT"], o_["far"], R=[FBK, FAR])
                for h in hp:
                    o_ = ops_[h]
                    k.mm(o_["ps3"], o_["ps3"][:, 0:128], o_["aT"], o_["bT"], R=[FBK, FAR])
                for h in hp:
                    ps1, ps3 = ops_[h]["ps1"], ops_[h]["ps3"]
                    g1 = tm["G1"].next()
                    k.tt("dve", g1, g1.ap, ps1.ap, M4[z].ap, ALU.mult, R=[ps1, M4[z]])
                    xx = tm["XX"].next()
                    zz = tm["ZZ"].next()
                    k.tt("dve", xx, xx[:, 0:128], ps3[:, 0:128], mS[z], ALU.mult, R=[ps3, cst])
                    k.tt("dve", xx, xx[:, 128:256], ps1[:, 0:128], M4[z][:, 0:128], ALU.mult, R=[ps1, M4[z]])
                    k.cp("act", zz, zz[:, 0:64], At[:, h * 64:(h + 1) * 64], R=[At])
                    G1[h], XZ[h], XT[h] = g1, xx, zz
                for h in hp:
                    ps4 = k.ps()
                    k.mm(ps4, ps4[:, 0:64], G1[h][:, 256:384], vb[:, h * 64:(h + 1) * 64], R=[G1[h], vb])
                    k.cp("act", XT[h], XT[h][:, 64:128], ps4[:, 0:64], R=[ps4])
            pump(bg, 1)
            for i in range(7):
                pr_ = {}
                for h in heads:
                    b_ = k.ps()
                    pr_[h] = b_
                    xx, zz = XZ[h], XT[h]
                    if i < 6:
                        k.mm(b_, b_[:, 0:128], xx[:, 128:256], xx[:, 0:128], R=[xx])
                        k.mm(b_, b_[:, 128:256], xx[:, 0:128], xx[:, 128:256], R=[xx])
                    k.mm(b_, b_[:, 256:384], xx[:, 128:256], zz.ap, R=[xx, zz])
                for h in heads:
                    b_ = pr_[h]
                    xx, zz = XZ[h], XT[h]
                    nzz = tm["ZZ"].next()
                    k.tt("dve", nzz, nzz.ap, b_[:, 256:384], zz.ap, ALU.add, R=[b_, zz])
                    if i < 6:
                        nxx = tm["XX"].next()
                        k.cp("act", nxx, nxx.ap, b_[:, 0:256], R=[b_])
                        XZ[h] = nxx
                    XT[h] = nzz
                pump(bg, 1)
            Zs, oms = {}, {}
            for h in heads:
                Z = tm["Zb"].next()
                k.cp("act", Z, Z.ap, XT[h].ap, R=[XT[h]])
                Zs[h] = Z
            psOm = k.ps()
            for q_, h in enumerate(heads):
                Z = Zs[h]
                Ah, Gh = Z[:, 0:64], Z[:, 64:128]
                hs_ = slice(h * 64, (h + 1) * 64)
                bank = psPP[h // 4]
                o = (h % 4) * 128
                k.mm(bank, bank[0:64, o:o + 64], Ah, Bt[:, hs_], R=[Z, Bt])
                k.mm(bank, bank[0:64, o + 64:o + 128], Bt[:, hs_], Gh, start=True, stop=False, R=[Z, Bt])
                k.mm(bank, bank[0:64, o + 64:o + 128], Kt[:, hs_], vb[:, hs_], start=False, stop=True, R=[Kt, vb])
                k.mm(psOm, psOm[0:64, q_ * 128:(q_ + 1) * 128], Ah, G1[h][:, 128:256], start=True, stop=False, R=[Z, G1[h]])
                k.mm(psOm, psOm[0:64, q_ * 128:(q_ + 1) * 128], Rt[:, hs_], identb.ap, start=False, stop=True, R=[Rt, identb])
            for q_, h in enumerate(heads):
                om = tm["OmT"].next()
                k.cp("act", om, om.ap, psOm[0:64, q_ * 128:(q_ + 1) * 128], R=[psOm])
                oms[h] = om
            for h in heads:
                Z = Zs[h]
                Gh = Z[:, 64:128]
                hs_ = slice(h * 64, (h + 1) * 64)
                k.mm(psY, psY[:, hs_], G1[h][:, 128:256], Gh, start=True, stop=False, R=[G1[h], Z])
                k.mm(psY, psY[:, hs_], G1[h][:, 384:512], vb[:, hs_], start=False, stop=False, R=[G1[h], vb])
                k.mm(psY, psY[:, hs_], oms[h].ap, S0b[0:64, hs_], start=False, stop=True, R=[oms[h], S0b])
            pump(bg, 1)
        PP = tm["PP"]
        PPv = v3(PP.ap, 8)
        for b in range(2):
            pv = v3(psPP[b][0:64, :], 4)
            k.tt("dve", PP, PPv[:, b * 4:(b + 1) * 4, 0:64], pv[:, :, 0:64], v3(tm["dW"].ap, 8)[:, b * 4:(b + 1) * 4, :], ALU.add,
                 R=[psPP[b], tm["dW"]])
            k.cp("act", PP, PPv[:, b * 4:(b + 1) * 4, 64:128], pv[:, :, 64:128], R=[psPP[b]])
        psS = k.ps()
        for h in range(8):
            k.mm(psS, psS[0:64, h * 64:(h + 1) * 64], PP[:, h * 128:h * 128 + 64], S[0:64, h * 64:(h + 1) * 64], R=[PP, S])
        k.tt("dve", S, v3(S.ap, 8), v3(psS[0:64, :], 8), PPv[:, :, 64:128], ALU.add, R=[psS, PP])
        k.cp("pool", S0b, S0b.ap, S.ap, R=[S])
        k.unpin(pidx[1:])
        return psY, pidx[0:1]

    def phaseF(K2T, Vall, Vv):
        with k.scope():
            Wkv = k.sb([128, 8 * 256], BF16, "Wkv")
            WK2 = k.sb([128, 8 * 256], BF16, "WK2")
            Wr1 = k.sb([128, 8 * 1664], BF16, "Wr1")
            Wr2 = k.sb([128, 8 * 1664], BF16, "Wr2")
            with k.scope():
                stg = Ring(k, 2, [128, 1664], F32, "wstg")
                mu1 = k.sb([128, 1664], F32, "mu1")
                mu2 = k.sb([128, 1664], F32, "mu2")
                load_bc(mu1, I["b_mu"][0:1, :])
                k.ts("dve", mu2, mu2.ap, mu1.ap, 0.5, None, ALU.mult, R=[mu1])
                k.ts("dve", mu1, mu1.ap, mu1.ap, -1.0, 1.0, ALU.mult, ALU.add, R=[mu1])
                d = [(Wkv, lambda kc: Wkv[:, kc * 256:(kc + 1) * 256], None, (0, 256))]
                for g in range(2):
                    for r in range(2):
                        d.append((WK2, (lambda kc, g=g, r=r: WK2[:, kc * 256 + g * 128 + r * 64:kc * 256 + g * 128 + r * 64 + 64]),
                                  None, (g * 64, g * 64 + 64)))
                load_cols2(d, I["w_in0"], 512, 256, stg)
                load_cols2([(Wr1, lambda kc: Wr1[:, kc * 1664:(kc + 1) * 1664], mu1, (0, 1664)),
                            (Wr2, lambda kc: Wr2[:, kc * 1664:(kc + 1) * 1664], mu2, (0, 1664))], I["w_in0"], 1280, 1664, stg)
                modulation_to_dram(0)
                Sh = k.sb([128, D], F32, "ShF")
                A = k.sb([128, D], F32, "AF")
                xr = Ring(k, 2, [128, D], F32, "xr")
                tmpfr = Ring(k, 2, [128, D], F32, "tmpf")
                hbr = Ring(k, 2, [128, D], BF16, "hb")
                junk = k.sb([128, D], BF16, "junk")
                ssr = Ring(k, 2, [128, 8], F32, "ssr")
                hTr = Ring(k, 4, [128, 1024], BF16, "hTr")
                hsr = Ring(k, 2, [128, 1024], BF16, "hsr")
                curg = None
                for sq in seqs:
                    if sq["g"] != curg:
                        curg = sq["g"]
                        k.load(Sh, Sh.ap, mod_s[0, curg, 0], R=[mod_t])
                        k.load(A, A.ap, mod_s[0, curg, 1], R=[mod_t])
                    n = sq["n"]
                    hTs = {}

                    def make_hs(cm):
                        cur = v3(hTs[cm].ap, 8)
                        hs = hsr.next()
                        hv = v3(hs.ap, 8)
                        k.tt("dve", hs, hv[:, :, 1:127], cur[:, :, 0:126], cur[:, :, 2:128], ALU.add, R=[hTs[cm]])
                        if cm > 0:
                            k.tt("pool", hs, hv[:, :, 0:1], cur[:, :, 1:2], v3(hTs[cm - 1].ap, 8)[:, :, 127:128], ALU.add,
                                 R=[hTs[cm], hTs[cm - 1]])
                        else:
                            k.cp("pool", hs, hv[:, :, 0:1], cur[:, :, 1:2], R=[hTs[cm]])
                        if cm < n - 1:
                            k.tt("pool", hs, hv[:, :, 127:128], cur[:, :, 126:127], v3(hTs[cm + 1].ap, 8)[:, :, 0:1], ALU.add,
                                 R=[hTs[cm], hTs[cm + 1]])
                        else:
                            k.cp("pool", hs, hv[:, :, 127:128], cur[:, :, 126:127], R=[hTs[cm]])
                        k.store(hsT_s[sq["t0"] + cm], hs, hs.ap, W=[dt_("hsT", sq["t0"] + cm)])

                    def f1_gen(c, sq=sq, n=n, hTs=hTs, make_hs=make_hs):
                        xt = xr.next()
                        k.load(xt, xt.ap, sq["x"][c * 128:(c + 1) * 128, :])
                        ss_t = ssr.next()
                        rstd = rstd_of(ss_t, junk, [(xt, xt.ap)], D, 1e-6)
                        yield
                        tmpf = tmpfr.next()
                        k.stt("dve", tmpf, tmpf.ap, xt.ap, rstd, A.ap, ALU.mult, ALU.mult, R=[xt, ss_t, A])
                        hb = hbr.next()
                        k.tt("pool", hb, hb.ap, tmpf.ap, Sh.ap, ALU.add, R=[tmpf, Sh])
                        yield
                        ps = k.ps()
                        pb = bank_bf(ps)
                        for kc in range(8):
                            k.tr(ps, pb[:, kc * 128:(kc + 1) * 128], hb[:, kc * 128:(kc + 1) * 128], identb.ap, R=[hb, identb])
                        hT = hTr.next()
                        k.cp("act", hT, hT.ap, pb, R=[ps])
                        hTs[c] = hT
                        k.store(hT_s[sq["t0"] + c], hT, hT.ap, W=[dt_("hT", sq["t0"] + c)])
                        yield
                        if c >= 1:
                            make_hs(c - 1)
                        if c == n - 1:
                            make_hs(n - 1)

                    run2([f1_gen(c) for c in range(n)])
            if stop_after == "F1":
                return
            with k.scope():
                C = rw_consts(["kkv", "kav", "w00", "a00", "w20", "a20"])
                tm = rw_temps()
                hTl = Ring(k, 2, [128, 1024], BF16, "hTl")
                hsTl = Ring(k, 2, [128, 1024], BF16, "hsTl")
                xfr = Ring(k, 2, [128, 1536], F32, "xfr")
                lTr = Ring(k, 2, [128, 128], F32, "lTr")
                kvr = Ring(k, 2, [128, 256], F32, "kvr")
                ropr = Ring(k, 2, [128, 256], F32, "ropr")
                kbr = Ring(k, 2, [128, 128], BF16, "kbr")
                t1r = Ring(k, 2, [128, 128], F32, "t1r")
                yfr = Ring(k, 2, [128, 512], F32, "yfr")
                S = k.sb([64, 512], F32, "S")
                S0b = k.sb([64, 512], BF16, "S0b")
                sto = k.sb([64, 512], F32, "sto")
                def proj_gen(sq, c, outd):
                    gt = sq["t0"] + c
                    hT = hTl.next()
                    k.load(hT, hT.ap, hT_s[gt], R=[dt_("hT", gt)])
                    hs = hsTl.next()
                    k.load(hs, hs.ap, hsT_s[gt], R=[dt_("hsT", gt)])
                    xf = xfr.next()
                    for nb in range(3):
                        ps = k.ps()
                        for kc in range(8):
                            k.mm(ps, ps.ap, hT[:, kc * 128:(kc + 1) * 128], Wr1[:, kc * 1664 + nb * 512:kc * 1664 + (nb + 1) * 512],
                                 start=(kc == 0), stop=False, R=[hT, Wr1])
                            k.mm(ps, ps.ap, hs[:, kc * 128:(kc + 1) * 128], Wr2[:, kc * 1664 + nb * 512:kc * 1664 + (nb + 1) * 512],
                                 start=False, stop=(kc == 7), R=[hs, Wr2])
                        k.cp("act", xf, xf[:, nb * 512:(nb + 1) * 512], ps.ap, R=[ps])
                        yield
                    ps = k.ps()
                    for kc in range(8):
                        k.mm(ps, ps[:, 0:128], Wr1[:, kc * 1664 + 1536:kc * 1664 + 1664], hT[:, kc * 128:(kc + 1) * 128],
                             start=(kc == 0), stop=False, R=[hT, Wr1])
                        k.mm(ps, ps[:, 0:128], Wr2[:, kc * 1664 + 1536:kc * 1664 + 1664], hs[:, kc * 128:(kc + 1) * 128],
                             start=False, stop=(kc == 7), R=[hs, Wr2])
                    lT = lTr.next()
                    k.cp("act", lT, lT.ap, ps[:, 0:128], R=[ps])
                    k.store(xf_s[gt], xf, xf.ap, W=[dt_("xf", gt)])
                    k.store(lora_s[gt], lT, lT.ap, W=[dt_("lora", gt)])
                    outd[c] = (xf, lT)
                    yield
                    ps = k.ps()
                    for kc in range(8):
                        k.mm(ps, ps[:, 0:256], hT[:, kc * 128:(kc + 1) * 128], Wkv[:, kc * 256:(kc + 1) * 256],
                             start=(kc == 0), stop=(kc == 7), R=[hT, Wkv])
                    blk = (sq["kc0"] // 128) + c
                    k.cp("dve", Vall, Vv[:, blk, :, 0:64], v3(ps[:, 128:256], 2), R=[ps])
                    if not sq["sample"]:
                        kv = kvr.next()
                        k.cp("act", kv, kv.ap, ps[:, 0:256], R=[ps])
                        k.store(O["nak"][sq["s"], c * 128:(c + 1) * 128, :], kv, kv[:, 0:128])
                        k.store(O["nav"][sq["s"], c * 128:(c + 1) * 128, :], kv, kv[:, 128:256])
                    yield
                    if sq["sample"]:
                        rp = ropr.next()
                        k.load(rp, v3(rp.ap, 2), I["rope"][:, :, c * 128:(c + 1) * 128])
                    for g in range(2):
                        ps = k.ps()
                        for kc in range(8):
                            k.mm(ps, ps[:, 0:128], WK2[:, kc * 256 + g * 128:kc * 256 + (g + 1) * 128], hT[:, kc * 128:(kc + 1) * 128],
                                 start=(kc == 0), stop=(kc == 7), R=[hT, WK2])
                        dst = K2T[:, g, sq["kc0"] + c * 128:sq["kc0"] + (c + 1) * 128]
                        if sq["sample"]:
                            rope_apply(ps, ps[:, 0:128], K2T, dst, rp, kbr, t1r, 1)
                        else:
                            k.cp("act", K2T, dst, ps[:, 0:128], R=[ps])
                        yield

                for sq in seqs:
                    n = sq["n"]
                    if sq["sample"]:
                        load_state(S, S0b, 0, sto)
                    else:
                        k.ms("dve", S, S.ap, 0.0)
                        k.ms("pool", S0b, S0b.ap, 0.0)
                    outd = {}
                    gens = [proj_gen(sq, c, outd) for c in range(n)]
                    exhaust(gens[0])
                    for c in range(n):
                        gt = sq["t0"] + c
                        xf, lT = outd[c]
                        bg = gens[c + 1] if c + 1 < n else None
                        for _ in rw_feats(C, tm, xf, lT, 0, [0]):
                            pump(bg, 1)
                        exhaust(bg)
                        psY, yidx = rw_chunk(tm, 0, S, S0b)
                        yf = yfr.next()
                        k.cp("act", yf, yf.ap, psY.ap, R=[psY])
                        k.unpin(yidx)
                        k.store(yf_s[gt], yf, yf.ap, W=[dt_("yf", gt)])
                    if not sq["sample"]:
                        store_state(S, sto, O["nst"][sq["s"], 0])

    def rope_apply(ps_t, ps_ap, dst_t, dst_ap, rp, kbr, t1r, npair):
        w = npair * 128
        kb = kbr.next()
        k.cp("act", kb, kb[:, 0:w], ps_ap, R=[ps_t])
        pr = k.ps()
        k.mm(pr, pr[:, 0:w], pmtb.ap, kb[:, 0:w], R=[pmtb, kb])
        t1 = t1r.next()
        cosb = rp[:, 0:128]
        sinb = rp[:, 128:256]
        if npair > 1:
            cosb = cosb.unsqueeze(1).to_broadcast([128, npair, 128])
            sinb = sinb.unsqueeze(1).to_broadcast([128, npair, 128])
            k.tt("dve", t1, v3(t1[:, 0:w], npair), v3(ps_ap, npair), cosb, ALU.mult, R=[ps_t, rp])
            k.tt("dve", kb, v3(kb[:, 0:w], npair), v3(pr[:, 0:w], npair), sinb, ALU.mult, R=[pr, rp])
        else:
            k.tt("dve", t1, t1[:, 0:w], ps_ap, cosb, ALU.mult, R=[ps_t, rp])
            k.tt("dve", kb, kb[:, 0:w], pr[:, 0:w], sinb, ALU.mult, R=[pr, rp])
        k.tt("pool", dst_t, dst_ap, t1[:, 0:w], kb[:, 0:w], ALU.add, R=[t1, kb])

    def load_state(S, S0b, z, raw):
        k.load(raw, v3(raw.ap, 8), I["st0"][z].rearrange("h v k -> v h k"))
        ps = k.ps()
        for h in range(8):
            k.mm(ps, ps[0:64, h * 64:(h + 1) * 64], raw[:, h * 64:(h + 1) * 64], cst[0:64, C_ID:C_ID + 64], R=[raw, cst])
        k.cp("act", S, S.ap, ps[0:64, :], R=[ps])
        k.cp("dve", S0b, S0b.ap, ps[0:64, :], R=[ps])

    def store_state(S, sto, out_ap):
        ps = k.ps()
        for h in range(8):
            k.mm(ps, ps[0:64, h * 64:(h + 1) * 64], S[:, h * 64:(h + 1) * 64], cst[0:64, C_ID:C_ID + 64], R=[S, cst])
        k.cp("act", sto, sto.ap, ps[0:64, :], R=[ps])
        k.store(out_ap.rearrange("h v k -> v h k"), sto, v3(sto.ap, 8))

    def phaseB(K2T, Vall, Vv):
        with k.scope():
            C = rw_consts(["kkv", "kav", "rkv", "lnw", "lnb", "w01", "a00", "a01", "w21", "a20", "a21"])
            tms = rw_temps(dbl=True)
            xfr = Ring(k, 2, [128, 1536], F32, "xfrB")
            lTr = Ring(k, 2, [128, 128], F32, "lTrB")
            yfr = Ring(k, 2, [128, 512], F32, "yfrB")
            ycr = Ring(k, 2, [128, 512], F32, "ycr")
            ysq = k.sb([128, 512], F32, "ysq")
            s8 = k.sb([128, 40], F32, "s8B")
            S = k.sb([64, 512], F32, "SB")
            S0b = k.sb([64, 512], BF16, "S0bB")
            sto = k.sb([64, 512], F32, "stoB")
            work = [(sq, c) for sq in seqs for c in range(sq["n"] - 1, -1, -1)]
            yfs = {}

            def feats_gen(i):
                sq, c = work[i]
                gt = sq["t0"] + c
                xf = xfr.next()
                k.load(xf, xf.ap, xf_s[gt], R=[dt_("xf", gt)])
                lT = lTr.next()
                k.load(lT, lT.ap, lora_s[gt], R=[dt_("lora", gt)])
                yf = yfr.next()
                k.load(yf, yf.ap, yf_s[gt], R=[dt_("yf", gt)])
                yfs[i] = yf
                yield from rw_feats(C, tms[i % 2], xf, lT, 1, [0, 1])

            ysr = Ring(k, 2, [128, 512], F32, "ysr")

            def fin_gen(i, ys_, tm):
                sq, c = work[i]
                gt = sq["t0"] + c
                yc = ycr.next()
                k.red("dve", s8, s8[:, 0:8], v3(ys_.ap, 8), R=[ys_])
                k.ts("dve", s8, s8[:, 8:16], s8[:, 0:8], 1.0 / 64, None, ALU.mult, R=[s8])
                k.tt("dve", yc, v3(yc.ap, 8), v3(ys_.ap, 8), s8[:, 8:16].unsqueeze(2).to_broadcast([128, 8, 64]), ALU.subtract,
                     R=[ys_, s8])
                yield
                k.tt("pool", ysq, ysq.ap, yc.ap, yc.ap, ALU.mult, R=[yc])
                k.red("dve", s8, s8[:, 16:24], v3(ysq.ap, 8), R=[ysq])
                k.ts("dve", s8, s8[:, 16:24], s8[:, 16:24], 1.0 / 64, 64e-5, ALU.mult, ALU.add, R=[s8])
                k.tt("gp", s8, s8[:, 32:40], s8[:, 16:24], mhalf[:, 0:8], ALU.pow, R=[s8, mhalf])
                yield
                k.tt("dve", yc, v3(yc.ap, 8), v3(yc.ap, 8), s8[:, 32:40].unsqueeze(2).to_broadcast([128, 8, 64]), ALU.mult, R=[yc, s8])
                k.tt("pool", yc, yc.ap, yc.ap, C["lnw"].ap, ALU.mult, R=[yc, C["lnw"]])
                yield
                k.tt("pool", yc, yc.ap, yc.ap, C["lnb"].ap, ALU.add, R=[yc, C["lnb"]])
                k.tt("pool", yc, yc.ap, yc.ap, tm["bon"].ap, ALU.add, R=[yc, tm["bon"]])
                k.store(yb_s[gt], yc, yc.ap, W=[dt_("yb", gt)])
                yield

            def chain(*gs):
                for g in gs:
                    if g is not None:
                        yield from g

            gens = [feats_gen(i) for i in range(len(work))]
            exhaust(gens[0])
            fin_prev = None
            for i, (sq, c) in enumerate(work):
                n = sq["n"]
                gt = sq["t0"] + c
                tm = tms[i % 2]
                if c == n - 1:
                    if sq["sample"]:
                        load_state(S, S0b, 1, sto)
                    else:
                        k.ms("dve", S, S.ap, 0.0)
                        k.ms("pool", S0b, S0b.ap, 0.0)
                nxt = gens[i + 1] if i + 1 < len(work) else None
                bg = chain(fin_prev, nxt)
                psY, yidx = rw_chunk(tm, 1, S, S0b, bg)
                yf = yfs.pop(i)
                ys_ = ysr.next()
                k.tt("dve", ys_, ys_.ap, psY.ap, yf.ap, ALU.add, R=[psY, yf])
                k.unpin(yidx)
                exhaust(bg)
                fin_prev = fin_gen(i, ys_, tm)
                if c == 0 and not sq["sample"]:
                    store_state(S, sto, O["nst"][sq["s"], 1])
            exhaust(fin_prev)
        with k.scope():
            Wq = k.sb([128, 8 * 512], BF16, "Wq")
            Wg = k.sb([128, 8 * 1024], BF16, "Wg")
            WO = k.sb([128, 8 * 1024], BF16, "WO")
            with k.scope():
                stg = Ring(k, 2, [128, 1024], F32, "wstgB")
                load_cols2([(Wq, lambda kc: Wq[:, kc * 512:(kc + 1) * 512], None, (0, 512))], I["w_in0"], 0, 512, stg)
                load_cols2([(Wg, lambda kc: Wg[:, kc * 1024:kc * 1024 + 512], None, (0, 512))], I["w_in0"], 768, 512, stg)
                load_cols2([(Wg, lambda kc: Wg[:, kc * 1024 + 512:(kc + 1) * 1024], None, (0, 512))], I["w_in0"], 2944, 512, stg)
                load_cols2([(WO, lambda kc: WO[:, kc * 1024:(kc + 1) * 1024], None, (0, 1024))], I["w_out"][0], 0, 1024, stg)
            esink = k.sb([128, 8], F32, "esink")
            load_bc(esink, I["a_sink"][0:1, :])
            k.act(esink, esink.ap, esink.ap, AF.Exp, R=[esink])
            mLi = k.sb([128, 128], BF16, "mLi")
            mUi = k.sb([128, 128], BF16, "mUi")
            k.cp("dve", mLi, mLi.ap, cst[:, C_LI:C_LI + 128], R=[cst])
            k.cp("dve", mUi, mUi.ap, cst[:, C_UI:C_UI + 128], R=[cst])
            Gms = [k.sb([128, D], F32, f"GmB{g}") for g in range(2)]
            for g in range(2):
                k.load(Gms[g], Gms[g].ap, mod_s[0, g, 2], R=[mod_t])
            hTl = Ring(k, 2, [128, 1024], BF16, "hTlB")
            xr = Ring(k, 2, [128, D], F32, "xrB")
            ybl = Ring(k, 2, [128, 512], F32, "ybl")
            ropr = Ring(k, 2, [128, 256], F32, "roprB")
            kbr = Ring(k, 2, [128, 512], BF16, "kbrB")
            t1r = Ring(k, 2, [128, 512], F32, "t1rB")
            qTr = Ring(k, 2, [128, 512], BF16, "qTr")
            ptr = Ring(k, 3, [128, 640], BF16, "ptr")
            yar = Ring(k, 2, [128, 512], F32, "yar")
            dn = k.sb([128, 16], F32, "dn")
            sgar = Ring(k, 2, [128, 1024], F32, "sgar")
            mbr = Ring(k, 2, [128, 1024], BF16, "mb")
            mTr = Ring(k, 2, [128, 1024], BF16, "mT")
            junk = k.sb([128, 512], BF16, "junkB")
            ssr = Ring(k, 2, [128, 8], F32, "ssrB")
            workC = [(sq, c) for sq in seqs for c in range(sq["n"])]
            resC = {}

            def tileC_gen(i):
                sq, c = workC[i]
                n = sq["n"]
                gt = sq["t0"] + c
                if True:
                    hT = hTl.next()
                    k.load(hT, hT.ap, hT_s[gt], R=[dt_("hT", gt)])
                    xt = xr.next()
                    k.load(xt, xt.ap, sq["x"][c * 128:(c + 1) * 128, :])
                    yc = ybl.next()
                    k.load(yc, yc.ap, yb_s[gt], R=[dt_("yb", gt)])
                    ya = yar.next()
                    sga = sgar.next()
                    resC[i] = (xt, yc, ya, sga)
                    def attn_gen():
                        psq = k.ps()
                        for j in range(4):
                            for kc in range(8):
                                k.mm(psq, psq[:, j * 128:(j + 1) * 128], Wq[:, kc * 512 + j * 128:kc * 512 + (j + 1) * 128],
                                     hT[:, kc * 128:(kc + 1) * 128], start=(kc == 0), stop=(kc == 7), R=[Wq, hT])
                        qT = qTr.next()
                        if sq["sample"]:
                            rp = ropr.next()
                            k.load(rp, v3(rp.ap, 2), I["rope"][:, :, c * 128:(c + 1) * 128])
                            rope_apply(psq, psq.ap, qT, qT.ap, rp, kbr, t1r, 4)
                            blocks = [(0, None), (128, None)]
                            if c > 0:
                                blocks.append((sq["kc0"] + (c - 1) * 128, mLi))
                            blocks.append((sq["kc0"] + c * 128, None))
                            if c < n - 1:
                                blocks.append((sq["kc0"] + (c + 1) * 128, mUi))
                        else:
                            k.cp("act", qT, qT.ap, psq.ap, R=[psq])
                            blocks = [(sq["kc0"] + cc * 128, None) for cc in range(n)]
                        nkb = len(blocks)
                        yield
                        (psO,), oidx = k.pin(1)
                        pssb, PTb = {}, {}

                        def atA(h):
                            j, p, g = h // 2, h % 2, h // 4
                            P_ = slice(64 * p, 64 * p + 64)
                            pss = [k.ps() for _ in range((nkb + 3) // 4)]
                            for i, (col, msk) in enumerate(blocks):
                                b_ = pss[i // 4]
                                k.mm(b_, b_[:, (i % 4) * 128:(i % 4 + 1) * 128], K2T[P_, g, col:col + 128], qT[P_, j * 128:(j + 1) * 128],
                                     R=[K2T, qT])
                            pssb[h] = pss

                        def atB(h):
                            pss = pssb.pop(h)
                            PT_ = ptr.next()
                            for bi, b_ in enumerate(pss):
                                w = min(4, nkb - bi * 4) * 128
                                k.act(PT_, PT_[:, bi * 512:bi * 512 + w], b_[:, 0:w], AF.Exp, R=[b_], scale=0.125)
                            for i, (col, msk) in enumerate(blocks):
                                if msk is not None:
                                    k.tt("pool", PT_, PT_[:, i * 128:(i + 1) * 128], PT_[:, i * 128:(i + 1) * 128], msk.ap, ALU.mult, R=[PT_, msk])
                            PTb[h] = PT_

                        def atC(h):
                            hg, hh, g = h // 4, h % 4, h // 4
                            PT_ = PTb.pop(h)
                            for i, (col, msk) in enumerate(blocks):
                                k.mm(psO, psO[:, hh * 65:(hh + 1) * 65], PT_[:, i * 128:(i + 1) * 128], Vv[:, col // 128, g, :],
                                     start=(i == 0), stop=(i == nkb - 1), R=[PT_, Vall])
                            if hh != 3:
                                return
                            pv = psO[:, 0:260].rearrange("p (a b) -> p a b", a=4)
                            k.tt("dve", dn, dn[:, 0:4], pv[:, :, 64], esink[:, hg * 4:(hg + 1) * 4], ALU.add, R=[psO, esink])
                            k.rcp(dn, dn[:, 4:8], dn[:, 0:4], R=[dn])
                            k.tt("dve", ya, v3(ya.ap, 8)[:, hg * 4:(hg + 1) * 4, :], pv[:, :, 0:64],
                                 dn[:, 4:8].unsqueeze(2).to_broadcast([128, 4, 64]), ALU.mult, R=[psO, dn])

                        for h in range(-1, 8):
                            if h + 1 < 8:
                                atA(h + 1)
                            if h >= 0:
                                atB(h)
                                atC(h)
                            yield
                        k.unpin(oidx)
                        for gi in range(2):
                            psg = k.ps()
                            for kc in range(8):
                                k.mm(psg, psg.ap, hT[:, kc * 128:(kc + 1) * 128], Wg[:, kc * 1024 + gi * 512:kc * 1024 + (gi + 1) * 512],
                                     start=(kc == 0), stop=(kc == 7), R=[hT, Wg])
                            k.act(sga, sga[:, gi * 512:(gi + 1) * 512], psg.ap, AF.Silu, R=[psg])
                            yield
                    yield from attn_gen()

            def fullC_gen(i):
                sq, c = workC[i]
                gt = sq["t0"] + c
                yield from tileC_gen(i)
                xt, yc, ya, sga = resC.pop(i)
                mb, mT = mbr.next(), mTr.next()
                k.tt("dve", mb, mb[:, 0:512], ya.ap, sga[:, 0:512], ALU.mult, R=[ya, sga])
                k.tt("pool", mb, mb[:, 512:1024], yc.ap, sga[:, 512:1024], ALU.mult, R=[yc, sga])
                yield
                yield from out_proj_gen(mb, mT, WO, Gms[sq["g"]], xt, sga, junk, ssr, x1_s[gt * 128:(gt + 1) * 128, :], dt_("x1", gt))

            run2([fullC_gen(i) for i in range(len(workC))])

    def out_proj_residual(*a):
        exhaust(out_proj_gen(*a))

    def out_proj_gen(mb, mT, WO, Gm, xt, tmpo, junk, ssr, out_ap, out_dt):
        ps = k.ps()
        pb = bank_bf(ps)
        for kc in range(8):
            k.tr(ps, pb[:, kc * 128:(kc + 1) * 128], mb[:, kc * 128:(kc + 1) * 128], identb.ap, R=[mb, identb])
        k.cp("act", mT, mT.ap, pb, R=[ps])
        yield
        py = [k.ps(), k.ps()]
        for nb in range(2):
            for kc in range(8):
                k.mm(py[nb], py[nb].ap, mT[:, kc * 128:(kc + 1) * 128], WO[:, kc * 1024 + nb * 512:kc * 1024 + (nb + 1) * 512],
                     start=(kc == 0), stop=(kc == 7), R=[mT, WO])
        ss = ssr.next()
        rstd = rstd_of(ss, junk, [(py[0], py[0].ap), (py[1], py[1].ap)], D, 1e-6)
        for nb in range(2):
            cs = slice(nb * 512, (nb + 1) * 512)
            k.stt("dve", tmpo, tmpo[:, cs], py[nb].ap, rstd, Gm[:, cs], ALU.mult, ALU.mult, R=[py[nb], ss, Gm])
        yield
        k.tt("pool", tmpo, tmpo.ap, tmpo.ap, xt.ap, ALU.add, R=[tmpo, xt])
        k.store(out_ap, tmpo, tmpo.ap, W=[out_dt] if out_dt is not None else [])

    NTOK = NT * 128
    qT_s = dscr("qT_s", (8, 128, NTOK), BF16)
    kT_s = dscr("kT_s", (8, 128, NTOK), BF16)
    v_s = dscr("v_s", (NT, 128, 1024), BF16)
    sg_s = dscr("sg_s", (NT, 128, 1024), BF16)

    def layer1():
        with k.scope():
            lamt = k.sb([128, 4], F32, "lamt")
            with k.scope():
                lq = [k.sb([128, 64], F32, f"lq{i}") for i in range(4)]
                for t, nm in zip(lq, ("c_lq1", "c_lk1", "c_lq2", "c_lk2")):
                    load_bc(t, I[nm][0:1, :])
                k.tt("dve", lq[0], lq[0].ap, lq[0].ap, lq[1].ap, ALU.mult, R=[lq[0], lq[1]])
                k.tt("dve", lq[2], lq[2].ap, lq[2].ap, lq[3].ap, ALU.mult, R=[lq[2], lq[3]])
                k.red("dve", lamt, lamt[:, 0:1], lq[0].ap, R=[lq[0]])
                k.red("dve", lamt, lamt[:, 1:2], lq[2].ap, R=[lq[2]])
                k.act(lamt, lamt[:, 0:2], lamt[:, 0:2], AF.Exp, R=[lamt])
                k.tt("dve", lamt, lamt[:, 2:3], lamt[:, 0:1], lamt[:, 1:2], ALU.subtract, R=[lamt])
                k.ts("dve", lamt, lamt[:, 2:3], lamt[:, 2:3], LAM_INIT, None, ALU.add, R=[lamt])
                k.ts("dve", lamt, lamt[:, 3:4], lamt[:, 2:3], -1.0, None, ALU.mult, R=[lamt])
            subl = k.sb([128, 128], F32, "subl")
            load_bc(subl, I["c_subln"][0:1, :])
            k.ts("dve", subl, subl.ap, subl.ap, 1.0 - LAM_INIT, None, ALU.mult, R=[subl])
            l1_phase1()
            if stop_after == "L1P1":
                return
            l1_phase23(lamt, subl)

    def l1_phase1():
        with k.scope():
            W1 = k.sb([128, 8 * 4096], BF16, "W1L1")
            with k.scope():
                stg = Ring(k, 2, [128, 1024], F32, "wstg1")
                for q4 in range(4):
                    load_cols2([(W1, (lambda kc, q4=q4: W1[:, kc * 4096 + q4 * 1024:kc * 4096 + (q4 + 1) * 1024]), None, (0, 1024))],
                               I["w_in1"], q4 * 1024, 1024, stg)
                modulation_to_dram(1)
            Sh = k.sb([128, D], F32, "Sh1")
            A = k.sb([128, D], F32, "A1")
            xr = Ring(k, 3, [128, D], F32, "xr1")
            tmpf = k.sb([128, D], F32, "tmpf1")
            hbr = Ring(k, 3, [128, D], BF16, "hb1")
            junk = k.sb([128, D], BF16, "junk1")
            ssr = Ring(k, 3, [128, 8], F32, "ssr1")
            hTr = Ring(k, 3, [128, 1024], BF16, "hTr1")
            ropr = Ring(k, 3, [128, 256], F32, "ropr1")
            kbr = Ring(k, 2, [128, 512], BF16, "kbr1")
            t1r = Ring(k, 2, [128, 512], F32, "t1r1")
            qkr = Ring(k, 4, [128, 512], BF16, "qkr1")
            vbr = Ring(k, 3, [128, 1024], BF16, "vbr1")
            vfr = Ring(k, 2, [128, 1024], F32, "vfr1")
            sgr = Ring(k, 3, [128, 1024], BF16, "sgr1")
            sgf = k.sb([128, 512], F32, "sgf1")
            myidx = k.sb([128, 4], mybir.dt.int32, "myidx")
            k.load(myidx, myidx.ap, I["myrows"])
            x1_all = [dt_("x1", NPS * PT + c) for c in range(ST)]

            def p1_tile(sq, c, gt_x, my_i, do_q, do_kv, do_g, rope_src, gt_q):
                xt = xr.next()
                if gt_x is not None:
                    k.load(xt, xt.ap, x1_s[gt_x * 128:(gt_x + 1) * 128, :], R=[dt_("x1", gt_x)])
                else:
                    k.gather(xt, xt.ap, x1_s, myidx, myidx[:, my_i:my_i + 1], R=x1_all)
                ss_t = ssr.next()
                rstd = rstd_of(ss_t, junk, [(xt, xt.ap)], D, 1e-6)
                k.stt("dve", tmpf, tmpf.ap, xt.ap, rstd, A.ap, ALU.mult, ALU.mult, R=[xt, ss_t, A])
                hb = hbr.next()
                k.tt("pool", hb, hb.ap, tmpf.ap, Sh.ap, ALU.add, R=[tmpf, Sh])
                ps = k.ps()
                pb = bank_bf(ps)
                for kc in range(8):
                    k.tr(ps, pb[:, kc * 128:(kc + 1) * 128], hb[:, kc * 128:(kc + 1) * 128], identb.ap, R=[hb, identb])
                hT = hTr.next()
                k.cp("act", hT, hT.ap, pb, R=[ps])
                yield
                rp = None
                if rope_src is not None:
                    rp = ropr.next()
                    k.load(rp, v3(rp.ap, 2), rope_src)
                for do_, coff, dst_s, nm, gdst in ((do_q, 0, qT_s, "qT", gt_q), (do_kv, 1024, kT_s, "kT", gt_x)):
                    if not do_:
                        continue
                    for hg in range(2):
                        ps = k.ps()
                        for hh in range(4):
                            h = hg * 4 + hh
                            for kc in range(8):
                                k.mm(ps, ps[:, hh * 128:(hh + 1) * 128], W1[:, kc * 4096 + coff + h * 128:kc * 4096 + coff + (h + 1) * 128],
                                     hT[:, kc * 128:(kc + 1) * 128], start=(kc == 0), stop=(kc == 7), R=[W1, hT])
                        qk = qkr.next()
                        if rp is not None:
                            rope_apply(ps, ps.ap, qk, qk.ap, rp, kbr, t1r, 4)
                        else:
                            k.cp("act", qk, qk.ap, ps.ap, R=[ps])
                        k.store(dst_s[hg * 4:(hg + 1) * 4, :, gdst * 128:(gdst + 1) * 128].rearrange("h p t -> p h t"), qk, v3(qk.ap, 4),
                                W=[dt_(nm, (gdst, hg))])
                        yield
                if do_kv:
                    vb = vbr.next()
                    vf = vfr.next()
                    for nb in range(2):
                        ps = k.ps()
                        for kc in range(8):
                            k.mm(ps, ps.ap, hT[:, kc * 128:(kc + 1) * 128], W1[:, kc * 4096 + 2048 + nb * 512:kc * 4096 + 2048 + (nb + 1) * 512],
                                 start=(kc == 0), stop=(kc == 7), R=[hT, W1])
                        k.cp("dve", vb, vb[:, nb * 512:(nb + 1) * 512], ps.ap, R=[ps])
                        if not sq["sample"]:
                            k.cp("act", vf, vf[:, nb * 512:(nb + 1) * 512], ps.ap, R=[ps])
                        yield
                    k.store(v_s[gt_x], vb, vb.ap, W=[dt_("v", gt_x)])
                    if not sq["sample"]:
                        k.store(O["ncv"][sq["s"], c * 128:(c + 1) * 128, :], vf, vf.ap)
                        kf = vfr.next()
                        for nb in range(2):
                            ps = k.ps()
                            for kc in range(8):
                                k.mm(ps, ps.ap, hT[:, kc * 128:(kc + 1) * 128], W1[:, kc * 4096 + 1024 + nb * 512:kc * 4096 + 1024 + (nb + 1) * 512],
                                     start=(kc == 0), stop=(kc == 7), R=[hT, W1])
                            k.cp("act", kf, kf[:, nb * 512:(nb + 1) * 512], ps.ap, R=[ps])
                            yield
                        k.store(O["nck"][sq["s"], c * 128:(c + 1) * 128, :], kf, kf.ap)
                if do_g:
                    sg = sgr.next()
                    for nb in range(2):
                        ps = k.ps()
                        for kc in range(8):
                            k.mm(ps, ps.ap, hT[:, kc * 128:(kc + 1) * 128], W1[:, kc * 4096 + 3072 + nb * 512:kc * 4096 + 3072 + (nb + 1) * 512],
                                 start=(kc == 0), stop=(kc == 7), R=[hT, W1])
                        k.act(sg, sg[:, nb * 512:(nb + 1) * 512], ps.ap, AF.Silu, R=[ps])
                        yield
                    k.store(sg_s[gt_q], sg, sg.ap, W=[dt_("sg", gt_q)])

            curg = None
            for sq in seqs:
                if sq["g"] != curg:
                    curg = sq["g"]
                    k.load(Sh, Sh.ap, mod_s[1, curg, 0], R=[mod_t])
                    k.load(A, A.ap, mod_s[1, curg, 1], R=[mod_t])
                tiles = []
                for c in range(sq["n"]):
                    gt = sq["t0"] + c
                    if sq["sample"]:
                        tiles.append(p1_tile(sq, c, gt, None, False, True, False, I["rope"][:, :, c * 128:(c + 1) * 128], None))
                    else:
                        tiles.append(p1_tile(sq, c, gt, None, True, True, True, None, gt))
                if sq["sample"]:
                    for i in range(MYT):
                        tiles.append(p1_tile(sq, i, None, i, True, False, True, I["rope_my"][:, :, i * 128:(i + 1) * 128], sq["t0"] + i))
                run2(tiles, 3)

    def l1_phase23(lamt, subl):
        with k.scope():
            WO = k.sb([128, 8 * 1024], BF16, "WO1")
            with k.scope():
                stg = Ring(k, 2, [128, 1024], F32, "wstg2")
                load_cols2([(WO, lambda kc: WO[:, kc * 1024:(kc + 1) * 1024], None, (0, 1024))], I["w_out"][1], 0, 1024, stg)
            Gm = k.sb([128, D], F32, "Gm1")
            osb = k.sb([128, MYT * 1024], BF16, "osb")
            KTr = Ring(k, 2, [128, 2304], BF16, "KTr")
            Vr = Ring(k, 2, [128, 18 * 129], BF16, "Vr")
            for t in Vr.t:
                k.ms("pool", t, t.ap, 1.0)
            cfr = Ring(k, 2, [128, 128], F32, "cfr")
            cbr = Ring(k, 2, [128, 128], BF16, "cbr")
            qr = Ring(k, 2, [128, 256], BF16, "qr")
            ptr = Ring(k, 4, [128, 512], BF16, "ptr1")
            Osb = [k.sb([128, 2 * 129], F32, f"Osb{z}") for z in range(2)]
            dn = k.sb([128, 8], F32, "dn1")
            ofa = k.sb([128, MYT * 1024], F32, "ofa")
            ofv = ofa.ap.rearrange("p (t h e) -> p t h e", t=MYT, h=8)
            ofs = k.sb([128, MYT * 1024], F32, "ofs")
            of2 = k.sb([128, 256], F32, "of2")
            of2v = v3(of2.ap, 2)
            sgrp = k.sb([128, 96], F32, "sgrp")
            mhalf32 = k.sb([128, 32], F32, "mhalf32")
            k.ms("dve", mhalf32, mhalf32.ap, -0.5)
            junk = k.sb([128, 512], BF16, "junk2")
            ssr = Ring(k, 2, [128, 8], F32, "ssr2")
            sgl = Ring(k, 2, [128, 1024], BF16, "sgl")
            xr = Ring(k, 2, [128, D], F32, "xr2")
            mbr = Ring(k, 2, [128, 1024], BF16, "mb1")
            mTr = Ring(k, 2, [128, 1024], BF16, "mT1")
            tmpor = Ring(k, 2, [128, 1024], F32, "tmpo1")
            (psO0, psO1), oidx = k.pin(2)
            psO = (psO0, psO1)
            myidx = k.sb([128, 4], mybir.dt.int32, "myidx2")
            k.load(myidx, myidx.ap, I["myrows"])
            x1_all = [dt_("x1", NPS * PT + c) for c in range(ST)]
            curg = None
            for sq in seqs:
                n = sq["n"]
                tok0 = sq["t0"] * 128
                nblk = n + (2 if sq["sample"] else 0)
                nq = MYT if sq["sample"] else n
                items = []
                for h in range(8):
                    hd = dict(h=h, KT=None, V=None, Vv=None, qT={})
                    for qg in range(nq // 2):
                        for z in range(2):
                            for kp in range(nblk // 2):
                                items.append((hd, qg, z, kp))
                Sb, PTb = {}, {}

                def stA(i):
                    hd, qg, z, kp = items[i]
                    h = hd["h"]
                    if hd["KT"] is None:
                        KT = KTr.next()
                        V = Vr.next()
                        Vv = V.ap.rearrange("p (b e) -> p b e", b=18)
                        hd["KT"], hd["V"], hd["Vv"] = KT, V, Vv
                        if sq["sample"]:
                            for blk in range(2):
                                cf = cfr.next()
                                k.load(cf, cf.ap, I["cck"][blk * 128:(blk + 1) * 128, h * 128:(h + 1) * 128])
                                cb = cbr.next()
                                k.cp("dve", cb, cb.ap, cf.ap, R=[cf])
                                ps = k.ps()
                                pb = bank_bf(ps)
                                k.tr(ps, pb[:, 0:128], cb.ap, identb.ap, R=[cb, identb])
                                k.cp("act", KT, KT[:, blk * 128:(blk + 1) * 128], pb[:, 0:128], R=[ps])
                                cf = cfr.next()
                                k.load(cf, cf.ap, I["ccv"][blk * 128:(blk + 1) * 128, h * 128:(h + 1) * 128])
                                k.cp("dve", V, Vv[:, blk, 0:128], cf.ap, R=[cf])
                            koff = 256
                        else:
                            koff = 0
                        k.load(KT, KT[:, koff:koff + n * 128], kT_s[h, :, tok0:tok0 + n * 128],
                               R=[dt_("kT", (sq["t0"] + c, h // 4)) for c in range(n)])
                        for c4 in range(0, n, 4):
                            m4 = min(4, n - c4)
                            k.load(V, Vv[:, koff // 128 + c4:koff // 128 + c4 + m4, 0:128],
                                   v_s[sq["t0"] + c4:sq["t0"] + c4 + m4, :, h * 128:(h + 1) * 128].rearrange("c p e -> p c e"),
                                   R=[dt_("v", sq["t0"] + c4 + c) for c in range(m4)])
                    if qg not in hd["qT"]:
                        qT = qr.next()
                        k.load(qT, qT.ap, qT_s[h, :, tok0 + qg * 256:tok0 + (qg + 1) * 256],
                               R=[dt_("qT", (sq["t0"] + qg * 2 + i_, h // 4)) for i_ in range(2)])
                        hd["qT"][qg] = qT
                    qT = hd["qT"][qg]
                    KT = hd["KT"]
                    P_ = slice(64 * z, 64 * z + 64)
                    pS = k.ps()
                    for i_ in range(2):
                        kb = kp * 2 + i_
                        k.mm(pS, pS[:, i_ * 256:(i_ + 1) * 256], KT[P_, kb * 128:(kb + 1) * 128], qT[P_, :], R=[KT, qT])
                    Sb[i] = pS

                def stB(i):
                    pS = Sb.pop(i)
                    PT_ = ptr.next()
                    k.act(PT_, PT_.ap, pS.ap, AF.Exp, R=[pS], scale=0.125)
                    PTb[i] = PT_

                def stC(i):
                    hd, qg, z, kp = items[i]
                    h = hd["h"]
                    PT_ = PTb.pop(i)
                    V, Vv = hd["V"], hd["Vv"]
                    for i_ in range(2):
                        kb = kp * 2 + i_
                        for qs in range(2):
                            k.mm(psO[qs], psO[qs][:, 0:129], PT_[:, i_ * 256 + qs * 128:i_ * 256 + (qs + 1) * 128], Vv[:, kb, :],
                                 start=(kb == 0), stop=(kb == nblk - 1), R=[PT_, V])
                    if kp != nblk // 2 - 1:
                        return
                    for qs in range(2):
                        k.cp("act" if qs == 0 else "dve", Osb[z], Osb[z][:, qs * 129:(qs + 1) * 129], psO[qs][:, 0:129], R=[psO[qs]])
                    if z != 1:
                        return
                    O0 = Osb[0].ap.rearrange("p (q e) -> p q e", q=2)
                    O1 = Osb[1].ap.rearrange("p (q e) -> p q e", q=2)
                    k.rcp(dn, dn[:, 0:2], O0[:, :, 128], R=[Osb[0]])
                    k.rcp(dn, dn[:, 2:4], O1[:, :, 128], R=[Osb[1]])
                    k.ts("dve", dn, dn[:, 4:6], dn[:, 2:4], lamt[:, 3:4], None, ALU.mult, R=[dn, lamt])
                    dst = ofv[:, qg * 2:qg * 2 + 2, h, :]
                    k.tt("dve", ofa, dst, O0[:, :, 0:128], dn[:, 0:2].unsqueeze(2).to_broadcast([128, 2, 128]), ALU.mult, R=[Osb[0], dn])
                    k.tt("dve", of2, of2v, O1[:, :, 0:128], dn[:, 4:6].unsqueeze(2).to_broadcast([128, 2, 128]), ALU.mult, R=[Osb[1], dn])
                    k.tt("dve", ofa, dst, dst, of2v, ALU.add, R=[ofa, of2])

                LA = 2
                for i in range(-LA, len(items)):
                    if i + LA < len(items):
                        stA(i + LA)
                    if i >= 0:
                        stB(i)
                        stC(i)
                ng = nq * 8
                fl = ofa[:, 0:ng * 128]
                k.tt("dve", ofs, ofs[:, 0:ng * 128], fl, fl, ALU.mult, R=[ofa])
                k.red("dve", sgrp, sgrp[:, 0:ng], v3(ofs[:, 0:ng * 128], ng), R=[ofs])
                k.ts("dve", sgrp, sgrp[:, 32:32 + ng], sgrp[:, 0:ng], 1.0 / 128, 1e-5, ALU.mult, ALU.add, R=[sgrp])
                k.tt("gp", sgrp, sgrp[:, 64:64 + ng], sgrp[:, 32:32 + ng], mhalf32[:, 0:ng], ALU.pow, R=[sgrp, mhalf32])
                k.tt("dve", ofs, v3(ofs[:, 0:ng * 128], ng), v3(fl, ng), sgrp[:, 64:64 + ng].unsqueeze(2).to_broadcast([128, ng, 128]),
                     ALU.mult, R=[ofa, sgrp])
                k.tt("dve", osb, v3(osb[:, 0:ng * 128], ng), v3(ofs[:, 0:ng * 128], ng), subl.ap.unsqueeze(1).to_broadcast([128, ng, 128]),
                     ALU.mult, R=[ofs, subl])
                if sq["g"] != curg:
                    curg = sq["g"]
                    k.load(Gm, Gm.ap, mod_s[1, curg, 2], R=[mod_t])
                def p3_gen(c):
                    gt = sq["t0"] + c
                    sg = sgl.next()
                    k.load(sg, sg.ap, sg_s[gt], R=[dt_("sg", gt)])
                    xt = xr.next()
                    if sq["sample"]:
                        k.gather(xt, xt.ap, x1_s, myidx, myidx[:, c:c + 1], R=x1_all)
                    else:
                        k.load(xt, xt.ap, x1_s[gt * 128:(gt + 1) * 128, :], R=[dt_("x1", gt)])
                    mb, mT, tmpo = mbr.next(), mTr.next(), tmpor.next()
                    k.tt("dve", mb, mb.ap, osb[:, c * 1024:(c + 1) * 1024], sg.ap, ALU.mult, R=[osb, sg])
                    yield
                    if sq["sample"]:
                        out_ap = O["ys"][c * 128:(c + 1) * 128, :]
                    else:
                        out_ap = O["yp"][sq["s"] * 256 + c * 128:sq["s"] * 256 + (c + 1) * 128, :]
                    yield from out_proj_gen(mb, mT, WO, Gm, xt, tmpo, junk, ssr, out_ap, None)

                run2([p3_gen(c) for c in range(nq)])
            k.unpin(oidx)

    layer0()
    if stop_after is None:
        layer1()
    k.finish()
    return k


def make_in_maps(inp):
    cst, rope = _consts()
    f = lambda a: np.ascontiguousarray(np.asarray(a, dtype=np.float32))
    maps = []
    for core in range(8):
        b = core % 2
        m = {
            "xp": f(inp["x_prompt"][core * NPS:(core + 1) * NPS]).reshape(NPS * 256, D),
            "xs": f(inp["x_sample"][b]),
            "cak": f(inp["cache_a_k"][b, 0]).reshape(256, 128),
            "cav": f(inp["cache_a_v"][b, 0]).reshape(256, 128),
            "st0": f(inp["state_rwkv"][b, 0]),
            "cck": f(inp["cache_c_k"][b, 0]).reshape(256, 1024),
            "ccv": f(inp["cache_c_v"][b, 0]).reshape(256, 1024),
            "cvec": f(np.stack([np.asarray(inp["c_ctx"]), np.asarray(inp["c"])[b]], 0)),
            "ada_w": f(inp["ada_w"]), "ada_b": f(inp["ada_b"]), "norm_pre": f(inp["norm_pre"]), "norm_post": f(inp["norm_post"]),
            "w_out": f(inp["w_out"]), "w_in0": f(inp["even_w_in"][0]), "a_sink": f(inp["a_sink"]), "b_mu": f(inp["b_mu"]),
            "b_w0": f(inp["b_w0"][0]), "b_w2": f(inp["b_w2"][0]), "b_a0": f(inp["b_a0"][0]), "b_a2": f(inp["b_a2"][0]),
            "b_kk": f(inp["b_kk"]), "b_ka": f(inp["b_ka"]), "b_rk": f(inp["b_rk"]), "b_ln_w": f(inp["b_ln_w"]), "b_ln_b": f(inp["b_ln_b"]),
            "w_in1": f(inp["odd_w_in"][0]), "c_lq1": f(inp["c_lq1"]), "c_lk1": f(inp["c_lk1"]), "c_lq2": f(inp["c_lq2"]),
            "c_lk2": f(inp["c_lk2"]), "c_subln": f(inp["c_subln"]), "cst": cst, "rope": rope,
            "rope_my": np.ascontiguousarray(rope[:, :, (core // 2) * MYT * 128:(core // 2 + 1) * MYT * 128]),
            "myrows": np.ascontiguousarray(
                (NPS * 256 + (core // 2) * MYT * 128 + np.arange(MYT)[None, :] * 128 + np.arange(128)[:, None]).astype(np.int32)),
        }
        maps.append(m)
    return maps


_CACHE = {}


def kernel(**inputs):
    inp = {k_: np.asarray(v) for k_, v in inputs.items()}
    if "k" not in _CACHE:
        k1 = build()
        _CACHE["k"] = build(plan=frozenset(k1.needed))
    kk = _CACHE["k"]
    maps = make_in_maps(inp)
    res = run_bass_kernel_spmd(kk.nc, maps, core_ids=list(range(8)))
    R = res.results
    B = inp["x_prompt"].shape[0]
    y_p = np.concatenate([R[c]["yp"].reshape(NPS, 256, D) for c in range(8)], 0).astype(np.float32)
    y_s = np.zeros((2, 2048, D), np.float32)
    for c in range(8):
        y_s[c % 2, (c // 2) * MYT * 128:(c // 2 + 1) * MYT * 128] = R[c]["ys"]
    nak = np.concatenate([R[c]["nak"] for c in range(8)], 0).reshape(B, 1, 256, 2, 64).astype(np.float32)
    nav = np.concatenate([R[c]["nav"] for c in range(8)], 0).reshape(B, 1, 256, 2, 64).astype(np.float32)
    nst = np.concatenate([R[c]["nst"] for c in range(8)], 0).reshape(B, 1, 2, 8, 64, 64).astype(np.float32)
    nck = np.concatenate([R[c]["nck"] for c in range(8)], 0).reshape(B, 1, 256, 8, 128).astype(np.float32)
    ncv = np.concatenate([R[c]["ncv"] for c in range(8)], 0).reshape(B, 1, 256, 8, 128).astype(np.float32)
    return (y_p, y_s, nak, nav, nst, nck, ncv)
```

```python
import math
from contextlib import ExitStack, contextmanager
import numpy as np
import concourse.bass as bass
import concourse.mybir as mybir
from concourse.bass_utils import run_bass_kernel_spmd

F32, BF16, F16 = mybir.dt.float32, mybir.dt.bfloat16, mybir.dt.float16
ALU, AF, AX = mybir.AluOpType, mybir.ActivationFunctionType, mybir.AxisListType

D = 1024
NPS = 4
PT = 2
ST = 16
MYT = 4
EVEN_IN = 3456
ODD_IN = 4096
CW = -math.exp(-0.5)
LAM_INIT = 0.8 - 0.6 * math.exp(-0.3 * 1)
NDMA = 12
NDMA_Q = {"sp": 12, "pool": 4}


class T:
    __slots__ = ("ap", "w", "r", "x")

    def __init__(self, ap, x=False):
        self.ap = ap
        self.w = None
        self.r = {}
        self.x = x

    def __getitem__(self, k):
        return self.ap[k]


class Kern:
    def __init__(self, plan=None):
        self.plan = plan
        self.needed = set()
        self.rank = {}
        self.val = {}
        nc = self.nc = bass.Bass("TRN2", target_bir_lowering=False)
        self.h = {"pe": nc.tensor, "act": nc.scalar, "dve": nc.vector, "pool": nc.gpsimd, "sp": nc.sync}
        self.sems = {k: nc.alloc_semaphore("s_" + k) for k in self.h}
        self.cnt = {k: 0 for k in self.h}
        self.seen = {k: {} for k in self.h}
        self.dval = {}
        self.dnext = {"sp": 0, "pool": 0}
        for q in ("sp", "pool"):
            for i in range(NDMA):
                n = f"d{q}{i}"
                self.sems[n] = nc.alloc_semaphore(n)
                self.dval[n] = 0
        self.banks = [T(nc.alloc_psum_tensor(f"bank{i}", [128, 512], F32).ap(), x=True) for i in range(8)]
        self.ring = list(range(8))
        self.rpos = 0
        self.nsb = 0
        self.stacks = []
        self.order = {}
        self.nev = 0
        self.snap = {}
        self.pool_compute = False

    def sb(self, shape, dt=F32, name=None):
        self.nsb += 1
        nm = f"{name or 't'}_{self.nsb}"
        if self.stacks:
            h = self.stacks[-1].enter_context(self.nc.sbuf_tensor(nm, list(shape), dt))
        else:
            h = self.nc.alloc_sbuf_tensor(nm, list(shape), dt)
        return T(h.ap())

    @contextmanager
    def scope(self):
        st = ExitStack()
        self.stacks.append(st)
        try:
            yield
        finally:
            self.barrier()
            self.stacks.pop()
            st.close()

    def ps(self):
        n = len(self.ring)
        for _ in range(n):
            b = self.banks[self.ring[self.rpos % n]]
            self.rpos += 1
            if not (b.w is not None and not b.r):
                return b
        raise RuntimeError("no free PSUM bank")

    def pin(self, n):
        free = [i for i in self.ring if not (self.banks[i].w is not None and not self.banks[i].r)]
        assert len(free) >= n, "not enough free PSUM banks to pin"
        out = free[-n:]
        for i in out:
            self.ring.remove(i)
        return [self.banks[i] for i in out], out

    def unpin(self, idx):
        self.ring.extend(idx)

    def _deps(self, R, W, en=None):
        deps = {}

        def add(s, v):
            if deps.get(s, 0) < v:
                deps[s] = v
        for t in R:
            if t.w:
                add(*t.w)
            if t.x:
                for s, v in t.r.items():
                    if s != en:
                        add(s, v)
        for t in W:
            if t.w:
                add(*t.w)
            for s, v in t.r.items():
                add(s, v)
        return deps

    def _wait(self, en, deps):
        seen = self.seen[en]
        need = []
        for s, v in sorted(deps.items(), key=lambda kv: -self.order.get(kv, 0)):
            if en == "pe" and s == "pe":
                continue
            if seen.get(s, 0) < v:
                need.append((s, v))
                seen[s] = v
                for s2, v2 in self.snap.get((s, v), {}).items():
                    if seen.get(s2, 0) < v2:
                        seen[s2] = v2
        for s, v in need:
            self.needed.add((s, v))
        for s, v in need[:-1]:
            self.h[en].wait_ge(self.sems[s], self.sv(s, v))
        return need[-1] if need else None

    def sv(self, s, v):
        if self.plan is None or s not in self.h:
            return v
        return self.val[(s, v)]

    def _mark(self, ev, R, W):
        for t in R:
            if t.r.get(ev[0], 0) < ev[1]:
                t.r[ev[0]] = ev[1]
        for t in W:
            t.w = ev
            t.r = {}

    def op(self, en, fn, R=(), W=()):
        if en == "gp":
            en = "pool"
        elif en == "pool" and not self.pool_compute:
            en = "dve"
        fw = self._wait(en, self._deps(R, W, en))
        inst = fn(self.h[en])
        if fw is not None:
            inst._wait_ge(self.sems[fw[0]], self.sv(*fw))
        self.cnt[en] += 1
        ev = (en, self.cnt[en])
        if self.plan is None or ev in self.plan:
            inst.then_inc(self.sems[en], 1)
            self.rank[en] = self.rank.get(en, 0) + 1
            self.val[ev] = self.rank[en]
        sn_ = dict(self.seen[en])
        if en != "pe":
            sn_[en] = max(sn_.get(en, 0), 0)
        self.snap[ev] = sn_
        self.nev += 1
        self.order[ev] = self.nev
        self._mark(ev, R, W)

    def dma(self, q, out, in_, R=(), W=(), **kw):
        slot = self.dnext[q]
        self.dnext[q] = (slot + 1) % NDMA_Q[q]
        sn = f"d{q}{slot}"
        deps = self._deps(R, W)
        if self.dval[sn] > 0 and deps.get(sn, 0) < self.dval[sn]:
            deps[sn] = self.dval[sn]
        fw = self._wait(q, deps)
        inst = self.h[q].dma_start(out=out, in_=in_, **kw)
        if fw is not None:
            inst._wait_ge(self.sems[fw[0]], self.sv(*fw))
        self.dval[sn] += 16
        inst.then_inc(self.sems[sn], 16)
        ev = (sn, self.dval[sn])
        self.snap[ev] = dict(self.seen[q])
        self.nev += 1
        self.order[ev] = self.nev
        self._mark(ev, R, W)

    def gather(self, out_t, out_ap, in_ap, idx_t, idx_ap, R=()):
        q = "pool"
        slot = self.dnext[q]
        self.dnext[q] = (slot + 1) % NDMA_Q[q]
        sn = f"d{q}{slot}"
        deps = self._deps(list(R) + [idx_t], [out_t], q)
        if self.dval[sn] > 0 and deps.get(sn, 0) < self.dval[sn]:
            deps[sn] = self.dval[sn]
        fw = self._wait(q, deps)
        if fw is not None:
            self.h[q].wait_ge(self.sems[fw[0]], self.sv(*fw))
        inst = self.h[q].indirect_dma_start(out=out_ap, out_offset=None, in_=in_ap,
                                            in_offset=bass.IndirectOffsetOnAxis(ap=idx_ap, axis=0))
        self.dval[sn] += 16
        inst.then_inc(self.sems[sn], 16)
        ev = (sn, self.dval[sn])
        self.snap[ev] = dict(self.seen[q])
        self.nev += 1
        self.order[ev] = self.nev
        self._mark(ev, list(R) + [idx_t], [out_t])

    def load(self, out_t, out_ap, in_ap, R=(), **kw):
        self.dma("sp", out_ap, in_ap, R=R, W=[out_t], **kw)

    def store(self, out_ap, in_t, in_ap, W=(), **kw):
        self.dma("pool", out_ap, in_ap, R=[in_t], W=W, **kw)

    def barrier(self):
        deps = {k: self.cnt[k] for k in ("pe", "act", "dve", "pool") if self.cnt[k] > 0}
        for n, v in self.dval.items():
            if v > 0:
                deps[n] = v
        for en in self.h:
            seen = self.seen[en]
            for s, v in deps.items():
                if seen.get(s, 0) < v:
                    self.needed.add((s, v))
                    self.h[en].wait_ge(self.sems[s], self.sv(s, v))
                    seen[s] = v

    def mm(self, out_t, out_ap, lhsT, rhs, start=True, stop=True, R=()):
        self.op("pe", lambda e: e.matmul(out_ap, lhsT=lhsT, rhs=rhs, start=start, stop=stop), R=R, W=[out_t])

    def tr(self, out_t, out_ap, in_ap, ident_ap, R=()):
        self.op("pe", lambda e: e.transpose(out_ap, in_ap, ident_ap), R=R, W=[out_t])

    def tt(self, en, out_t, out_ap, a, b, op, R=()):
        self.op(en, lambda e: e.tensor_tensor(out=out_ap, in0=a, in1=b, op=op), R=R, W=[out_t])

    def stt(self, en, out_t, out_ap, a, s, b, op0, op1, R=()):
        en = "dve"
        self.op(en, lambda e: e.scalar_tensor_tensor(out=out_ap, in0=a, scalar=s, in1=b, op0=op0, op1=op1), R=R, W=[out_t])

    def ts(self, en, out_t, out_ap, a, s1, s2, op0, op1=None, R=()):
        if op1 is None:
            self.op(en, lambda e: e.tensor_scalar(out=out_ap, in0=a, scalar1=s1, scalar2=None, op0=op0), R=R, W=[out_t])
        else:
            self.op(en, lambda e: e.tensor_scalar(out=out_ap, in0=a, scalar1=s1, scalar2=s2, op0=op0, op1=op1), R=R, W=[out_t])

    def cp(self, en, out_t, out_ap, a, R=()):
        if en == "act":
            self.op(en, lambda e: e.copy(out=out_ap, in_=a), R=R, W=[out_t])
        else:
            self.op(en, lambda e: e.tensor_copy(out=out_ap, in_=a), R=R, W=[out_t])

    def act(self, out_t, out_ap, a, func, R=(), Wx=(), **kw):
        self.op("act", lambda e: e.activation(out=out_ap, in_=a, func=func, **kw), R=R, W=[out_t] + list(Wx))

    def red(self, en, out_t, out_ap, a, R=(), op=ALU.add):
        self.op(en, lambda e: e.tensor_reduce(out=out_ap, in_=a, axis=AX.X, op=op), R=R, W=[out_t])

    def rcp(self, out_t, out_ap, a, R=()):
        self.op("dve", lambda e: e.reciprocal(out=out_ap, in_=a), R=R, W=[out_t])

    def ms(self, en, out_t, out_ap, v):
        self.op(en, lambda e: e.memset(out_ap, v), W=[out_t])

    def finish(self):
        self.barrier()


class Ring:
    def __init__(self, k, n, shape, dt=F32, name=None):
        self.t = [k.sb(shape, dt, name=(f"{name}{i}" if name else None)) for i in range(n)]
        self.i = 0

    def next(self):
        t = self.t[self.i % len(self.t)]
        self.i += 1
        return t


def v3(ap, a):
    return ap.rearrange("p (a b) -> p a b", a=a)


def pump(gen, n=1):
    if gen is None:
        return
    for _ in range(n):
        try:
            next(gen)
        except StopIteration:
            return


def exhaust(gen):
    if gen is None:
        return
    for _ in gen:
        pass


def _consts():
    i = np.arange(128)
    r, c = i[:, None], i[None, :]
    ident = (r == c)
    U = (r < c)
    Lw = (r > c)
    Ui = (r <= c)
    Li = (r >= c)
    Pm = np.zeros((64, 64), np.float32)
    for a in range(16):
        Pm[a, a + 16] = -1.0
        Pm[a + 16, a] = 1.0
        Pm[a + 32, a + 48] = -1.0
        Pm[a + 48, a + 32] = 1.0
    PmT = np.zeros((128, 128), np.float32)
    PmT[:64, :64] = Pm.T
    PmT[64:, 64:] = Pm.T
    ones = np.ones((128, 128), np.float32)
    sel = np.zeros((128, 256), np.float32)
    sel[0, 0:128] = 1.0
    sel[1, 128:256] = 1.0
    cst = np.concatenate([x.astype(np.float32) for x in (ident, U, Lw, Ui, Li, PmT, ones, sel)], axis=1)
    nf = 16
    inv = 1.0 / (10000.0 ** (np.arange(nf, dtype=np.float32) / nf))
    t = np.arange(2048)
    row = (t // 64).astype(np.float32)
    col = (t % 64).astype(np.float32)
    ang_r = row[:, None] * inv[None]
    ang_c = col[:, None] * inv[None]
    ang = np.concatenate([ang_r, ang_r, ang_c, ang_c], -1).astype(np.float32)
    cos, sin = np.cos(ang).T, np.sin(ang).T
    rope = np.zeros((128, 2, 2048), np.float32)
    rope[:64, 0], rope[64:, 0] = cos, cos
    rope[:64, 1], rope[64:, 1] = sin, sin
    return np.ascontiguousarray(cst), rope


C_ID, C_U, C_LW, C_UI, C_LI, C_PM, C_ONE, C_SEL = [128 * i for i in range(8)]
NCST = 128 * 9


def build(debug=False, stop_after=None, plan=None):
    k = Kern(plan)
    nc = k.nc

    def din(name, shape):
        return nc.dram_tensor(name, list(shape), F32, kind="ExternalInput").ap()

    def dout(name, shape):
        return nc.dram_tensor(name, list(shape), F32, kind="ExternalOutput").ap()

    def dscr(name, shape, dt=F32, dbg=False):
        if dbg and debug:
            return nc.dram_tensor(name, list(shape), dt, kind="ExternalOutput").ap()
        return nc.dram_tensor(name, list(shape), dt).ap()

    I = {}
    for name, shape in [
        ("xp", (NPS * 256, D)), ("xs", (2048, D)), ("cak", (256, 128)), ("cav", (256, 128)),
        ("st0", (2, 8, 64, 64)), ("cck", (256, 1024)), ("ccv", (256, 1024)), ("cvec", (2, D)),
        ("ada_w", (2, D, 3 * D)), ("ada_b", (2, 3 * D)), ("norm_pre", (2, D)), ("norm_post", (2, D)),
        ("w_out", (2, D, D)), ("w_in0", (D, EVEN_IN)), ("a_sink", (1, 8)), ("b_mu", (1, 1664)),
        ("b_w0", (2, 512)), ("b_w2", (2, 64, 512)), ("b_a0", (2, 512)), ("b_a2", (2, 64, 512)),
        ("b_kk", (1, 512)), ("b_ka", (1, 512)), ("b_rk", (1, 512)), ("b_ln_w", (1, 512)), ("b_ln_b", (1, 512)),
        ("w_in1", (D, ODD_IN)), ("c_lq1", (1, 64)), ("c_lk1", (1, 64)), ("c_lq2", (1, 64)), ("c_lk2", (1, 64)),
        ("c_subln", (1, 128)), ("cst", (128, NCST)), ("rope", (128, 2, 2048)), ("rope_my", (128, 2, 512)),
    ]:
        I[name] = din(name, shape)
    I["myrows"] = nc.dram_tensor("myrows", [128, 4], mybir.dt.int32, kind="ExternalInput").ap()
    O = {}
    for name, shape in [("yp", (NPS * 256, D)), ("ys", (MYT * 128, D)), ("nak", (NPS, 256, 128)), ("nav", (NPS, 256, 128)),
                        ("nst", (NPS, 2, 8, 64, 64)), ("nck", (NPS, 256, 1024)), ("ncv", (NPS, 256, 1024))]:
        O[name] = dout(name, shape)

    NT = NPS * PT + ST
    hT_s = dscr("hT_s", (NT, 128, 1024), BF16)
    hsT_s = dscr("hsT_s", (NT, 128, 1024), BF16)
    xf_s = dscr("xf_s", (NT, 128, 1536), F32)
    lora_s = dscr("lora_s", (NT, 128, 128), F32)
    yf_s = dscr("yf_s", (NT, 128, 512), F32, dbg=True)
    yb_s = dscr("yb_s", (NT, 128, 512), F32)
    x1_s = dscr("x1_s", (NT * 128, D), F32, dbg=True)
    DT = {}

    def dt_(name, i):
        key = (name, i)
        if key not in DT:
            DT[key] = T(None)
        return DT[key]

    seqs = []
    for s in range(NPS):
        seqs.append(dict(n=PT, x=I["xp"][s * 256:(s + 1) * 256, :], t0=s * PT, g=0, kc0=2304 + 256 * s, sample=False, s=s))
    seqs.append(dict(n=ST, x=I["xs"], t0=NPS * PT, g=1, kc0=256, sample=True, s=None))

    cst = k.sb([128, NCST], F32, "cst")
    k.load(cst, cst.ap, I["cst"])
    cws = k.sb([128, 5 * 128], F32, "cws")
    for i, off in enumerate((C_UI, C_LI, C_U, C_LW, C_ONE)):
        k.ts("dve", cws, cws[:, i * 128:(i + 1) * 128], cst[:, off:off + 128], CW, None, ALU.mult, R=[cst])
    W_UI, W_LI, W_U, W_LW, W_ONE = [cws[:, i * 128:(i + 1) * 128] for i in range(5)]
    cI = [W_UI, W_LI]
    cE = [W_U, W_LW]
    cR = [W_LW, W_U]
    M4 = [k.sb([128, 512], F32, f"M4_{z}") for z in range(2)]
    for z, (a, b) in enumerate(((C_U, C_UI), (C_LW, C_LI))):
        for q in range(2):
            k.cp("dve", M4[z], M4[z][:, q * 256:q * 256 + 128], cst[:, a:a + 128], R=[cst])
            k.cp("dve", M4[z], M4[z][:, q * 256 + 128:q * 256 + 256], cst[:, b:b + 128], R=[cst])
    mS = [cst[:, C_LW:C_LW + 128], cst[:, C_U:C_U + 128]]
    identb = k.sb([128, 128], BF16, "identb")
    k.cp("dve", identb, identb.ap, cst[:, C_ID:C_ID + 128], R=[cst])
    pmtb = k.sb([128, 128], BF16, "pmtb")
    k.cp("dve", pmtb, pmtb.ap, cst[:, C_PM:C_PM + 128], R=[cst])
    identf = cst[:, C_ID:C_ID + 128]
    mhalf = k.sb([128, 16], F32, "mhalf")
    k.ms("dve", mhalf, mhalf.ap, -0.5)

    def bank_bf(b):
        return b.ap.bitcast(BF16)

    def load_bc(dst, src_row_ap, n=128):
        k.load(dst, dst.ap, src_row_ap.partition_broadcast(n))

    def modulation(l, Sh, A, G):
        with k.scope():
            cT = k.sb([128, 8, 2], F32, "cT")
            for g in range(2):
                cg = k.sb([8, 128], F32, "cg")
                k.load(cg, cg.ap, I["cvec"][g].rearrange("(c p) -> c p", p=128))
                psc = k.ps()
                k.mm(psc, psc[:, 0:8], cg.ap, cst[0:8, C_ID:C_ID + 8], R=[cg, cst])
                k.cp("dve", cT, cT[:, :, g], psc[:, 0:8], R=[psc])
            cTs = k.sb([128, 8, 2], F32, "cTs")
            k.act(cTs, cTs.ap, cT.ap, AF.Silu, R=[cT])
            stg = Ring(k, 2, [128, 3 * D], F32, "adastg")
            adab = k.sb([2, 3 * D], F32, "adab")
            load_bc(adab, I["ada_b"][l:l + 1, :], 2)
            npre = k.sb([128, D], F32, "npre")
            npost = k.sb([128, D], F32, "npost")
            load_bc(npre, I["norm_pre"][l:l + 1, :])
            load_bc(npost, I["norm_post"][l:l + 1, :])
            banks, idx = k.pin(6)
            for kc in range(8):
                st = stg.next()
                k.load(st, st.ap, I["ada_w"][l, kc * 128:(kc + 1) * 128, :])
                for nb in range(6):
                    k.mm(banks[nb], banks[nb][0:2, :], cTs[:, kc, :], st[:, nb * 512:(nb + 1) * 512],
                         start=(kc == 0), stop=(kc == 7), R=[cTs, st])
            mod2 = k.sb([2, 3 * D], F32, "mod2")
            for nb in range(6):
                k.tt("dve", mod2, mod2[0:2, nb * 512:(nb + 1) * 512], banks[nb][0:2, :], adab[0:2, nb * 512:(nb + 1) * 512],
                     ALU.add, R=[banks[nb], adab])
            k.unpin(idx)
            for g in range(2):
                for nb in range(6):
                    ps = k.ps()
                    k.mm(ps, ps.ap, cst[0:2, C_SEL + g * 128:C_SEL + (g + 1) * 128], mod2[0:2, nb * 512:(nb + 1) * 512], R=[cst, mod2])
                    cs = slice((nb % 2) * 512, (nb % 2) * 512 + 512)
                    if nb < 2:
                        k.cp("act", Sh[g], Sh[g][:, cs], ps.ap, R=[ps])
                    elif nb < 4:
                        k.stt("dve", A[g], A[g][:, cs], ps.ap, 1.0, npre[:, cs], ALU.add, ALU.mult, R=[ps, npre])
                    else:
                        k.tt("dve", G[g], G[g][:, cs], ps.ap, npost[:, cs], ALU.mult, R=[ps, npost])

    def load_cols(dsts, w_ap, c0, ncols, stg):
        for kc in range(8):
            st = stg.next()
            k.load(st, st[:, 0:ncols], w_ap[kc * 128:(kc + 1) * 128, c0:c0 + ncols])
            for i, (dt, fn, sc) in enumerate(dsts):
                en = "dve" if (kc + i) % 2 == 0 else "pool"
                if sc is None:
                    k.cp(en, dt, fn(kc), st[:, 0:ncols], R=[st])
                else:
                    k.tt(en, dt, fn(kc), st[:, 0:ncols], sc[:, 0:ncols], ALU.mult, R=[st, sc])


    mod_s = dscr("mod_s", (2, 2, 3, 128, D), F32)
    mod_t = T(None)

    def modulation_to_dram(l):
        with k.scope():
            Sh = [k.sb([128, D], F32, "Sh") for _ in range(2)]
            A = [k.sb([128, D], F32, "A") for _ in range(2)]
            G = [k.sb([128, D], F32, "G") for _ in range(2)]
            modulation(l, Sh, A, G)
            for g in range(2):
                for i, t in enumerate((Sh[g], A[g], G[g])):
                    k.store(mod_s[l, g, i], t, t.ap, W=[mod_t])

    def load_cols2(dsts, w_ap, c0, ncols, stg):
        for kc in range(8):
            st = stg.next()
            k.load(st, st[:, 0:ncols], w_ap[kc * 128:(kc + 1) * 128, c0:c0 + ncols])
            for i, (dt, fn, sc, (a, b)) in enumerate(dsts):
                en = "dve" if (kc + i) % 2 == 0 else "act"
                if sc is None:
                    k.cp(en, dt, fn(kc), st[:, a:b], R=[st])
                else:
                    k.tt("dve", dt, fn(kc), st[:, a:b], sc[:, a:b], ALU.mult, R=[st, sc])

    def run2(gens, width=2):
        active = []
        it = iter(gens)
        while True:
            while len(active) < width:
                g = next(it, None)
                if g is None:
                    break
                active.append(g)
            if not active:
                break
            for g in list(active):
                try:
                    next(g)
                except StopIteration:
                    active.remove(g)


    def rstd_of(ss, junk, parts, n_el, eps):
        k.ms("dve", ss, ss.ap, 0.0)
        for i, (t, ap) in enumerate(parts):
            k.act(junk, junk[:, 0:ap.shape[1]], ap, AF.Square, R=[t, ss], Wx=[ss], scale=float(n_el) ** -0.5,
                  accum_out=ss[:, i:i + 1])
        if len(parts) == 2:
            k.tt("dve", ss, ss[:, 0:1], ss[:, 0:1], ss[:, 1:2], ALU.add, R=[ss])
        k.ts("dve", ss, ss[:, 4:5], ss[:, 0:1], eps, None, ALU.add, R=[ss])
        k.tt("gp", ss, ss[:, 6:7], ss[:, 4:5], mhalf[:, 0:1], ALU.pow, R=[ss, mhalf])
        return ss[:, 6:7]

    def layer0():
        with k.scope():
            K2T = k.sb([128, 2, 3328], BF16, "K2T")
            Vall = k.sb([128, 26 * 2 * 65], BF16, "Vall")
            Vv = Vall.ap.rearrange("p (b g e) -> p b g e", b=26, g=2)
            k.ms("pool", Vall, Vall.ap, 1.0)
            with k.scope():
                for blk in range(2):
                    cvt = k.sb([128, 128], F32, "cvt")
                    k.load(cvt, cvt.ap, I["cav"][blk * 128:(blk + 1) * 128, :])
                    k.cp("dve", Vall, Vv[:, blk, :, 0:64], v3(cvt.ap, 2), R=[cvt])
                    ckt = k.sb([128, 128], F32, "ckt")
                    k.load(ckt, ckt.ap, I["cak"][blk * 128:(blk + 1) * 128, :])
                    kd = k.sb([128, 256], BF16, "kd")
                    for g in range(2):
                        for r in range(2):
                            k.cp("dve", kd, kd[:, g * 128 + r * 64:g * 128 + r * 64 + 64], ckt[:, g * 64:(g + 1) * 64], R=[ckt])
                    ps = k.ps()
                    pb = bank_bf(ps)
                    for g in range(2):
                        k.tr(ps, pb[:, g * 128:(g + 1) * 128], kd[:, g * 128:(g + 1) * 128], identb.ap, R=[kd, identb])
                    for g in range(2):
                        k.cp("act", K2T, K2T[:, g, blk * 128:(blk + 1) * 128], pb[:, g * 128:(g + 1) * 128], R=[ps])
            phaseF(K2T, Vall, Vv)
            if stop_after in ("F", "F1", "F2a"):
                return
            phaseB(K2T, Vall, Vv)

    def rw_consts(names):
        out = {}
        for nm in names:
            if nm in ("kkv", "kav", "rkv", "lnw", "lnb"):
                src = {"kkv": "b_kk", "kav": "b_ka", "rkv": "b_rk", "lnw": "b_ln_w", "lnb": "b_ln_b"}[nm]
                t = k.sb([128, 512], F32, nm)
                load_bc(t, I[src][0:1, :])
            elif nm[:2] in ("w0", "a0"):
                z = int(nm[2])
                t = k.sb([128, 512], F32, nm)
                load_bc(t, I["b_" + nm[:2]][z:z + 1, :])
            elif nm[:2] == "w2":
                z = int(nm[2])
                t = k.sb([128, 512], F32, nm)
                k.load(t, t[0:64, :], I["b_w2"][z])
            elif nm[:2] == "a2":
                z = int(nm[2])
                t = k.sb([128, 512], F32, nm)
                k.load(t, t[64:128, :], I["b_a2"][z])
            out[nm] = t
        return out

    def rw_temps(dbl=False):
        tm = {}
        for nm in ("kkr", "kk", "sg", "Ei", "Einv", "Ee", "Er", "beta"):
            tm[nm] = k.sb([128, 512], F32, nm)
        for nm in ("a0t", "keff0", "a1t", "keff1"):
            tm[nm] = k.sb([128, 512], F32, nm)
        tm["sq"], tm["wpre"], tm["apre0"], tm["apre1"] = tm["Ee"], tm["Er"], tm["Ei"], tm["Einv"]
        for nm in ("vb", "At", "Rt", "Bh", "Kh", "Bt", "Kt"):
            tm[nm] = k.sb([128, 512], BF16, nm)
        tm["s8"] = k.sb([128, 32], F32, "s8")
        tm["lTt"] = k.sb([128, 128], F32, "lTt")
        tm["FAR"] = k.sb([128, 1024], BF16, "FAR")
        tm["FBK"] = k.sb([128, 1024], BF16, "FBK")
        tm["G1"] = Ring(k, 4, [128, 512], BF16, "G1")
        tm["XX"] = Ring(k, 8, [128, 256], F32, "XX")
        tm["ZZ"] = Ring(k, 8, [128, 128], F32, "ZZ")
        tm["Zb"] = Ring(k, 4, [128, 128], BF16, "Zb")
        tm["OmT"] = Ring(k, 4, [64, 128], BF16, "OmT")
        tm["PP"] = k.sb([64, 8 * 128], F32, "PP")
        tm["dW"] = k.sb([64, 512], F32, "dW")
        tm["Wt"] = k.sb([64, 512], F32, "Wt")
        tm["bon"] = k.sb([128, 512], F32, "bon")
        if not dbl:
            return tm
        tm2 = dict(tm)
        for nm in ("vb", "At", "Rt", "Bh", "Kh", "Bt", "Kt"):
            tm2[nm] = k.sb([128, 512], BF16, nm + "2")
        tm2["FAR"] = k.sb([128, 1024], BF16, "FAR2")
        tm2["FBK"] = k.sb([128, 1024], BF16, "FBK2")
        tm2["dW"] = k.sb([64, 512], F32, "dW2")
        tm2["Wt"] = k.sb([64, 512], F32, "Wt2")
        tm2["bon"] = k.sb([128, 512], F32, "bon2")
        return tm, tm2

    def rw_feats(C, tm, xf, lT, zdec, za):
        r, kx, v = xf[:, 0:512], xf[:, 512:1024], xf[:, 1024:1536]
        k.cp("act", tm["vb"], tm["vb"].ap, v, R=[xf])
        yield
        k.tt("dve", tm["kkr"], tm["kkr"].ap, kx, C["kkv"].ap, ALU.mult, R=[xf, C["kkv"]])
        k.tt("pool", tm["sq"], tm["sq"].ap, tm["kkr"].ap, tm["kkr"].ap, ALU.mult, R=[tm["kkr"]])
        s8 = tm["s8"]
        k.red("dve", s8, s8[:, 0:8], v3(tm["sq"].ap, 8), R=[tm["sq"]])
        k.ts("dve", s8, s8[:, 8:16], s8[:, 0:8], 1e-24, None, ALU.max, R=[s8])
        k.tt("gp", s8, s8[:, 24:32], s8[:, 8:16], mhalf[:, 0:8], ALU.pow, R=[s8, mhalf])
        k.tt("dve", tm["kk"], v3(tm["kk"].ap, 8), v3(tm["kkr"].ap, 8), s8[:, 24:32].unsqueeze(2).to_broadcast([128, 8, 64]),
             ALU.mult, R=[tm["kkr"], s8])
        yield
        lTt = tm["lTt"]
        k.act(lTt, lTt[0:64, :], lT[0:64, :], AF.Tanh, R=[lT])
        k.cp("act", lTt, lTt[64:128, :], lT[64:128, :], R=[lT])
        yield
        ps = k.ps()
        k.mm(ps, ps.ap, lTt[0:64, :], C[f"w2{zdec}"][0:64, :], R=[lTt, C[f"w2{zdec}"]])
        k.tt("dve", tm["wpre"], tm["wpre"].ap, ps.ap, C[f"w0{zdec}"].ap, ALU.add, R=[ps, C[f"w0{zdec}"]])
        k.act(tm["sg"], tm["sg"].ap, tm["wpre"].ap, AF.Sigmoid, R=[tm["wpre"]])
        yield
        for z in za:
            ps = k.ps()
            k.mm(ps, ps.ap, lTt[64:128, :], C[f"a2{z}"][64:128, :], R=[lTt, C[f"a2{z}"]])
            ap_, a_, ke_ = tm[f"apre{z}"], tm[f"a{z}t"], tm[f"keff{z}"]
            k.tt("dve", ap_, ap_.ap, ps.ap, C[f"a0{z}"].ap, ALU.add, R=[ps, C[f"a0{z}"]])
            k.act(a_, a_.ap, ap_.ap, AF.Sigmoid, R=[ap_])
            k.stt("pool", ap_, ap_.ap, a_.ap, -1.0, C["kav"].ap, ALU.add, ALU.mult, R=[a_, C["kav"]])
            k.stt("dve", ke_, ke_.ap, ap_.ap, 1.0, kx, ALU.add, ALU.mult, R=[ap_, xf])
            yield
        if len(za) == 2:
            bt, s8b = tm["beta"], tm["s8"]
            k.tt("dve", bt, bt.ap, tm["keff0"].ap, tm["keff1"].ap, ALU.add, R=[tm["keff0"], tm["keff1"]])
            k.tt("dve", bt, bt.ap, bt.ap, r, ALU.mult, R=[bt, xf])
            k.tt("dve", bt, bt.ap, bt.ap, C["rkv"].ap, ALU.mult, R=[bt, C["rkv"]])
            k.red("dve", s8b, s8b[:, 0:8], v3(bt.ap, 8), R=[bt])
            k.ts("dve", s8b, s8b[:, 0:8], s8b[:, 0:8], 0.5, None, ALU.mult, R=[s8b])
            k.tt("dve", tm["bon"], v3(tm["bon"].ap, 8), v3(v, 8), s8b[:, 0:8].unsqueeze(2).to_broadcast([128, 8, 64]), ALU.mult,
                 R=[xf, s8b])
            yield
        z = zdec
        sg = tm["sg"]
        psI, psE, psR, psT = k.ps(), k.ps(), k.ps(), k.ps()
        k.mm(psI, psI.ap, cI[z], sg.ap, R=[cws, sg])
        k.mm(psE, psE.ap, cE[z], sg.ap, R=[cws, sg])
        k.mm(psR, psR.ap, cR[z], sg.ap, R=[cws, sg])
        k.mm(psT, psT[0:64, :], W_ONE[:, 0:64], sg.ap, R=[cws, sg])
        k.act(tm["Ei"], tm["Ei"].ap, psI.ap, AF.Exp, R=[psI])
        k.act(tm["Einv"], tm["Einv"].ap, psI.ap, AF.Exp, R=[psI], scale=-1.0)
        k.act(tm["Ee"], tm["Ee"].ap, psE.ap, AF.Exp, R=[psE])
        k.act(tm["Er"], tm["Er"].ap, psR.ap, AF.Exp, R=[psR])
        k.act(tm["Wt"], tm["Wt"].ap, psT[0:64, :], AF.Exp, R=[psT])
        yield
        a_, ke_ = tm[f"a{z}t"], tm[f"keff{z}"]
        k.stt("dve", tm["At"], tm["At"].ap, tm["kk"].ap, -1.0, tm["Ee"].ap, ALU.mult, ALU.mult, R=[tm["kk"], tm["Ee"]])
        k.tt("pool", tm["Rt"], tm["Rt"].ap, r, tm["Ei"].ap, ALU.mult, R=[xf, tm["Ei"]])
        yield
        k.tt("dve", tm["beta"], tm["beta"].ap, tm["kk"].ap, a_.ap, ALU.mult, R=[tm["kk"], a_])
        k.tt("dve", tm["Bh"], tm["Bh"].ap, tm["beta"].ap, tm["Einv"].ap, ALU.mult, R=[tm["beta"], tm["Einv"]])
        k.tt("pool", tm["Kh"], tm["Kh"].ap, ke_.ap, tm["Einv"].ap, ALU.mult, R=[ke_, tm["Einv"]])
        yield
        k.tt("dve", tm["Bt"], tm["Bt"].ap, tm["beta"].ap, tm["Er"].ap, ALU.mult, R=[tm["beta"], tm["Er"]])
        k.tt("pool", tm["Kt"], tm["Kt"].ap, ke_.ap, tm["Er"].ap, ALU.mult, R=[ke_, tm["Er"]])
        yield
        k.tt("dve", tm["dW"], v3(tm["dW"].ap, 8), cst[0:64, C_ID:C_ID + 64].unsqueeze(1).to_broadcast([64, 8, 64]),
             v3(tm["Wt"].ap, 8), ALU.mult, R=[cst, tm["Wt"]])
        for (dst, srcs) in ((tm["FAR"], (tm["At"], tm["Rt"])), (tm["FBK"], (tm["Bh"], tm["Kh"]))):
            ps = k.ps()
            pb = bank_bf(ps)
            for j in range(4):
                for x_, src in enumerate(srcs):
                    o = (j * 2 + x_) * 128
                    k.tr(ps, pb[:, o:o + 128], src[:, j * 128:(j + 1) * 128], identb.ap, R=[src, identb])
            k.cp("act", dst, dst.ap, pb, R=[ps])
            yield

    def rw_chunk(tm, z, S, S0b, bg=None):
        (psY, psP0, psP1), pidx = k.pin(3)
        psPP = (psP0, psP1)
        FAR, FBK, vb = tm["FAR"], tm["FBK"], tm["vb"]
        At, Rt, Bt, Kt = tm["At"], tm["Rt"], tm["Bt"], tm["Kt"]
        for grp in range(2):
            heads = [grp * 4 + i for i in range(4)]
            G1, XZ, XT = {}, {}, {}
            for pr2 in range(2):
                hp = heads[2 * pr2:2 * pr2 + 2]
                ops_ = {}
                for h in hp:
                    j, p = h // 2, h % 2
                    P_ = slice(64 * p, 64 * p + 64)
                    ops_[h] = dict(far=FAR[P_, j * 256:(j + 1) * 256], bT=FBK[P_, j * 256:j * 256 + 128],
                                   kT=FBK[P_, j * 256 + 128:j * 256 + 256], aT=FAR[P_, j * 256:j * 256 + 128],
                                   ps1=k.ps(), ps3=k.ps())
                for h in hp:
                    o_ = ops_[h]
                    k.mm(o_["ps1"], o_["ps1"][:, 0:256], o_["bT"], o_["far"], R=[FBK, FAR])
                for h in hp:
                    o_ = ops_[h]
                    k.mm(o_["ps1"], o_["ps1"][:, 256:512], o_["kT"], o_["far"], R=[FBK, FAR])
                for h in hp:
                    o_ = ops_[h]
                    k.mm(o_["ps3"], o_["ps3"][:, 0:128], o_["aT"], o_["bT"], R=[FBK, FAR])
                for h in hp:
                    ps1, ps3 = ops_[h]["ps1"], ops_[h]["ps3"]
                    g1 = tm["G1"].next()
                    k.tt("dve", g1, g1.ap, ps1.ap, M4[z].ap, ALU.mult, R=[ps1, M4[z]])
                    xx = tm["XX"].next()
                    zz = tm["ZZ"].next()
                    k.tt("dve", xx, xx[:, 0:128], ps3[:, 0:128], mS[z], ALU.mult, R=[ps3, cst])
                    k.tt("dve", xx, xx[:, 128:256], ps1[:, 0:128], M4[z][:, 0:128], ALU.mult, R=[ps1, M4[z]])
                    k.cp("act", zz, zz[:, 0:64], At[:, h * 64:(h + 1) * 64], R=[At])
                    G1[h], XZ[h], XT[h] = g1, xx, zz
                for h in hp:
                    ps4 = k.ps()
                    k.mm(ps4, ps4[:, 0:64], G1[h][:, 256:384], vb[:, h * 64:(h + 1) * 64], R=[G1[h], vb])
                    k.cp("act", XT[h], XT[h][:, 64:128], ps4[:, 0:64], R=[ps4])
            pump(bg, 2)
            for i in range(7):
                pr_ = {}
                for h in heads:
                    b_ = k.ps()
                    pr_[h] = b_
                    xx, zz = XZ[h], XT[h]
                    if i < 6:
                        k.mm(b_, b_[:, 0:128], xx[:, 128:256], xx[:, 0:128], R=[xx])
                        k.mm(b_, b_[:, 128:256], xx[:, 0:128], xx[:, 128:256], R=[xx])
                    k.mm(b_, b_[:, 256:384], xx[:, 128:256], zz.ap, R=[xx, zz])
                for h in heads:
                    b_ = pr_[h]
                    xx, zz = XZ[h], XT[h]
                    nzz = tm["ZZ"].next()
                    k.tt("dve", nzz, nzz.ap, b_[:, 256:384], zz.ap, ALU.add, R=[b_, zz])
                    if i < 6:
                        nxx = tm["XX"].next()
                        k.cp("act", nxx, nxx.ap, b_[:, 0:256], R=[b_])
                        XZ[h] = nxx
                    XT[h] = nzz
                pump(bg, 2)
            Zs, oms = {}, {}
            for h in heads:
                Z = tm["Zb"].next()
                k.cp("act", Z, Z.ap, XT[h].ap, R=[XT[h]])
                Zs[h] = Z
            psOm = k.ps()
            for q_, h in enumerate(heads):
                Z = Zs[h]
                Ah, Gh = Z[:, 0:64], Z[:, 64:128]
                hs_ = slice(h * 64, (h + 1) * 64)
                bank = psPP[h // 4]
                o = (h % 4) * 128
                k.mm(bank, bank[0:64, o:o + 64], Ah, Bt[:, hs_], R=[Z, Bt])
                k.mm(bank, bank[0:64, o + 64:o + 128], Bt[:, hs_], Gh, start=True, stop=False, R=[Z, Bt])
                k.mm(bank, bank[0:64, o + 64:o + 128], Kt[:, hs_], vb[:, hs_], start=False, stop=True, R=[Kt, vb])
                k.mm(psOm, psOm[0:64, q_ * 128:(q_ + 1) * 128], Ah, G1[h][:, 128:256], start=True, stop=False, R=[Z, G1[h]])
                k.mm(psOm, psOm[0:64, q_ * 128:(q_ + 1) * 128], Rt[:, hs_], identb.ap, start=False, stop=True, R=[Rt, identb])
            for q_, h in enumerate(heads):
                om = tm["OmT"].next()
                k.cp("act", om, om.ap, psOm[0:64, q_ * 128:(q_ + 1) * 128], R=[psOm])
                oms[h] = om
            for h in heads:
                Z = Zs[h]
                Gh = Z[:, 64:128]
                hs_ = slice(h * 64, (h + 1) * 64)
                k.mm(psY, psY[:, hs_], G1[h][:, 128:256], Gh, start=True, stop=False, R=[G1[h], Z])
                k.mm(psY, psY[:, hs_], G1[h][:, 384:512], vb[:, hs_], start=False, stop=False, R=[G1[h], vb])
                k.mm(psY, psY[:, hs_], oms[h].ap, S0b[0:64, hs_], start=False, stop=True, R=[oms[h], S0b])
            pump(bg, 2)
        PP = tm["PP"]
        PPv = v3(PP.ap, 8)
        for b in range(2):
            pv = v3(psPP[b][0:64, :], 4)
            k.tt("dve", PP, PPv[:, b * 4:(b + 1) * 4, 0:64], pv[:, :, 0:64], v3(tm["dW"].ap, 8)[:, b * 4:(b + 1) * 4, :], ALU.add,
                 R=[psPP[b], tm["dW"]])
            k.cp("act", PP, PPv[:, b * 4:(b + 1) * 4, 64:128], pv[:, :, 64:128], R=[psPP[b]])
        psS = k.ps()
        for h in range(8):
            k.mm(psS, psS[0:64, h * 64:(h + 1) * 64], PP[:, h * 128:h * 128 + 64], S[0:64, h * 64:(h + 1) * 64], R=[PP, S])
        k.tt("dve", S, v3(S.ap, 8), v3(psS[0:64, :], 8), PPv[:, :, 64:128], ALU.add, R=[psS, PP])
        k.cp("pool", S0b, S0b.ap, S.ap, R=[S])
        k.unpin(pidx[1:])
        return psY, pidx[0:1]

    def phaseF(K2T, Vall, Vv):
        with k.scope():
            Wkv = k.sb([128, 8 * 256], BF16, "Wkv")
            WK2 = k.sb([128, 8 * 256], BF16, "WK2")
            Wr1 = k.sb([128, 8 * 1664], BF16, "Wr1")
            Wr2 = k.sb([128, 8 * 1664], BF16, "Wr2")
            with k.scope():
                stg = Ring(k, 2, [128, 1664], F32, "wstg")
                mu1 = k.sb([128, 1664], F32, "mu1")
                mu2 = k.sb([128, 1664], F32, "mu2")
                load_bc(mu1, I["b_mu"][0:1, :])
                k.ts("dve", mu2, mu2.ap, mu1.ap, 0.5, None, ALU.mult, R=[mu1])
                k.ts("dve", mu1, mu1.ap, mu1.ap, -1.0, 1.0, ALU.mult, ALU.add, R=[mu1])
                d = [(Wkv, lambda kc: Wkv[:, kc * 256:(kc + 1) * 256], None, (0, 256))]
                for g in range(2):
                    for r in range(2):
                        d.append((WK2, (lambda kc, g=g, r=r: WK2[:, kc * 256 + g * 128 + r * 64:kc * 256 + g * 128 + r * 64 + 64]),
                                  None, (g * 64, g * 64 + 64)))
                load_cols2(d, I["w_in0"], 512, 256, stg)
                load_cols2([(Wr1, lambda kc: Wr1[:, kc * 1664:(kc + 1) * 1664], mu1, (0, 1664)),
                            (Wr2, lambda kc: Wr2[:, kc * 1664:(kc + 1) * 1664], mu2, (0, 1664))], I["w_in0"], 1280, 1664, stg)
                modulation_to_dram(0)
                Sh = k.sb([128, D], F32, "ShF")
                A = k.sb([128, D], F32, "AF")
                xr = Ring(k, 2, [128, D], F32, "xr")
                tmpfr = Ring(k, 2, [128, D], F32, "tmpf")
                hbr = Ring(k, 2, [128, D], BF16, "hb")
                junk = k.sb([128, D], BF16, "junk")
                ssr = Ring(k, 2, [128, 8], F32, "ssr")
                hTr = Ring(k, 4, [128, 1024], BF16, "hTr")
                hsr = Ring(k, 2, [128, 1024], BF16, "hsr")
                curg = None
                for sq in seqs:
                    if sq["g"] != curg:
                        curg = sq["g"]
                        k.load(Sh, Sh.ap, mod_s[0, curg, 0], R=[mod_t])
                        k.load(A, A.ap, mod_s[0, curg, 1], R=[mod_t])
                    n = sq["n"]
                    hTs = {}

                    def make_hs(cm):
                        cur = v3(hTs[cm].ap, 8)
                        hs = hsr.next()
                        hv = v3(hs.ap, 8)
                        k.tt("dve", hs, hv[:, :, 1:127], cur[:, :, 0:126], cur[:, :, 2:128], ALU.add, R=[hTs[cm]])
                        if cm > 0:
                            k.tt("pool", hs, hv[:, :, 0:1], cur[:, :, 1:2], v3(hTs[cm - 1].ap, 8)[:, :, 127:128], ALU.add,
                                 R=[hTs[cm], hTs[cm - 1]])
                        else:
                            k.cp("pool", hs, hv[:, :, 0:1], cur[:, :, 1:2], R=[hTs[cm]])
                        if cm < n - 1:
                            k.tt("pool", hs, hv[:, :, 127:128], cur[:, :, 126:127], v3(hTs[cm + 1].ap, 8)[:, :, 0:1], ALU.add,
                                 R=[hTs[cm], hTs[cm + 1]])
                        else:
                            k.cp("pool", hs, hv[:, :, 127:128], cur[:, :, 126:127], R=[hTs[cm]])
                        k.store(hsT_s[sq["t0"] + cm], hs, hs.ap, W=[dt_("hsT", sq["t0"] + cm)])

                    def f1_gen(c, sq=sq, n=n, hTs=hTs, make_hs=make_hs):
                        xt = xr.next()
                        k.load(xt, xt.ap, sq["x"][c * 128:(c + 1) * 128, :])
                        ss_t = ssr.next()
                        rstd = rstd_of(ss_t, junk, [(xt, xt.ap)], D, 1e-6)
                        yield
                        tmpf = tmpfr.next()
                        k.stt("dve", tmpf, tmpf.ap, xt.ap, rstd, A.ap, ALU.mult, ALU.mult, R=[xt, ss_t, A])
                        hb = hbr.next()
                        k.tt("pool", hb, hb.ap, tmpf.ap, Sh.ap, ALU.add, R=[tmpf, Sh])
                        yield
                        ps = k.ps()
                        pb = bank_bf(ps)
                        for kc in range(8):
                            k.tr(ps, pb[:, kc * 128:(kc + 1) * 128], hb[:, kc * 128:(kc + 1) * 128], identb.ap, R=[hb, identb])
                        hT = hTr.next()
                        k.cp("act", hT, hT.ap, pb, R=[ps])
                        hTs[c] = hT
                        k.store(hT_s[sq["t0"] + c], hT, hT.ap, W=[dt_("hT", sq["t0"] + c)])
                        yield
                        if c >= 1:
                            make_hs(c - 1)
                        if c == n - 1:
                            make_hs(n - 1)

                    run2([f1_gen(c) for c in range(n)])
            if stop_after == "F1":
                return
            with k.scope():
                C = rw_consts(["kkv", "kav", "w00", "a00", "w20", "a20"])
                tm = rw_temps()
                hTl = Ring(k, 2, [128, 1024], BF16, "hTl")
                hsTl = Ring(k, 2, [128, 1024], BF16, "hsTl")
                xfr = Ring(k, 2, [128, 1536], F32, "xfr")
                lTr = Ring(k, 2, [128, 128], F32, "lTr")
                kvr = Ring(k, 2, [128, 256], F32, "kvr")
                ropr = Ring(k, 2, [128, 256], F32, "ropr")
                kbr = Ring(k, 2, [128, 128], BF16, "kbr")
                t1r = Ring(k, 2, [128, 128], F32, "t1r")
                yfr = Ring(k, 2, [128, 512], F32, "yfr")
                S = k.sb([64, 512], F32, "S")
                S0b = k.sb([64, 512], BF16, "S0b")
                sto = k.sb([64, 512], F32, "sto")
                def proj_gen(sq, c, outd):
                    gt = sq["t0"] + c
                    hT = hTl.next()
                    k.load(hT, hT.ap, hT_s[gt], R=[dt_("hT", gt)])
                    hs = hsTl.next()
                    k.load(hs, hs.ap, hsT_s[gt], R=[dt_("hsT", gt)])
                    xf = xfr.next()
                    for nb in range(3):
                        ps = k.ps()
                        for kc in range(8):
                            k.mm(ps, ps.ap, hT[:, kc * 128:(kc + 1) * 128], Wr1[:, kc * 1664 + nb * 512:kc * 1664 + (nb + 1) * 512],
                                 start=(kc == 0), stop=False, R=[hT, Wr1])
                            k.mm(ps, ps.ap, hs[:, kc * 128:(kc + 1) * 128], Wr2[:, kc * 1664 + nb * 512:kc * 1664 + (nb + 1) * 512],
                                 start=False, stop=(kc == 7), R=[hs, Wr2])
                        k.cp("act", xf, xf[:, nb * 512:(nb + 1) * 512], ps.ap, R=[ps])
                        yield
                    ps = k.ps()
                    for kc in range(8):
                        k.mm(ps, ps[:, 0:128], Wr1[:, kc * 1664 + 1536:kc * 1664 + 1664], hT[:, kc * 128:(kc + 1) * 128],
                             start=(kc == 0), stop=False, R=[hT, Wr1])
                        k.mm(ps, ps[:, 0:128], Wr2[:, kc * 1664 + 1536:kc * 1664 + 1664], hs[:, kc * 128:(kc + 1) * 128],
                             start=False, stop=(kc == 7), R=[hs, Wr2])
                    lT = lTr.next()
                    k.cp("act", lT, lT.ap, ps[:, 0:128], R=[ps])
                    k.store(xf_s[gt], xf, xf.ap, W=[dt_("xf", gt)])
                    k.store(lora_s[gt], lT, lT.ap, W=[dt_("lora", gt)])
                    outd[c] = (xf, lT)
                    yield
                    ps = k.ps()
                    for kc in range(8):
                        k.mm(ps, ps[:, 0:256], hT[:, kc * 128:(kc + 1) * 128], Wkv[:, kc * 256:(kc + 1) * 256],
                             start=(kc == 0), stop=(kc == 7), R=[hT, Wkv])
                    blk = (sq["kc0"] // 128) + c
                    k.cp("dve", Vall, Vv[:, blk, :, 0:64], v3(ps[:, 128:256], 2), R=[ps])
                    if not sq["sample"]:
                        kv = kvr.next()
                        k.cp("act", kv, kv.ap, ps[:, 0:256], R=[ps])
                        k.store(O["nak"][sq["s"], c * 128:(c + 1) * 128, :], kv, kv[:, 0:128])
                        k.store(O["nav"][sq["s"], c * 128:(c + 1) * 128, :], kv, kv[:, 128:256])
                    yield
                    if sq["sample"]:
                        rp = ropr.next()
                        k.load(rp, v3(rp.ap, 2), I["rope"][:, :, c * 128:(c + 1) * 128])
                    for g in range(2):
                        ps = k.ps()
                        for kc in range(8):
                            k.mm(ps, ps[:, 0:128], WK2[:, kc * 256 + g * 128:kc * 256 + (g + 1) * 128], hT[:, kc * 128:(kc + 1) * 128],
                                 start=(kc == 0), stop=(kc == 7), R=[hT, WK2])
                        dst = K2T[:, g, sq["kc0"] + c * 128:sq["kc0"] + (c + 1) * 128]
                        if sq["sample"]:
                            rope_apply(ps, ps[:, 0:128], K2T, dst, rp, kbr, t1r, 1)
                        else:
                            k.cp("act", K2T, dst, ps[:, 0:128], R=[ps])
                        yield

                for sq in seqs:
                    n = sq["n"]
                    if sq["sample"]:
                        load_state(S, S0b, 0, sto)
                    else:
                        k.ms("dve", S, S.ap, 0.0)
                        k.ms("pool", S0b, S0b.ap, 0.0)
                    outd = {}
                    gens = [proj_gen(sq, c, outd) for c in range(n)]
                    exhaust(gens[0])
                    for c in range(n):
                        gt = sq["t0"] + c
                        xf, lT = outd[c]
                        bg = gens[c + 1] if c + 1 < n else None
                        for _ in rw_feats(C, tm, xf, lT, 0, [0]):
                            pump(bg, 1)
                        exhaust(bg)
                        psY, yidx = rw_chunk(tm, 0, S, S0b)
                        yf = yfr.next()
                        k.cp("act", yf, yf.ap, psY.ap, R=[psY])
                        k.unpin(yidx)
                        k.store(yf_s[gt], yf, yf.ap, W=[dt_("yf", gt)])
                    if not sq["sample"]:
                        store_state(S, sto, O["nst"][sq["s"], 0])

    def rope_apply(ps_t, ps_ap, dst_t, dst_ap, rp, kbr, t1r, npair):
        w = npair * 128
        kb = kbr.next()
        k.cp("act", kb, kb[:, 0:w], ps_ap, R=[ps_t])
        pr = k.ps()
        k.mm(pr, pr[:, 0:w], pmtb.ap, kb[:, 0:w], R=[pmtb, kb])
        t1 = t1r.next()
        cosb = rp[:, 0:128]
        sinb = rp[:, 128:256]
        if npair > 1:
            cosb = cosb.unsqueeze(1).to_broadcast([128, npair, 128])
            sinb = sinb.unsqueeze(1).to_broadcast([128, npair, 128])
            k.tt("dve", t1, v3(t1[:, 0:w], npair), v3(ps_ap, npair), cosb, ALU.mult, R=[ps_t, rp])
            k.tt("dve", kb, v3(kb[:, 0:w], npair), v3(pr[:, 0:w], npair), sinb, ALU.mult, R=[pr, rp])
        else:
            k.tt("dve", t1, t1[:, 0:w], ps_ap, cosb, ALU.mult, R=[ps_t, rp])
            k.tt("dve", kb, kb[:, 0:w], pr[:, 0:w], sinb, ALU.mult, R=[pr, rp])
        k.tt("pool", dst_t, dst_ap, t1[:, 0:w], kb[:, 0:w], ALU.add, R=[t1, kb])

    def load_state(S, S0b, z, raw):
        k.load(raw, v3(raw.ap, 8), I["st0"][z].rearrange("h v k -> v h k"))
        ps = k.ps()
        for h in range(8):
            k.mm(ps, ps[0:64, h * 64:(h + 1) * 64], raw[:, h * 64:(h + 1) * 64], cst[0:64, C_ID:C_ID + 64], R=[raw, cst])
        k.cp("act", S, S.ap, ps[0:64, :], R=[ps])
        k.cp("dve", S0b, S0b.ap, ps[0:64, :], R=[ps])

    def store_state(S, sto, out_ap):
        ps = k.ps()
        for h in range(8):
            k.mm(ps, ps[0:64, h * 64:(h + 1) * 64], S[:, h * 64:(h + 1) * 64], cst[0:64, C_ID:C_ID + 64], R=[S, cst])
        k.cp("act", sto, sto.ap, ps[0:64, :], R=[ps])
        k.store(out_ap.rearrange("h v k -> v h k"), sto, v3(sto.ap, 8))

    def phaseB(K2T, Vall, Vv):
        with k.scope():
            C = rw_consts(["kkv", "kav", "rkv", "lnw", "lnb", "w01", "a00", "a01", "w21", "a20", "a21"])
            tms = rw_temps(dbl=True)
            xfr = Ring(k, 2, [128, 1536], F32, "xfrB")
            lTr = Ring(k, 2, [128, 128], F32, "lTrB")
            yfr = Ring(k, 2, [128, 512], F32, "yfrB")
            ycr = Ring(k, 2, [128, 512], F32, "ycr")
            ysq = k.sb([128, 512], F32, "ysq")
            s8 = k.sb([128, 40], F32, "s8B")
            S = k.sb([64, 512], F32, "SB")
            S0b = k.sb([64, 512], BF16, "S0bB")
            sto = k.sb([64, 512], F32, "stoB")
            work = [(sq, c) for sq in seqs for c in range(sq["n"] - 1, -1, -1)]
            yfs = {}

            def feats_gen(i):
                sq, c = work[i]
                gt = sq["t0"] + c
                xf = xfr.next()
                k.load(xf, xf.ap, xf_s[gt], R=[dt_("xf", gt)])
                lT = lTr.next()
                k.load(lT, lT.ap, lora_s[gt], R=[dt_("lora", gt)])
                yf = yfr.next()
                k.load(yf, yf.ap, yf_s[gt], R=[dt_("yf", gt)])
                yfs[i] = yf
                yield from rw_feats(C, tms[i % 2], xf, lT, 1, [0, 1])

            ysr = Ring(k, 2, [128, 512], F32, "ysr")

            def fin_gen(i, ys_, tm):
                sq, c = work[i]
                gt = sq["t0"] + c
                yc = ycr.next()
                k.red("dve", s8, s8[:, 0:8], v3(ys_.ap, 8), R=[ys_])
                k.ts("dve", s8, s8[:, 8:16], s8[:, 0:8], 1.0 / 64, None, ALU.mult, R=[s8])
                k.tt("dve", yc, v3(yc.ap, 8), v3(ys_.ap, 8), s8[:, 8:16].unsqueeze(2).to_broadcast([128, 8, 64]), ALU.subtract,
                     R=[ys_, s8])
                yield
                k.tt("pool", ysq, ysq.ap, yc.ap, yc.ap, ALU.mult, R=[yc])
                k.red("dve", s8, s8[:, 16:24], v3(ysq.ap, 8), R=[ysq])
                k.ts("dve", s8, s8[:, 16:24], s8[:, 16:24], 1.0 / 64, 64e-5, ALU.mult, ALU.add, R=[s8])
                k.tt("gp", s8, s8[:, 32:40], s8[:, 16:24], mhalf[:, 0:8], ALU.pow, R=[s8, mhalf])
                yield
                k.tt("dve", yc, v3(yc.ap, 8), v3(yc.ap, 8), s8[:, 32:40].unsqueeze(2).to_broadcast([128, 8, 64]), ALU.mult, R=[yc, s8])
                k.tt("pool", yc, yc.ap, yc.ap, C["lnw"].ap, ALU.mult, R=[yc, C["lnw"]])
                yield
                k.tt("pool", yc, yc.ap, yc.ap, C["lnb"].ap, ALU.add, R=[yc, C["lnb"]])
                k.tt("pool", yc, yc.ap, yc.ap, tm["bon"].ap, ALU.add, R=[yc, tm["bon"]])
                k.store(yb_s[gt], yc, yc.ap, W=[dt_("yb", gt)])
                yield

            def chain(*gs):
                for g in gs:
                    if g is not None:
                        yield from g

            gens = [feats_gen(i) for i in range(len(work))]
            exhaust(gens[0])
            fin_prev = None
            for i, (sq, c) in enumerate(work):
                n = sq["n"]
                gt = sq["t0"] + c
                tm = tms[i % 2]
                if c == n - 1:
                    if sq["sample"]:
                        load_state(S, S0b, 1, sto)
                    else:
                        k.ms("dve", S, S.ap, 0.0)
                        k.ms("pool", S0b, S0b.ap, 0.0)
                nxt = gens[i + 1] if i + 1 < len(work) else None
                bg = chain(fin_prev, nxt)
                psY, yidx = rw_chunk(tm, 1, S, S0b, bg)
                yf = yfs.pop(i)
                ys_ = ysr.next()
                k.tt("dve", ys_, ys_.ap, psY.ap, yf.ap, ALU.add, R=[psY, yf])
                k.unpin(yidx)
                exhaust(bg)
                fin_prev = fin_gen(i, ys_, tm)
                if c == 0 and not sq["sample"]:
                    store_state(S, sto, O["nst"][sq["s"], 1])
            exhaust(fin_prev)
        with k.scope():
            Wq = k.sb([128, 8 * 512], BF16, "Wq")
            Wg = k.sb([128, 8 * 1024], BF16, "Wg")
            WO = k.sb([128, 8 * 1024], BF16, "WO")
            with k.scope():
                stg = Ring(k, 2, [128, 1024], F32, "wstgB")
                load_cols2([(Wq, lambda kc: Wq[:, kc * 512:(kc + 1) * 512], None, (0, 512))], I["w_in0"], 0, 512, stg)
                load_cols2([(Wg, lambda kc: Wg[:, kc * 1024:kc * 1024 + 512], None, (0, 512))], I["w_in0"], 768, 512, stg)
                load_cols2([(Wg, lambda kc: Wg[:, kc * 1024 + 512:(kc + 1) * 1024], None, (0, 512))], I["w_in0"], 2944, 512, stg)
                load_cols2([(WO, lambda kc: WO[:, kc * 1024:(kc + 1) * 1024], None, (0, 1024))], I["w_out"][0], 0, 1024, stg)
            esink = k.sb([128, 8], F32, "esink")
            load_bc(esink, I["a_sink"][0:1, :])
            k.act(esink, esink.ap, esink.ap, AF.Exp, R=[esink])
            mLi = k.sb([128, 128], BF16, "mLi")
            mUi = k.sb([128, 128], BF16, "mUi")
            k.cp("dve", mLi, mLi.ap, cst[:, C_LI:C_LI + 128], R=[cst])
            k.cp("dve", mUi, mUi.ap, cst[:, C_UI:C_UI + 128], R=[cst])
            Gms = [k.sb([128, D], F32, f"GmB{g}") for g in range(2)]
            for g in range(2):
                k.load(Gms[g], Gms[g].ap, mod_s[0, g, 2], R=[mod_t])
            hTl = Ring(k, 2, [128, 1024], BF16, "hTlB")
            xr = Ring(k, 2, [128, D], F32, "xrB")
            ybl = Ring(k, 2, [128, 512], F32, "ybl")
            ropr = Ring(k, 2, [128, 256], F32, "roprB")
            kbr = Ring(k, 2, [128, 512], BF16, "kbrB")
            t1r = Ring(k, 2, [128, 512], F32, "t1rB")
            qTr = Ring(k, 2, [128, 512], BF16, "qTr")
            ptr = Ring(k, 3, [128, 640], BF16, "ptr")
            yar = Ring(k, 2, [128, 512], F32, "yar")
            dn = k.sb([128, 16], F32, "dn")
            sgar = Ring(k, 2, [128, 1024], F32, "sgar")
            mbr = Ring(k, 2, [128, 1024], BF16, "mb")
            mTr = Ring(k, 2, [128, 1024], BF16, "mT")
            junk = k.sb([128, 512], BF16, "junkB")
            ssr = Ring(k, 2, [128, 8], F32, "ssrB")
            workC = [(sq, c) for sq in seqs for c in range(sq["n"])]
            resC = {}

            def tileC_gen(i):
                sq, c = workC[i]
                n = sq["n"]
                gt = sq["t0"] + c
                if True:
                    hT = hTl.next()
                    k.load(hT, hT.ap, hT_s[gt], R=[dt_("hT", gt)])
                    xt = xr.next()
                    k.load(xt, xt.ap, sq["x"][c * 128:(c + 1) * 128, :])
                    yc = ybl.next()
                    k.load(yc, yc.ap, yb_s[gt], R=[dt_("yb", gt)])
                    ya = yar.next()
                    sga = sgar.next()
                    resC[i] = (xt, yc, ya, sga)
                    def attn_gen():
                        psq = k.ps()
                        for j in range(4):
                            for kc in range(8):
                                k.mm(psq, psq[:, j * 128:(j + 1) * 128], Wq[:, kc * 512 + j * 128:kc * 512 + (j + 1) * 128],
                                     hT[:, kc * 128:(kc + 1) * 128], start=(kc == 0), stop=(kc == 7), R=[Wq, hT])
                        qT = qTr.next()
                        if sq["sample"]:
                            rp = ropr.next()
                            k.load(rp, v3(rp.ap, 2), I["rope"][:, :, c * 128:(c + 1) * 128])
                            rope_apply(psq, psq.ap, qT, qT.ap, rp, kbr, t1r, 4)
                            blocks = [(0, None), (128, None)]
                            if c > 0:
                                blocks.append((sq["kc0"] + (c - 1) * 128, mLi))
                            blocks.append((sq["kc0"] + c * 128, None))
                            if c < n - 1:
                                blocks.append((sq["kc0"] + (c + 1) * 128, mUi))
                        else:
                            k.cp("act", qT, qT.ap, psq.ap, R=[psq])
                            blocks = [(sq["kc0"] + cc * 128, None) for cc in range(n)]
                        nkb = len(blocks)
                        yield
                        (psO,), oidx = k.pin(1)
                        pssb, PTb = {}, {}

                        def atA(h):
                            j, p, g = h // 2, h % 2, h // 4
                            P_ = slice(64 * p, 64 * p + 64)
                            pss = [k.ps() for _ in range((nkb + 3) // 4)]
                            for i, (col, msk) in enumerate(blocks):
                                b_ = pss[i // 4]
                                k.mm(b_, b_[:, (i % 4) * 128:(i % 4 + 1) * 128], K2T[P_, g, col:col + 128], qT[P_, j * 128:(j + 1) * 128],
                                     R=[K2T, qT])
                            pssb[h] = pss

                        def atB(h):
                            pss = pssb.pop(h)
                            PT_ = ptr.next()
                            for bi, b_ in enumerate(pss):
                                w = min(4, nkb - bi * 4) * 128
                                k.act(PT_, PT_[:, bi * 512:bi * 512 + w], b_[:, 0:w], AF.Exp, R=[b_], scale=0.125)
                            for i, (col, msk) in enumerate(blocks):
                                if msk is not None:
                                    k.tt("pool", PT_, PT_[:, i * 128:(i + 1) * 128], PT_[:, i * 128:(i + 1) * 128], msk.ap, ALU.mult, R=[PT_, msk])
                            PTb[h] = PT_

                        def atC(h):
                            hg, hh, g = h // 4, h % 4, h // 4
                            PT_ = PTb.pop(h)
                            for i, (col, msk) in enumerate(blocks):
                                k.mm(psO, psO[:, hh * 65:(hh + 1) * 65], PT_[:, i * 128:(i + 1) * 128], Vv[:, col // 128, g, :],
                                     start=(i == 0), stop=(i == nkb - 1), R=[PT_, Vall])
                            if hh != 3:
                                return
                            pv = psO[:, 0:260].rearrange("p (a b) -> p a b", a=4)
                            k.tt("dve", dn, dn[:, 0:4], pv[:, :, 64], esink[:, hg * 4:(hg + 1) * 4], ALU.add, R=[psO, esink])
                            k.rcp(dn, dn[:, 4:8], dn[:, 0:4], R=[dn])
                            k.tt("dve", ya, v3(ya.ap, 8)[:, hg * 4:(hg + 1) * 4, :], pv[:, :, 0:64],
                                 dn[:, 4:8].unsqueeze(2).to_broadcast([128, 4, 64]), ALU.mult, R=[psO, dn])

                        for h in range(-1, 8):
                            if h + 1 < 8:
                                atA(h + 1)
                            if h >= 0:
                                atB(h)
                                atC(h)
                            yield
                        k.unpin(oidx)
                        for gi in range(2):
                            psg = k.ps()
                            for kc in range(8):
                                k.mm(psg, psg.ap, hT[:, kc * 128:(kc + 1) * 128], Wg[:, kc * 1024 + gi * 512:kc * 1024 + (gi + 1) * 512],
                                     start=(kc == 0), stop=(kc == 7), R=[hT, Wg])
                            k.act(sga, sga[:, gi * 512:(gi + 1) * 512], psg.ap, AF.Silu, R=[psg])
                            yield
                    yield from attn_gen()

            def fullC_gen(i):
                sq, c = workC[i]
                gt = sq["t0"] + c
                yield from tileC_gen(i)
                xt, yc, ya, sga = resC.pop(i)
                mb, mT = mbr.next(), mTr.next()
                k.tt("dve", mb, mb[:, 0:512], ya.ap, sga[:, 0:512], ALU.mult, R=[ya, sga])
                k.tt("pool", mb, mb[:, 512:1024], yc.ap, sga[:, 512:1024], ALU.mult, R=[yc, sga])
                yield
                yield from out_proj_gen(mb, mT, WO, Gms[sq["g"]], xt, sga, junk, ssr, x1_s[gt * 128:(gt + 1) * 128, :], dt_("x1", gt))

            run2([fullC_gen(i) for i in range(len(workC))])

    def out_proj_residual(*a):
        exhaust(out_proj_gen(*a))

    def out_proj_gen(mb, mT, WO, Gm, xt, tmpo, junk, ssr, out_ap, out_dt):
        ps = k.ps()
        pb = bank_bf(ps)
        for kc in range(8):
            k.tr(ps, pb[:, kc * 128:(kc + 1) * 128], mb[:, kc * 128:(kc + 1) * 128], identb.ap, R=[mb, identb])
        k.cp("act", mT, mT.ap, pb, R=[ps])
        yield
        py = [k.ps(), k.ps()]
        for nb in range(2):
            for kc in range(8):
                k.mm(py[nb], py[nb].ap, mT[:, kc * 128:(kc + 1) * 128], WO[:, kc * 1024 + nb * 512:kc * 1024 + (nb + 1) * 512],
                     start=(kc == 0), stop=(kc == 7), R=[mT, WO])
        ss = ssr.next()
        rstd = rstd_of(ss, junk, [(py[0], py[0].ap), (py[1], py[1].ap)], D, 1e-6)
        for nb in range(2):
            cs = slice(nb * 512, (nb + 1) * 512)
            k.stt("dve", tmpo, tmpo[:, cs], py[nb].ap, rstd, Gm[:, cs], ALU.mult, ALU.mult, R=[py[nb], ss, Gm])
        yield
        k.tt("pool", tmpo, tmpo.ap, tmpo.ap, xt.ap, ALU.add, R=[tmpo, xt])
        k.store(out_ap, tmpo, tmpo.ap, W=[out_dt] if out_dt is not None else [])

    NTOK = NT * 128
    qT_s = dscr("qT_s", (8, 128, NTOK), BF16)
    kT_s = dscr("kT_s", (8, 128, NTOK), BF16)
    v_s = dscr("v_s", (NT, 128, 1024), BF16)
    sg_s = dscr("sg_s", (NT, 128, 1024), BF16)

    def layer1():
        with k.scope():
            lamt = k.sb([128, 4], F32, "lamt")
            with k.scope():
                lq = [k.sb([128, 64], F32, f"lq{i}") for i in range(4)]
                for t, nm in zip(lq, ("c_lq1", "c_lk1", "c_lq2", "c_lk2")):
                    load_bc(t, I[nm][0:1, :])
                k.tt("dve", lq[0], lq[0].ap, lq[0].ap, lq[1].ap, ALU.mult, R=[lq[0], lq[1]])
                k.tt("dve", lq[2], lq[2].ap, lq[2].ap, lq[3].ap, ALU.mult, R=[lq[2], lq[3]])
                k.red("dve", lamt, lamt[:, 0:1], lq[0].ap, R=[lq[0]])
                k.red("dve", lamt, lamt[:, 1:2], lq[2].ap, R=[lq[2]])
                k.act(lamt, lamt[:, 0:2], lamt[:, 0:2], AF.Exp, R=[lamt])
                k.tt("dve", lamt, lamt[:, 2:3], lamt[:, 0:1], lamt[:, 1:2], ALU.subtract, R=[lamt])
                k.ts("dve", lamt, lamt[:, 2:3], lamt[:, 2:3], LAM_INIT, None, ALU.add, R=[lamt])
                k.ts("dve", lamt, lamt[:, 3:4], lamt[:, 2:3], -1.0, None, ALU.mult, R=[lamt])
            subl = k.sb([128, 128], F32, "subl")
            load_bc(subl, I["c_subln"][0:1, :])
            k.ts("dve", subl, subl.ap, subl.ap, 1.0 - LAM_INIT, None, ALU.mult, R=[subl])
            l1_phase1()
            if stop_after == "L1P1":
                return
            l1_phase23(lamt, subl)

    def l1_phase1():
        with k.scope():
            W1 = k.sb([128, 8 * 4096], BF16, "W1L1")
            with k.scope():
                stg = Ring(k, 2, [128, 1024], F32, "wstg1")
                for q4 in range(4):
                    load_cols2([(W1, (lambda kc, q4=q4: W1[:, kc * 4096 + q4 * 1024:kc * 4096 + (q4 + 1) * 1024]), None, (0, 1024))],
                               I["w_in1"], q4 * 1024, 1024, stg)
                modulation_to_dram(1)
            Sh = k.sb([128, D], F32, "Sh1")
            A = k.sb([128, D], F32, "A1")
            xr = Ring(k, 3, [128, D], F32, "xr1")
            tmpf = k.sb([128, D], F32, "tmpf1")
            hbr = Ring(k, 3, [128, D], BF16, "hb1")
            junk = k.sb([128, D], BF16, "junk1")
            ssr = Ring(k, 3, [128, 8], F32, "ssr1")
            hTr = Ring(k, 3, [128, 1024], BF16, "hTr1")
            ropr = Ring(k, 3, [128, 256], F32, "ropr1")
            kbr = Ring(k, 2, [128, 512], BF16, "kbr1")
            t1r = Ring(k, 2, [128, 512], F32, "t1r1")
            qkr = Ring(k, 4, [128, 512], BF16, "qkr1")
            vbr = Ring(k, 3, [128, 1024], BF16, "vbr1")
            vfr = Ring(k, 2, [128, 1024], F32, "vfr1")
            sgr = Ring(k, 3, [128, 1024], BF16, "sgr1")
            sgf = k.sb([128, 512], F32, "sgf1")
            myidx = k.sb([128, 4], mybir.dt.int32, "myidx")
            k.load(myidx, myidx.ap, I["myrows"])
            x1_all = [dt_("x1", NPS * PT + c) for c in range(ST)]

            def p1_tile(sq, c, gt_x, my_i, do_q, do_kv, do_g, rope_src, gt_q):
                xt = xr.next()
                if gt_x is not None:
                    k.load(xt, xt.ap, x1_s[gt_x * 128:(gt_x + 1) * 128, :], R=[dt_("x1", gt_x)])
                else:
                    k.gather(xt, xt.ap, x1_s, myidx, myidx[:, my_i:my_i + 1], R=x1_all)
                ss_t = ssr.next()
                rstd = rstd_of(ss_t, junk, [(xt, xt.ap)], D, 1e-6)
                k.stt("dve", tmpf, tmpf.ap, xt.ap, rstd, A.ap, ALU.mult, ALU.mult, R=[xt, ss_t, A])
                hb = hbr.next()
                k.tt("pool", hb, hb.ap, tmpf.ap, Sh.ap, ALU.add, R=[tmpf, Sh])
                ps = k.ps()
                pb = bank_bf(ps)
                for kc in range(8):
                    k.tr(ps, pb[:, kc * 128:(kc + 1) * 128], hb[:, kc * 128:(kc + 1) * 128], identb.ap, R=[hb, identb])
                hT = hTr.next()
                k.cp("act", hT, hT.ap, pb, R=[ps])
                yield
                rp = None
                if rope_src is not None:
                    rp = ropr.next()
                    k.load(rp, v3(rp.ap, 2), rope_src)
                for do_, coff, dst_s, nm, gdst in ((do_q, 0, qT_s, "qT", gt_q), (do_kv, 1024, kT_s, "kT", gt_x)):
                    if not do_:
                        continue
                    for hg in range(2):
                        ps = k.ps()
                        for hh in range(4):
                            h = hg * 4 + hh
                            for kc in range(8):
                                k.mm(ps, ps[:, hh * 128:(hh + 1) * 128], W1[:, kc * 4096 + coff + h * 128:kc * 4096 + coff + (h + 1) * 128],
                                     hT[:, kc * 128:(kc + 1) * 128], start=(kc == 0), stop=(kc == 7), R=[W1, hT])
                        qk = qkr.next()
                        if rp is not None:
                            rope_apply(ps, ps.ap, qk, qk.ap, rp, kbr, t1r, 4)
                        else:
                            k.cp("act", qk, qk.ap, ps.ap, R=[ps])
                        k.store(dst_s[hg * 4:(hg + 1) * 4, :, gdst * 128:(gdst + 1) * 128].rearrange("h p t -> p h t"), qk, v3(qk.ap, 4),
                                W=[dt_(nm, (gdst, hg))])
                        yield
                if do_kv:
                    vb = vbr.next()
                    vf = vfr.next()
                    for nb in range(2):
                        ps = k.ps()
                        for kc in range(8):
                            k.mm(ps, ps.ap, hT[:, kc * 128:(kc + 1) * 128], W1[:, kc * 4096 + 2048 + nb * 512:kc * 4096 + 2048 + (nb + 1) * 512],
                                 start=(kc == 0), stop=(kc == 7), R=[hT, W1])
                        k.cp("dve", vb, vb[:, nb * 512:(nb + 1) * 512], ps.ap, R=[ps])
                        if not sq["sample"]:
                            k.cp("act", vf, vf[:, nb * 512:(nb + 1) * 512], ps.ap, R=[ps])
                        yield
                    k.store(v_s[gt_x], vb, vb.ap, W=[dt_("v", gt_x)])
                    if not sq["sample"]:
                        k.store(O["ncv"][sq["s"], c * 128:(c + 1) * 128, :], vf, vf.ap)
                        kf = vfr.next()
                        for nb in range(2):
                            ps = k.ps()
                            for kc in range(8):
                                k.mm(ps, ps.ap, hT[:, kc * 128:(kc + 1) * 128], W1[:, kc * 4096 + 1024 + nb * 512:kc * 4096 + 1024 + (nb + 1) * 512],
                                     start=(kc == 0), stop=(kc == 7), R=[hT, W1])
                            k.cp("act", kf, kf[:, nb * 512:(nb + 1) * 512], ps.ap, R=[ps])
                            yield
                        k.store(O["nck"][sq["s"], c * 128:(c + 1) * 128, :], kf, kf.ap)
                if do_g:
                    sg = sgr.next()
                    for nb in range(2):
                        ps = k.ps()
                        for kc in range(8):
                            k.mm(ps, ps.ap, hT[:, kc * 128:(kc + 1) * 128], W1[:, kc * 4096 + 3072 + nb * 512:kc * 4096 + 3072 + (nb + 1) * 512],
                                 start=(kc == 0), stop=(kc == 7), R=[hT, W1])
                        k.act(sg, sg[:, nb * 512:(nb + 1) * 512], ps.ap, AF.Silu, R=[ps])
                        yield
                    k.store(sg_s[gt_q], sg, sg.ap, W=[dt_("sg", gt_q)])

            curg = None
            for sq in seqs:
                if sq["g"] != curg:
                    curg = sq["g"]
                    k.load(Sh, Sh.ap, mod_s[1, curg, 0], R=[mod_t])
                    k.load(A, A.ap, mod_s[1, curg, 1], R=[mod_t])
                tiles = []
                for c in range(sq["n"]):
                    gt = sq["t0"] + c
                    if sq["sample"]:
                        tiles.append(p1_tile(sq, c, gt, None, False, True, False, I["rope"][:, :, c * 128:(c + 1) * 128], None))
                    else:
                        tiles.append(p1_tile(sq, c, gt, None, True, True, True, None, gt))
                if sq["sample"]:
                    for i in range(MYT):
                        tiles.append(p1_tile(sq, i, None, i, True, False, True, I["rope_my"][:, :, i * 128:(i + 1) * 128], sq["t0"] + i))
                run2(tiles, 3)

    def l1_phase23(lamt, subl):
        with k.scope():
            WO = k.sb([128, 8 * 1024], BF16, "WO1")
            with k.scope():
                stg = Ring(k, 2, [128, 1024], F32, "wstg2")
                load_cols2([(WO, lambda kc: WO[:, kc * 1024:(kc + 1) * 1024], None, (0, 1024))], I["w_out"][1], 0, 1024, stg)
            Gm = k.sb([128, D], F32, "Gm1")
            osb = k.sb([128, MYT * 1024], BF16, "osb")
            KTr = Ring(k, 2, [128, 2304], BF16, "KTr")
            Vr = Ring(k, 2, [128, 18 * 129], BF16, "Vr")
            for t in Vr.t:
                k.ms("pool", t, t.ap, 1.0)
            cfr = Ring(k, 2, [128, 128], F32, "cfr")
            cbr = Ring(k, 2, [128, 128], BF16, "cbr")
            qr = Ring(k, 2, [128, 256], BF16, "qr")
            ptr = Ring(k, 4, [128, 512], BF16, "ptr1")
            Osb = [k.sb([128, 2 * 129], F32, f"Osb{z}") for z in range(2)]
            dn = k.sb([128, 8], F32, "dn1")
            ofa = k.sb([128, MYT * 1024], F32, "ofa")
            ofv = ofa.ap.rearrange("p (t h e) -> p t h e", t=MYT, h=8)
            ofs = k.sb([128, MYT * 1024], F32, "ofs")
            of2 = k.sb([128, 256], F32, "of2")
            of2v = v3(of2.ap, 2)
            sgrp = k.sb([128, 96], F32, "sgrp")
            mhalf32 = k.sb([128, 32], F32, "mhalf32")
            k.ms("dve", mhalf32, mhalf32.ap, -0.5)
            junk = k.sb([128, 512], BF16, "junk2")
            ssr = Ring(k, 2, [128, 8], F32, "ssr2")
            sgl = Ring(k, 2, [128, 1024], BF16, "sgl")
            xr = Ring(k, 2, [128, D], F32, "xr2")
            mbr = Ring(k, 2, [128, 1024], BF16, "mb1")
            mTr = Ring(k, 2, [128, 1024], BF16, "mT1")
            tmpor = Ring(k, 2, [128, 1024], F32, "tmpo1")
            (psO0, psO1), oidx = k.pin(2)
            psO = (psO0, psO1)
            myidx = k.sb([128, 4], mybir.dt.int32, "myidx2")
            k.load(myidx, myidx.ap, I["myrows"])
            x1_all = [dt_("x1", NPS * PT + c) for c in range(ST)]
            curg = None
            for sq in seqs:
                n = sq["n"]
                tok0 = sq["t0"] * 128
                nblk = n + (2 if sq["sample"] else 0)
                nq = MYT if sq["sample"] else n
                items = []
                for h in range(8):
                    hd = dict(h=h, KT=None, V=None, Vv=None, qT={})
                    for qg in range(nq // 2):
                        for z in range(2):
                            for kp in range(nblk // 2):
                                items.append((hd, qg, z, kp))
                Sb, PTb = {}, {}

                def stA(i):
                    hd, qg, z, kp = items[i]
                    h = hd["h"]
                    if hd["KT"] is None:
                        KT = KTr.next()
                        V = Vr.next()
                        Vv = V.ap.rearrange("p (b e) -> p b e", b=18)
                        hd["KT"], hd["V"], hd["Vv"] = KT, V, Vv
                        if sq["sample"]:
                            for blk in range(2):
                                cf = cfr.next()
                                k.load(cf, cf.ap, I["cck"][blk * 128:(blk + 1) * 128, h * 128:(h + 1) * 128])
                                cb = cbr.next()
                                k.cp("dve", cb, cb.ap, cf.ap, R=[cf])
                                ps = k.ps()
                                pb = bank_bf(ps)
                                k.tr(ps, pb[:, 0:128], cb.ap, identb.ap, R=[cb, identb])
                                k.cp("act", KT, KT[:, blk * 128:(blk + 1) * 128], pb[:, 0:128], R=[ps])
                                cf = cfr.next()
                                k.load(cf, cf.ap, I["ccv"][blk * 128:(blk + 1) * 128, h * 128:(h + 1) * 128])
                                k.cp("dve", V, Vv[:, blk, 0:128], cf.ap, R=[cf])
                            koff = 256
                        else:
                            koff = 0
                        k.load(KT, KT[:, koff:koff + n * 128], kT_s[h, :, tok0:tok0 + n * 128],
                               R=[dt_("kT", (sq["t0"] + c, h // 4)) for c in range(n)])
                        for c4 in range(0, n, 4):
                            m4 = min(4, n - c4)
                            k.load(V, Vv[:, koff // 128 + c4:koff // 128 + c4 + m4, 0:128],
                                   v_s[sq["t0"] + c4:sq["t0"] + c4 + m4, :, h * 128:(h + 1) * 128].rearrange("c p e -> p c e"),
                                   R=[dt_("v", sq["t0"] + c4 + c) for c in range(m4)])
                    if qg not in hd["qT"]:
                        qT = qr.next()
                        k.load(qT, qT.ap, qT_s[h, :, tok0 + qg * 256:tok0 + (qg + 1) * 256],
                               R=[dt_("qT", (sq["t0"] + qg * 2 + i_, h // 4)) for i_ in range(2)])
                        hd["qT"][qg] = qT
                    qT = hd["qT"][qg]
                    KT = hd["KT"]
                    P_ = slice(64 * z, 64 * z + 64)
                    pS = k.ps()
                    for i_ in range(2):
                        kb = kp * 2 + i_
                        k.mm(pS, pS[:, i_ * 256:(i_ + 1) * 256], KT[P_, kb * 128:(kb + 1) * 128], qT[P_, :], R=[KT, qT])
                    Sb[i] = pS

                def stB(i):
                    pS = Sb.pop(i)
                    PT_ = ptr.next()
                    k.act(PT_, PT_.ap, pS.ap, AF.Exp, R=[pS], scale=0.125)
                    PTb[i] = PT_

                def stC(i):
                    hd, qg, z, kp = items[i]
                    h = hd["h"]
                    PT_ = PTb.pop(i)
                    V, Vv = hd["V"], hd["Vv"]
                    for i_ in range(2):
                        kb = kp * 2 + i_
                        for qs in range(2):
                            k.mm(psO[qs], psO[qs][:, 0:129], PT_[:, i_ * 256 + qs * 128:i_ * 256 + (qs + 1) * 128], Vv[:, kb, :],
                                 start=(kb == 0), stop=(kb == nblk - 1), R=[PT_, V])
                    if kp != nblk // 2 - 1:
                        return
                    for qs in range(2):
                        k.cp("act" if qs == 0 else "dve", Osb[z], Osb[z][:, qs * 129:(qs + 1) * 129], psO[qs][:, 0:129], R=[psO[qs]])
                    if z != 1:
                        return
                    O0 = Osb[0].ap.rearrange("p (q e) -> p q e", q=2)
                    O1 = Osb[1].ap.rearrange("p (q e) -> p q e", q=2)
                    k.rcp(dn, dn[:, 0:2], O0[:, :, 128], R=[Osb[0]])
                    k.rcp(dn, dn[:, 2:4], O1[:, :, 128], R=[Osb[1]])
                    k.ts("dve", dn, dn[:, 4:6], dn[:, 2:4], lamt[:, 3:4], None, ALU.mult, R=[dn, lamt])
                    dst = ofv[:, qg * 2:qg * 2 + 2, h, :]
                    k.tt("dve", ofa, dst, O0[:, :, 0:128], dn[:, 0:2].unsqueeze(2).to_broadcast([128, 2, 128]), ALU.mult, R=[Osb[0], dn])
                    k.tt("dve", of2, of2v, O1[:, :, 0:128], dn[:, 4:6].unsqueeze(2).to_broadcast([128, 2, 128]), ALU.mult, R=[Osb[1], dn])
                    k.tt("dve", ofa, dst, dst, of2v, ALU.add, R=[ofa, of2])

                LA = 2
                for i in range(-LA, len(items)):
                    if i + LA < len(items):
                        stA(i + LA)
                    if i >= 0:
                        stB(i)
                        stC(i)
                ng = nq * 8
                fl = ofa[:, 0:ng * 128]
                k.tt("dve", ofs, ofs[:, 0:ng * 128], fl, fl, ALU.mult, R=[ofa])
                k.red("dve", sgrp, sgrp[:, 0:ng], v3(ofs[:, 0:ng * 128], ng), R=[ofs])
                k.ts("dve", sgrp, sgrp[:, 32:32 + ng], sgrp[:, 0:ng], 1.0 / 128, 1e-5, ALU.mult, ALU.add, R=[sgrp])
                k.tt("gp", sgrp, sgrp[:, 64:64 + ng], sgrp[:, 32:32 + ng], mhalf32[:, 0:ng], ALU.pow, R=[sgrp, mhalf32])
                k.tt("dve", ofs, v3(ofs[:, 0:ng * 128], ng), v3(fl, ng), sgrp[:, 64:64 + ng].unsqueeze(2).to_broadcast([128, ng, 128]),
                     ALU.mult, R=[ofa, sgrp])
                k.tt("dve", osb, v3(osb[:, 0:ng * 128], ng), v3(ofs[:, 0:ng * 128], ng), subl.ap.unsqueeze(1).to_broadcast([128, ng, 128]),
                     ALU.mult, R=[ofs, subl])
                if sq["g"] != curg:
                    curg = sq["g"]
                    k.load(Gm, Gm.ap, mod_s[1, curg, 2], R=[mod_t])
                def p3_gen(c):
                    gt = sq["t0"] + c
                    sg = sgl.next()
                    k.load(sg, sg.ap, sg_s[gt], R=[dt_("sg", gt)])
                    xt = xr.next()
                    if sq["sample"]:
                        k.gather(xt, xt.ap, x1_s, myidx, myidx[:, c:c + 1], R=x1_all)
                    else:
                        k.load(xt, xt.ap, x1_s[gt * 128:(gt + 1) * 128, :], R=[dt_("x1", gt)])
                    mb, mT, tmpo = mbr.next(), mTr.next(), tmpor.next()
                    k.tt("dve", mb, mb.ap, osb[:, c * 1024:(c + 1) * 1024], sg.ap, ALU.mult, R=[osb, sg])
                    yield
                    if sq["sample"]:
                        out_ap = O["ys"][c * 128:(c + 1) * 128, :]
                    else:
                        out_ap = O["yp"][sq["s"] * 256 + c * 128:sq["s"] * 256 + (c + 1) * 128, :]
                    yield from out_proj_gen(mb, mT, WO, Gm, xt, tmpo, junk, ssr, out_ap, None)

                run2([p3_gen(c) for c in range(nq)])
            k.unpin(oidx)

    layer0()
    if stop_after is None:
        layer1()
    k.finish()
    return k


def make_in_maps(inp):
    cst, rope = _consts()
    f = lambda a: np.ascontiguousarray(np.asarray(a, dtype=np.float32))
    maps = []
    for core in range(8):
        b = core % 2
        m = {
            "xp": f(inp["x_prompt"][core * NPS:(core + 1) * NPS]).reshape(NPS * 256, D),
            "xs": f(inp["x_sample"][b]),
            "cak": f(inp["cache_a_k"][b, 0]).reshape(256, 128),
            "cav": f(inp["cache_a_v"][b, 0]).reshape(256, 128),
            "st0": f(inp["state_rwkv"][b, 0]),
            "cck": f(inp["cache_c_k"][b, 0]).reshape(256, 1024),
            "ccv": f(inp["cache_c_v"][b, 0]).reshape(256, 1024),
            "cvec": f(np.stack([np.asarray(inp["c_ctx"]), np.asarray(inp["c"])[b]], 0)),
            "ada_w": f(inp["ada_w"]), "ada_b": f(inp["ada_b"]), "norm_pre": f(inp["norm_pre"]), "norm_post": f(inp["norm_post"]),
            "w_out": f(inp["w_out"]), "w_in0": f(inp["even_w_in"][0]), "a_sink": f(inp["a_sink"]), "b_mu": f(inp["b_mu"]),
            "b_w0": f(inp["b_w0"][0]), "b_w2": f(inp["b_w2"][0]), "b_a0": f(inp["b_a0"][0]), "b_a2": f(inp["b_a2"][0]),
            "b_kk": f(inp["b_kk"]), "b_ka": f(inp["b_ka"]), "b_rk": f(inp["b_rk"]), "b_ln_w": f(inp["b_ln_w"]), "b_ln_b": f(inp["b_ln_b"]),
            "w_in1": f(inp["odd_w_in"][0]), "c_lq1": f(inp["c_lq1"]), "c_lk1": f(inp["c_lk1"]), "c_lq2": f(inp["c_lq2"]),
            "c_lk2": f(inp["c_lk2"]), "c_subln": f(inp["c_subln"]), "cst": cst, "rope": rope,
            "rope_my": np.ascontiguousarray(rope[:, :, (core // 2) * MYT * 128:(core // 2 + 1) * MYT * 128]),
            "myrows": np.ascontiguousarray(
                (NPS * 256 + (core // 2) * MYT * 128 + np.arange(MYT)[None, :] * 128 + np.arange(128)[:, None]).astype(np.int32)),
        }
        maps.append(m)
    return maps


_CACHE = {}


def kernel(**inputs):
    inp = {k_: np.asarray(v) for k_, v in inputs.items()}
    if "k" not in _CACHE:
        k1 = build()
        _CACHE["k"] = build(plan=frozenset(k1.needed))
    kk = _CACHE["k"]
    maps = make_in_maps(inp)
    res = run_bass_kernel_spmd(kk.nc, maps, core_ids=list(range(8)))
    R = res.results
    B = inp["x_prompt"].shape[0]
    y_p = np.concatenate([R[c]["yp"].reshape(NPS, 256, D) for c in range(8)], 0).astype(np.float32)
    y_s = np.zeros((2, 2048, D), np.float32)
    for c in range(8):
        y_s[c % 2, (c // 2) * MYT * 128:(c // 2 + 1) * MYT * 128] = R[c]["ys"]
    nak = np.concatenate([R[c]["nak"] for c in range(8)], 0).reshape(B, 1, 256, 2, 64).astype(np.float32)
    nav = np.concatenate([R[c]["nav"] for c in range(8)], 0).reshape(B, 1, 256, 2, 64).astype(np.float32)
    nst = np.concatenate([R[c]["nst"] for c in range(8)], 0).reshape(B, 1, 2, 8, 64, 64).astype(np.float32)
    nck = np.concatenate([R[c]["nck"] for c in range(8)], 0).reshape(B, 1, 256, 8, 128).astype(np.float32)
    ncv = np.concatenate([R[c]["ncv"] for c in range(8)], 0).reshape(B, 1, 256, 8, 128).astype(np.float32)
    return (y_p, y_s, nak, nav, nst, nck, ncv)
```

```python
import math
from contextlib import ExitStack, contextmanager
import numpy as np
import concourse.bass as bass
import concourse.mybir as mybir
from concourse.bass_utils import run_bass_kernel_spmd

F32, BF16, F16 = mybir.dt.float32, mybir.dt.bfloat16, mybir.dt.float16
ALU, AF, AX = mybir.AluOpType, mybir.ActivationFunctionType, mybir.AxisListType

D = 1024
NPS = 4
PT = 2
ST = 16
MYT = 4
EVEN_IN = 3456
ODD_IN = 4096
CW = -math.exp(-0.5)
LAM_INIT = 0.8 - 0.6 * math.exp(-0.3 * 1)
NDMA = 12
NDMA_Q = {"sp": 12, "pool": 4}


class T:
    __slots__ = ("ap", "w", "r", "x")

    def __init__(self, ap, x=False):
        self.ap = ap
        self.w = None
        self.r = {}
        self.x = x

    def __getitem__(self, k):
        return self.ap[k]


class Kern:
    def __init__(self, plan=None):
        self.plan = plan
        self.needed = set()
        self.rank = {}
        self.val = {}
        nc = self.nc = bass.Bass("TRN2", target_bir_lowering=False)
        self.h = {"pe": nc.tensor, "act": nc.scalar, "dve": nc.vector, "pool": nc.gpsimd, "sp": nc.sync}
        self.sems = {k: nc.alloc_semaphore("s_" + k) for k in self.h}
        self.cnt = {k: 0 for k in self.h}
        self.seen = {k: {} for k in self.h}
        self.dval = {}
        self.dnext = {"sp": 0, "pool": 0}
        for q in ("sp", "pool"):
            for i in range(NDMA):
                n = f"d{q}{i}"
                self.sems[n] = nc.alloc_semaphore(n)
                self.dval[n] = 0
        self.banks = [T(nc.alloc_psum_tensor(f"bank{i}", [128, 512], F32).ap(), x=True) for i in range(8)]
        self.ring = list(range(8))
        self.rpos = 0
        self.nsb = 0
        self.stacks = []
        self.order = {}
        self.nev = 0
        self.snap = {}
        self.pool_compute = False

    def sb(self, shape, dt=F32, name=None):
        self.nsb += 1
        nm = f"{name or 't'}_{self.nsb}"
        if self.stacks:
            h = self.stacks[-1].enter_context(self.nc.sbuf_tensor(nm, list(shape), dt))
        else:
            h = self.nc.alloc_sbuf_tensor(nm, list(shape), dt)
        return T(h.ap())

    @contextmanager
    def scope(self):
        st = ExitStack()
        self.stacks.append(st)
        try:
            yield
        finally:
            self.barrier()
            self.stacks.pop()
            st.close()

    def ps(self):
        n = len(self.ring)
        for _ in range(n):
            b = self.banks[self.ring[self.rpos % n]]
            self.rpos += 1
            if not (b.w is not None and not b.r):
                return b
        raise RuntimeError("no free PSUM bank")

    def pin(self, n):
        free = [i for i in self.ring if not (self.banks[i].w is not None and not self.banks[i].r)]
        assert len(free) >= n, "not enough free PSUM banks to pin"
        out = free[-n:]
        for i in out:
            self.ring.remove(i)
        return [self.banks[i] for i in out], out

    def unpin(self, idx):
        self.ring.extend(idx)

    def _deps(self, R, W, en=None):
        deps = {}

        def add(s, v):
            if deps.get(s, 0) < v:
                deps[s] = v
        for t in R:
            if t.w:
                add(*t.w)
            if t.x:
                for s, v in t.r.items():
                    if s != en:
                        add(s, v)
        for t in W:
            if t.w:
                add(*t.w)
            for s, v in t.r.items():
                add(s, v)
        return deps

    def _wait(self, en, deps):
        seen = self.seen[en]
        need = []
        for s, v in sorted(deps.items(), key=lambda kv: -self.order.get(kv, 0)):
            if en == "pe" and s == "pe":
                continue
            if seen.get(s, 0) < v:
                need.append((s, v))
                seen[s] = v
                for s2, v2 in self.snap.get((s, v), {}).items():
                    if seen.get(s2, 0) < v2:
                        seen[s2] = v2
        for s, v in need:
            self.needed.add((s, v))
        for s, v in need[:-1]:
            self.h[en].wait_ge(self.sems[s], self.sv(s, v))
        return need[-1] if need else None

    def sv(self, s, v):
        if self.plan is None or s not in self.h:
            return v
        return self.val[(s, v)]

    def _mark(self, ev, R, W):
        for t in R:
            if t.r.get(ev[0], 0) < ev[1]:
                t.r[ev[0]] = ev[1]
        for t in W:
            t.w = ev
            t.r = {}

    def op(self, en, fn, R=(), W=()):
        if en == "gp":
            en = "pool"
        elif en == "pool" and not self.pool_compute:
            en = "dve"
        fw = self._wait(en, self._deps(R, W, en))
        inst = fn(self.h[en])
        if fw is not None:
            inst._wait_ge(self.sems[fw[0]], self.sv(*fw))
        self.cnt[en] += 1
        ev = (en, self.cnt[en])
        if self.plan is None or ev in self.plan:
            inst.then_inc(self.sems[en], 1)
            self.rank[en] = self.rank.get(en, 0) + 1
            self.val[ev] = self.rank[en]
        sn_ = dict(self.seen[en])
        if en != "pe":
            sn_[en] = max(sn_.get(en, 0), 0)
        self.snap[ev] = sn_
        self.nev += 1
        self.order[ev] = self.nev
        self._mark(ev, R, W)

    def dma(self, q, out, in_, R=(), W=(), **kw):
        slot = self.dnext[q]
        self.dnext[q] = (slot + 1) % NDMA_Q[q]
        sn = f"d{q}{slot}"
        deps = self._deps(R, W)
        if self.dval[sn] > 0 and deps.get(sn, 0) < self.dval[sn]:
            deps[sn] = self.dval[sn]
        fw = self._wait(q, deps)
        inst = self.h[q].dma_start(out=out, in_=in_, **kw)
        if fw is not None:
            inst._wait_ge(self.sems[fw[0]], self.sv(*fw))
        self.dval[sn] += 16
        inst.then_inc(self.sems[sn], 16)
        ev = (sn, self.dval[sn])
        self.snap[ev] = dict(self.seen[q])
        self.nev += 1
        self.order[ev] = self.nev
        self._mark(ev, R, W)

    def gather(self, out_t, out_ap, in_ap, idx_t, idx_ap, R=()):
        q = "pool"
        slot = self.dnext[q]
        self.dnext[q] = (slot + 1) % NDMA_Q[q]
        sn = f"d{q}{slot}"
        deps = self._deps(list(R) + [idx_t], [out_t], q)
        if self.dval[sn] > 0 and deps.get(sn, 0) < self.dval[sn]:
            deps[sn] = self.dval[sn]
        fw = self._wait(q, deps)
        if fw is not None:
            self.h[q].wait_ge(self.sems[fw[0]], self.sv(*fw))
        inst = self.h[q].indirect_dma_start(out=out_ap, out_offset=None, in_=in_ap,
                                            in_offset=bass.IndirectOffsetOnAxis(ap=idx_ap, axis=0))
        self.dval[sn] += 16
        inst.then_inc(self.sems[sn], 16)
        ev = (sn, self.dval[sn])
        self.snap[ev] = dict(self.seen[q])
        self.nev += 1
        self.order[ev] = self.nev
        self._mark(ev, list(R) + [idx_t], [out_t])

    def load(self, out_t, out_ap, in_ap, R=(), **kw):
        self.dma("sp", out_ap, in_ap, R=R, W=[out_t], **kw)

    def store(self, out_ap, in_t, in_ap, W=(), **kw):
        self.dma("pool", out_ap, in_ap, R=[in_t], W=W, **kw)

    def barrier(self):
        deps = {k: self.cnt[k] for k in ("pe", "act", "dve", "pool") if self.cnt[k] > 0}
        for n, v in self.dval.items():
            if v > 0:
                deps[n] = v
        for en in self.h:
            seen = self.seen[en]
            for s, v in deps.items():
                if seen.get(s, 0) < v:
                    self.needed.add((s, v))
                    self.h[en].wait_ge(self.sems[s], self.sv(s, v))
                    seen[s] = v

    def mm(self, out_t, out_ap, lhsT, rhs, start=True, stop=True, R=()):
        self.op("pe", lambda e: e.matmul(out_ap, lhsT=lhsT, rhs=rhs, start=start, stop=stop), R=R, W=[out_t])

    def tr(self, out_t, out_ap, in_ap, ident_ap, R=()):
        self.op("pe", lambda e: e.transpose(out_ap, in_ap, ident_ap), R=R, W=[out_t])

    def tt(self, en, out_t, out_ap, a, b, op, R=()):
        self.op(en, lambda e: e.tensor_tensor(out=out_ap, in0=a, in1=b, op=op), R=R, W=[out_t])

    def stt(self, en, out_t, out_ap, a, s, b, op0, op1, R=()):
        en = "dve"
        self.op(en, lambda e: e.scalar_tensor_tensor(out=out_ap, in0=a, scalar=s, in1=b, op0=op0, op1=op1), R=R, W=[out_t])

    def ts(self, en, out_t, out_ap, a, s1, s2, op0, op1=None, R=()):
        if op1 is None:
            self.op(en, lambda e: e.tensor_scalar(out=out_ap, in0=a, scalar1=s1, scalar2=None, op0=op0), R=R, W=[out_t])
        else:
            self.op(en, lambda e: e.tensor_scalar(out=out_ap, in0=a, scalar1=s1, scalar2=s2, op0=op0, op1=op1), R=R, W=[out_t])

    def cp(self, en, out_t, out_ap, a, R=()):
        if en == "act":
            self.op(en, lambda e: e.copy(out=out_ap, in_=a), R=R, W=[out_t])
        else:
            self.op(en, lambda e: e.tensor_copy(out=out_ap, in_=a), R=R, W=[out_t])

    def act(self, out_t, out_ap, a, func, R=(), Wx=(), **kw):
        self.op("act", lambda e: e.activation(out=out_ap, in_=a, func=func, **kw), R=R, W=[out_t] + list(Wx))

    def red(self, en, out_t, out_ap, a, R=(), op=ALU.add):
        self.op(en, lambda e: e.tensor_reduce(out=out_ap, in_=a, axis=AX.X, op=op), R=R, W=[out_t])

    def rcp(self, out_t, out_ap, a, R=()):
        self.op("dve", lambda e: e.reciprocal(out=out_ap, in_=a), R=R, W=[out_t])

    def ms(self, en, out_t, out_ap, v):
        self.op(en, lambda e: e.memset(out_ap, v), W=[out_t])

    def finish(self):
        self.barrier()


class Ring:
    def __init__(self, k, n, shape, dt=F32, name=None):
        self.t = [k.sb(shape, dt, name=(f"{name}{i}" if name else None)) for i in range(n)]
        self.i = 0

    def next(self):
        t = self.t[self.i % len(self.t)]
        self.i += 1
        return t


def v3(ap, a):
    return ap.rearrange("p (a b) -> p a b", a=a)


def pump(gen, n=1):
    if gen is None:
        return
    for _ in range(n):
        try:
            next(gen)
        except StopIteration:
            return


def exhaust(gen):
    if gen is None:
        return
    for _ in gen:
        pass


def _consts():
    i = np.arange(128)
    r, c = i[:, None], i[None, :]
    ident = (r == c)
    U = (r < c)
    Lw = (r > c)
    Ui = (r <= c)
    Li = (r >= c)
    Pm = np.zeros((64, 64), np.float32)
    for a in range(16):
        Pm[a, a + 16] = -1.0
        Pm[a + 16, a] = 1.0
        Pm[a + 32, a + 48] = -1.0
        Pm[a + 48, a + 32] = 1.0
    PmT = np.zeros((128, 128), np.float32)
    PmT[:64, :64] = Pm.T
    PmT[64:, 64:] = Pm.T
    ones = np.ones((128, 128), np.float32)
    sel = np.zeros((128, 256), np.float32)
    sel[0, 0:128] = 1.0
    sel[1, 128:256] = 1.0
    cst = np.concatenate([x.astype(np.float32) for x in (ident, U, Lw, Ui, Li, PmT, ones, sel)], axis=1)
    nf = 16
    inv = 1.0 / (10000.0 ** (np.arange(nf, dtype=np.float32) / nf))
    t = np.arange(2048)
    row = (t // 64).astype(np.float32)
    col = (t % 64).astype(np.float32)
    ang_r = row[:, None] * inv[None]
    ang_c = col[:, None] * inv[None]
    ang = np.concatenate([ang_r, ang_r, ang_c, ang_c], -1).astype(np.float32)
    cos, sin = np.cos(ang).T, np.sin(ang).T
    rope = np.zeros((128, 2, 2048), np.float32)
    rope[:64, 0], rope[64:, 0] = cos, cos
    rope[:64, 1], rope[64:, 1] = sin, sin
    return np.ascontiguousarray(cst), rope


C_ID, C_U, C_LW, C_UI, C_LI, C_PM, C_ONE, C_SEL = [128 * i for i in range(8)]
NCST = 128 * 9


def build(debug=False, stop_after=None, plan=None):
    k = Kern(plan)
    nc = k.nc

    def din(name, shape):
        return nc.dram_tensor(name, list(shape), F32, kind="ExternalInput").ap()

    def dout(name, shape):
        return nc.dram_tensor(name, list(shape), F32, kind="ExternalOutput").ap()

    def dscr(name, shape, dt=F32, dbg=False):
        if dbg and debug:
            return nc.dram_tensor(name, list(shape), dt, kind="ExternalOutput").ap()
        return nc.dram_tensor(name, list(shape), dt).ap()

    I = {}
    for name, shape in [
        ("xp", (NPS * 256, D)), ("xs", (2048, D)), ("cak", (256, 128)), ("cav", (256, 128)),
        ("st0", (2, 8, 64, 64)), ("cck", (256, 1024)), ("ccv", (256, 1024)), ("cvec", (2, D)),
        ("ada_w", (2, D, 3 * D)), ("ada_b", (2, 3 * D)), ("norm_pre", (2, D)), ("norm_post", (2, D)),
        ("w_out", (2, D, D)), ("w_in0", (D, EVEN_IN)), ("a_sink", (1, 8)), ("b_mu", (1, 1664)),
        ("b_w0", (2, 512)), ("b_w2", (2, 64, 512)), ("b_a0", (2, 512)), ("b_a2", (2, 64, 512)),
        ("b_kk", (1, 512)), ("b_ka", (1, 512)), ("b_rk", (1, 512)), ("b_ln_w", (1, 512)), ("b_ln_b", (1, 512)),
        ("w_in1", (D, ODD_IN)), ("c_lq1", (1, 64)), ("c_lk1", (1, 64)), ("c_lq2", (1, 64)), ("c_lk2", (1, 64)),
        ("c_subln", (1, 128)), ("cst", (128, NCST)), ("rope", (128, 2, 2048)), ("rope_my", (128, 2, 512)),
    ]:
        I[name] = din(name, shape)
    I["myrows"] = nc.dram_tensor("myrows", [128, 4], mybir.dt.int32, kind="ExternalInput").ap()
    O = {}
    for name, shape in [("yp", (NPS * 256, D)), ("ys", (MYT * 128, D)), ("nak", (NPS, 256, 128)), ("nav", (NPS, 256, 128)),
                        ("nst", (NPS, 2, 8, 64, 64)), ("nck", (NPS, 256, 1024)), ("ncv", (NPS, 256, 1024))]:
        O[name] = dout(name, shape)

    NT = NPS * PT + ST
    hT_s = dscr("hT_s", (NT, 128, 1024), BF16)
    hsT_s = dscr("hsT_s", (NT, 128, 1024), BF16)
    xf_s = dscr("xf_s", (NT, 128, 1536), F32)
    lora_s = dscr("lora_s", (NT, 128, 128), F32)
    yf_s = dscr("yf_s", (NT, 128, 512), F32, dbg=True)
    yb_s = dscr("yb_s", (NT, 128, 512), F32)
    x1_s = dscr("x1_s", (NT * 128, D), F32, dbg=True)
    DT = {}

    def dt_(name, i):
        key = (name, i)
        if key not in DT:
            DT[key] = T(None)
        return DT[key]

    seqs = []
    for s in range(NPS):
        seqs.append(dict(n=PT, x=I["xp"][s * 256:(s + 1) * 256, :], t0=s * PT, g=0, kc0=2304 + 256 * s, sample=False, s=s))
    seqs.append(dict(n=ST, x=I["xs"], t0=NPS * PT, g=1, kc0=256, sample=True, s=None))

    cst = k.sb([128, NCST], F32, "cst")
    k.load(cst, cst.ap, I["cst"])
    cws = k.sb([128, 5 * 128], F32, "cws")
    for i, off in enumerate((C_UI, C_LI, C_U, C_LW, C_ONE)):
        k.ts("dve", cws, cws[:, i * 128:(i + 1) * 128], cst[:, off:off + 128], CW, None, ALU.mult, R=[cst])
    W_UI, W_LI, W_U, W_LW, W_ONE = [cws[:, i * 128:(i + 1) * 128] for i in range(5)]
    cI = [W_UI, W_LI]
    cE = [W_U, W_LW]
    cR = [W_LW, W_U]
    M4 = [k.sb([128, 512], F32, f"M4_{z}") for z in range(2)]
    for z, (a, b) in enumerate(((C_U, C_UI), (C_LW, C_LI))):
        for q in range(2):
            k.cp("dve", M4[z], M4[z][:, q * 256:q * 256 + 128], cst[:, a:a + 128], R=[cst])
            k.cp("dve", M4[z], M4[z][:, q * 256 + 128:q * 256 + 256], cst[:, b:b + 128], R=[cst])
    mS = [cst[:, C_LW:C_LW + 128], cst[:, C_U:C_U + 128]]
    identb = k.sb([128, 128], BF16, "identb")
    k.cp("dve", identb, identb.ap, cst[:, C_ID:C_ID + 128], R=[cst])
    pmtb = k.sb([128, 128], BF16, "pmtb")
    k.cp("dve", pmtb, pmtb.ap, cst[:, C_PM:C_PM + 128], R=[cst])
    identf = cst[:, C_ID:C_ID + 128]
    mhalf = k.sb([128, 16], F32, "mhalf")
    k.ms("dve", mhalf, mhalf.ap, -0.5)

    def bank_bf(b):
        return b.ap.bitcast(BF16)

    def load_bc(dst, src_row_ap, n=128):
        k.load(dst, dst.ap, src_row_ap.partition_broadcast(n))

    def modulation(l, Sh, A, G):
        with k.scope():
            cT = k.sb([128, 8, 2], F32, "cT")
            for g in range(2):
                cg = k.sb([8, 128], F32, "cg")
                k.load(cg, cg.ap, I["cvec"][g].rearrange("(c p) -> c p", p=128))
                psc = k.ps()
                k.mm(psc, psc[:, 0:8], cg.ap, cst[0:8, C_ID:C_ID + 8], R=[cg, cst])
                k.cp("dve", cT, cT[:, :, g], psc[:, 0:8], R=[psc])
            cTs = k.sb([128, 8, 2], F32, "cTs")
            k.act(cTs, cTs.ap, cT.ap, AF.Silu, R=[cT])
            stg = Ring(k, 2, [128, 3 * D], F32, "adastg")
            adab = k.sb([2, 3 * D], F32, "adab")
            load_bc(adab, I["ada_b"][l:l + 1, :], 2)
            npre = k.sb([128, D], F32, "npre")
            npost = k.sb([128, D], F32, "npost")
            load_bc(npre, I["norm_pre"][l:l + 1, :])
            load_bc(npost, I["norm_post"][l:l + 1, :])
            banks, idx = k.pin(6)
            for kc in range(8):
                st = stg.next()
                k.load(st, st.ap, I["ada_w"][l, kc * 128:(kc + 1) * 128, :])
                for nb in range(6):
                    k.mm(banks[nb], banks[nb][0:2, :], cTs[:, kc, :], st[:, nb * 512:(nb + 1) * 512],
                         start=(kc == 0), stop=(kc == 7), R=[cTs, st])
            mod2 = k.sb([2, 3 * D], F32, "mod2")
            for nb in range(6):
                k.tt("dve", mod2, mod2[0:2, nb * 512:(nb + 1) * 512], banks[nb][0:2, :], adab[0:2, nb * 512:(nb + 1) * 512],
                     ALU.add, R=[banks[nb], adab])
            k.unpin(idx)
            for g in range(2):
                for nb in range(6):
                    ps = k.ps()
                    k.mm(ps, ps.ap, cst[0:2, C_SEL + g * 128:C_SEL + (g + 1) * 128], mod2[0:2, nb * 512:(nb + 1) * 512], R=[cst, mod2])
                    cs = slice((nb % 2) * 512, (nb % 2) * 512 + 512)
                    if nb < 2:
                        k.cp("act", Sh[g], Sh[g][:, cs], ps.ap, R=[ps])
                    elif nb < 4:
                        k.stt("dve", A[g], A[g][:, cs], ps.ap, 1.0, npre[:, cs], ALU.add, ALU.mult, R=[ps, npre])
                    else:
                        k.tt("dve", G[g], G[g][:, cs], ps.ap, npost[:, cs], ALU.mult, R=[ps, npost])

    def load_cols(dsts, w_ap, c0, ncols, stg):
        for kc in range(8):
            st = stg.next()
            k.load(st, st[:, 0:ncols], w_ap[kc * 128:(kc + 1) * 128, c0:c0 + ncols])
            for i, (dt, fn, sc) in enumerate(dsts):
                en = "dve" if (kc + i) % 2 == 0 else "pool"
                if sc is None:
                    k.cp(en, dt, fn(kc), st[:, 0:ncols], R=[st])
                else:
                    k.tt(en, dt, fn(kc), st[:, 0:ncols], sc[:, 0:ncols], ALU.mult, R=[st, sc])


    mod_s = dscr("mod_s", (2, 2, 3, 128, D), F32)
    mod_t = T(None)

    def modulation_to_dram(l):
        with k.scope():
            Sh = [k.sb([128, D], F32, "Sh") for _ in range(2)]
            A = [k.sb([128, D], F32, "A") for _ in range(2)]
            G = [k.sb([128, D], F32, "G") for _ in range(2)]
            modulation(l, Sh, A, G)
            for g in range(2):
                for i, t in enumerate((Sh[g], A[g], G[g])):
                    k.store(mod_s[l, g, i], t, t.ap, W=[mod_t])

    def load_cols2(dsts, w_ap, c0, ncols, stg):
        for kc in range(8):
            st = stg.next()
            k.load(st, st[:, 0:ncols], w_ap[kc * 128:(kc + 1) * 128, c0:c0 + ncols])
            for i, (dt, fn, sc, (a, b)) in enumerate(dsts):
                en = "dve" if (kc + i) % 2 == 0 else "act"
                if sc is None:
                    k.cp(en, dt, fn(kc), st[:, a:b], R=[st])
                else:
                    k.tt("dve", dt, fn(kc), st[:, a:b], sc[:, a:b], ALU.mult, R=[st, sc])

    def run2(gens, width=2):
        active = []
        it = iter(gens)
        while True:
            while len(active) < width:
                g = next(it, None)
                if g is None:
                    break
                active.append(g)
            if not active:
                break
            for g in list(active):
                try:
                    next(g)
                except StopIteration:
                    active.remove(g)


    def rstd_of(ss, junk, parts, n_el, eps):
        k.ms("dve", ss, ss.ap, 0.0)
        for i, (t, ap) in enumerate(parts):
            k.act(junk, junk[:, 0:ap.shape[1]], ap, AF.Square, R=[t, ss], Wx=[ss], scale=float(n_el) ** -0.5,
                  accum_out=ss[:, i:i + 1])
        if len(parts) == 2:
            k.tt("dve", ss, ss[:, 0:1], ss[:, 0:1], ss[:, 1:2], ALU.add, R=[ss])
        k.ts("dve", ss, ss[:, 4:5], ss[:, 0:1], eps, None, ALU.add, R=[ss])
        k.tt("gp", ss, ss[:, 6:7], ss[:, 4:5], mhalf[:, 0:1], ALU.pow, R=[ss, mhalf])
        return ss[:, 6:7]

    def layer0():
        with k.scope():
            K2T = k.sb([128, 2, 3328], BF16, "K2T")
            Vall = k.sb([128, 26 * 2 * 65], BF16, "Vall")
            Vv = Vall.ap.rearrange("p (b g e) -> p b g e", b=26, g=2)
            k.ms("pool", Vall, Vall.ap, 1.0)
            with k.scope():
                for blk in range(2):
                    cvt = k.sb([128, 128], F32, "cvt")
                    k.load(cvt, cvt.ap, I["cav"][blk * 128:(blk + 1) * 128, :])
                    k.cp("dve", Vall, Vv[:, blk, :, 0:64], v3(cvt.ap, 2), R=[cvt])
                    ckt = k.sb([128, 128], F32, "ckt")
                    k.load(ckt, ckt.ap, I["cak"][blk * 128:(blk + 1) * 128, :])
                    kd = k.sb([128, 256], BF16, "kd")
                    for g in range(2):
                        for r in range(2):
                            k.cp("dve", kd, kd[:, g * 128 + r * 64:g * 128 + r * 64 + 64], ckt[:, g * 64:(g + 1) * 64], R=[ckt])
                    ps = k.ps()
                    pb = bank_bf(ps)
                    for g in range(2):
                        k.tr(ps, pb[:, g * 128:(g + 1) * 128], kd[:, g * 128:(g + 1) * 128], identb.ap, R=[kd, identb])
                    for g in range(2):
                        k.cp("act", K2T, K2T[:, g, blk * 128:(blk + 1) * 128], pb[:, g * 128:(g + 1) * 128], R=[ps])
            phaseF(K2T, Vall, Vv)
            if stop_after in ("F", "F1", "F2a"):
                return
            phaseB(K2T, Vall, Vv)

    def rw_consts(names):
        out = {}
        for nm in names:
            if nm in ("kkv", "kav", "rkv", "lnw", "lnb"):
                src = {"kkv": "b_kk", "kav": "b_ka", "rkv": "b_rk", "lnw": "b_ln_w", "lnb": "b_ln_b"}[nm]
                t = k.sb([128, 512], F32, nm)
                load_bc(t, I[src][0:1, :])
            elif nm[:2] in ("w0", "a0"):
                z = int(nm[2])
                t = k.sb([128, 512], F32, nm)
                load_bc(t, I["b_" + nm[:2]][z:z + 1, :])
            elif nm[:2] == "w2":
                z = int(nm[2])
                t = k.sb([128, 512], F32, nm)
                k.load(t, t[0:64, :], I["b_w2"][z])
            elif nm[:2] == "a2":
                z = int(nm[2])
                t = k.sb([128, 512], F32, nm)
                k.load(t, t[64:128, :], I["b_a2"][z])
            out[nm] = t
        return out

    def rw_temps(dbl=False):
        tm = {}
        for nm in ("kkr", "kk", "sg", "Ei", "Einv", "Ee", "Er", "beta"):
            tm[nm] = k.sb([128, 512], F32, nm)
        for nm in ("a0t", "keff0", "a1t", "keff1"):
            tm[nm] = k.sb([128, 512], F32, nm)
        tm["sq"], tm["wpre"], tm["apre0"], tm["apre1"] = tm["Ee"], tm["Er"], tm["Ei"], tm["Einv"]
        for nm in ("vb", "At", "Rt", "Bh", "Kh", "Bt", "Kt"):
            tm[nm] = k.sb([128, 512], BF16, nm)
        tm["s8"] = k.sb([128, 32], F32, "s8")
        tm["lTt"] = k.sb([128, 128], F32, "lTt")
        tm["FAR"] = k.sb([128, 1024], BF16, "FAR")
        tm["FBK"] = k.sb([128, 1024], BF16, "FBK")
        tm["G1"] = Ring(k, 4, [128, 512], BF16, "G1")
        tm["XX"] = Ring(k, 8, [128, 256], F32, "XX")
        tm["ZZ"] = Ring(k, 8, [128, 128], F32, "ZZ")
        tm["Zb"] = Ring(k, 4, [128, 128], BF16, "Zb")
        tm["OmT"] = Ring(k, 4, [64, 128], BF16, "OmT")
        tm["PP"] = k.sb([64, 8 * 128], F32, "PP")
        tm["dW"] = k.sb([64, 512], F32, "dW")
        tm["Wt"] = k.sb([64, 512], F32, "Wt")
        tm["bon"] = k.sb([128, 512], F32, "bon")
        if not dbl:
            return tm
        tm2 = dict(tm)
        for nm in ("vb", "At", "Rt", "Bh", "Kh", "Bt", "Kt"):
            tm2[nm] = k.sb([128, 512], BF16, nm + "2")
        tm2["FAR"] = k.sb([128, 1024], BF16, "FAR2")
        tm2["FBK"] = k.sb([128, 1024], BF16, "FBK2")
        tm2["dW"] = k.sb([64, 512], F32, "dW2")
        tm2["Wt"] = k.sb([64, 512], F32, "Wt2")
        tm2["bon"] = k.sb([128, 512], F32, "bon2")
        return tm, tm2

    def rw_feats(C, tm, xf, lT, zdec, za):
        r, kx, v = xf[:, 0:512], xf[:, 512:1024], xf[:, 1024:1536]
        k.cp("act", tm["vb"], tm["vb"].ap, v, R=[xf])
        yield
        k.tt("dve", tm["kkr"], tm["kkr"].ap, kx, C["kkv"].ap, ALU.mult, R=[xf, C["kkv"]])
        k.tt("pool", tm["sq"], tm["sq"].ap, tm["kkr"].ap, tm["kkr"].ap, ALU.mult, R=[tm["kkr"]])
        s8 = tm["s8"]
        k.red("dve", s8, s8[:, 0:8], v3(tm["sq"].ap, 8), R=[tm["sq"]])
        k.ts("dve", s8, s8[:, 8:16], s8[:, 0:8], 1e-24, None, ALU.max, R=[s8])
        k.tt("gp", s8, s8[:, 24:32], s8[:, 8:16], mhalf[:, 0:8], ALU.pow, R=[s8, mhalf])
        k.tt("dve", tm["kk"], v3(tm["kk"].ap, 8), v3(tm["kkr"].ap, 8), s8[:, 24:32].unsqueeze(2).to_broadcast([128, 8, 64]),
             ALU.mult, R=[tm["kkr"], s8])
        yield
        lTt = tm["lTt"]
        k.act(lTt, lTt[0:64, :], lT[0:64, :], AF.Tanh, R=[lT])
        k.cp("act", lTt, lTt[64:128, :], lT[64:128, :], R=[lT])
        yield
        ps = k.ps()
        k.mm(ps, ps.ap, lTt[0:64, :], C[f"w2{zdec}"][0:64, :], R=[lTt, C[f"w2{zdec}"]])
        k.tt("dve", tm["wpre"], tm["wpre"].ap, ps.ap, C[f"w0{zdec}"].ap, ALU.add, R=[ps, C[f"w0{zdec}"]])
        k.act(tm["sg"], tm["sg"].ap, tm["wpre"].ap, AF.Sigmoid, R=[tm["wpre"]])
        yield
        for z in za:
            ps = k.ps()
            k.mm(ps, ps.ap, lTt[64:128, :], C[f"a2{z}"][64:128, :], R=[lTt, C[f"a2{z}"]])
            ap_, a_, ke_ = tm[f"apre{z}"], tm[f"a{z}t"], tm[f"keff{z}"]
            k.tt("dve", ap_, ap_.ap, ps.ap, C[f"a0{z}"].ap, ALU.add, R=[ps, C[f"a0{z}"]])
            k.act(a_, a_.ap, ap_.ap, AF.Sigmoid, R=[ap_])
            k.stt("pool", ap_, ap_.ap, a_.ap, -1.0, C["kav"].ap, ALU.add, ALU.mult, R=[a_, C["kav"]])
            k.stt("dve", ke_, ke_.ap, ap_.ap, 1.0, kx, ALU.add, ALU.mult, R=[ap_, xf])
            yield
        if len(za) == 2:
            bt, s8b = tm["beta"], tm["s8"]
            k.tt("dve", bt, bt.ap, tm["keff0"].ap, tm["keff1"].ap, ALU.add, R=[tm["keff0"], tm["keff1"]])
            k.tt("dve", bt, bt.ap, bt.ap, r, ALU.mult, R=[bt, xf])
            k.tt("dve", bt, bt.ap, bt.ap, C["rkv"].ap, ALU.mult, R=[bt, C["rkv"]])
            k.red("dve", s8b, s8b[:, 0:8], v3(bt.ap, 8), R=[bt])
            k.ts("dve", s8b, s8b[:, 0:8], s8b[:, 0:8], 0.5, None, ALU.mult, R=[s8b])
            k.tt("dve", tm["bon"], v3(tm["bon"].ap, 8), v3(v, 8), s8b[:, 0:8].unsqueeze(2).to_broadcast([128, 8, 64]), ALU.mult,
                 R=[xf, s8b])
            yield
        z = zdec
        sg = tm["sg"]
        psI, psE, psR, psT = k.ps(), k.ps(), k.ps(), k.ps()
        k.mm(psI, psI.ap, cI[z], sg.ap, R=[cws, sg])
        k.mm(psE, psE.ap, cE[z], sg.ap, R=[cws, sg])
        k.mm(psR, psR.ap, cR[z], sg.ap, R=[cws, sg])
        k.mm(psT, psT[0:64, :], W_ONE[:, 0:64], sg.ap, R=[cws, sg])
        k.act(tm["Ei"], tm["Ei"].ap, psI.ap, AF.Exp, R=[psI])
        k.act(tm["Einv"], tm["Einv"].ap, psI.ap, AF.Exp, R=[psI], scale=-1.0)
        k.act(tm["Ee"], tm["Ee"].ap, psE.ap, AF.Exp, R=[psE])
        k.act(tm["Er"], tm["Er"].ap, psR.ap, AF.Exp, R=[psR])
        k.act(tm["Wt"], tm["Wt"].ap, psT[0:64, :], AF.Exp, R=[psT])
        yield
        a_, ke_ = tm[f"a{z}t"], tm[f"keff{z}"]
        k.stt("dve", tm["At"], tm["At"].ap, tm["kk"].ap, -1.0, tm["Ee"].ap, ALU.mult, ALU.mult, R=[tm["kk"], tm["Ee"]])
        k.tt("pool", tm["Rt"], tm["Rt"].ap, r, tm["Ei"].ap, ALU.mult, R=[xf, tm["Ei"]])
        yield
        k.tt("dve", tm["beta"], tm["beta"].ap, tm["kk"].ap, a_.ap, ALU.mult, R=[tm["kk"], a_])
        k.tt("dve", tm["Bh"], tm["Bh"].ap, tm["beta"].ap, tm["Einv"].ap, ALU.mult, R=[tm["beta"], tm["Einv"]])
        k.tt("pool", tm["Kh"], tm["Kh"].ap, ke_.ap, tm["Einv"].ap, ALU.mult, R=[ke_, tm["Einv"]])
        yield
        k.tt("dve", tm["Bt"], tm["Bt"].ap, tm["beta"].ap, tm["Er"].ap, ALU.mult, R=[tm["beta"], tm["Er"]])
        k.tt("pool", tm["Kt"], tm["Kt"].ap, ke_.ap, tm["Er"].ap, ALU.mult, R=[ke_, tm["Er"]])
        yield
        k.tt("dve", tm["dW"], v3(tm["dW"].ap, 8), cst[0:64, C_ID:C_ID + 64].unsqueeze(1).to_broadcast([64, 8, 64]),
             v3(tm["Wt"].ap, 8), ALU.mult, R=[cst, tm["Wt"]])
        for (dst, srcs) in ((tm["FAR"], (tm["At"], tm["Rt"])), (tm["FBK"], (tm["Bh"], tm["Kh"]))):
            ps = k.ps()
            pb = bank_bf(ps)
            for j in range(4):
                for x_, src in enumerate(srcs):
                    o = (j * 2 + x_) * 128
                    k.tr(ps, pb[:, o:o + 128], src[:, j * 128:(j + 1) * 128], identb.ap, R=[src, identb])
            k.cp("act", dst, dst.ap, pb, R=[ps])
            yield

    def rw_chunk(tm, z, S, S0b, bg=None):
        (psY, psP0, psP1), pidx = k.pin(3)
        psPP = (psP0, psP1)
        FAR, FBK, vb = tm["FAR"], tm["FBK"], tm["vb"]
        At, Rt, Bt, Kt = tm["At"], tm["Rt"], tm["Bt"], tm["Kt"]
        for grp in range(2):
            heads = [grp * 4 + i for i in range(4)]
            G1, XZ, XT = {}, {}, {}
            for pr2 in range(2):
                hp = heads[2 * pr2:2 * pr2 + 2]
                ops_ = {}
                for h in hp:
                    j, p = h // 2, h % 2
                    P_ = slice(64 * p, 64 * p + 64)
                    ops_[h] = dict(far=FAR[P_, j * 256:(j + 1) * 256], bT=FBK[P_, j * 256:j * 256 + 128],
                                   kT=FBK[P_, j * 256 + 128:j * 256 + 256], aT=FAR[P_, j * 256:j * 256 + 128],
                                   ps1=k.ps(), ps3=k.ps())
                for h in hp:
                    o_ = ops_[h]
                    k.mm(o_["ps1"], o_["ps1"][:, 0:256], o_["bT"], o_["far"], R=[FBK, FAR])
                for h in hp:
                    o_ = ops_[h]
                    k.mm(o_["ps1"], o_["ps1"][:, 256:512], o_["kT"], o_["far"], R=[FBK, FAR])
                for h in hp:
                    o_ = ops_[h]
                    k.mm(o_["ps3"], o_["ps3"][:, 0:128], o_["aT"], o_["bT"], R=[FBK, FAR])
                for h in hp:
                    ps1, ps3 = ops_[h]["ps1"], ops_[h]["ps3"]
                    g1 = tm["G1"].next()
                    k.tt("dve", g1, g1.ap, ps1.ap, M4[z].ap, ALU.mult, R=[ps1, M4[z]])
                    xx = tm["XX"].next()
                    zz = tm["ZZ"].next()
                    k.tt("dve", xx, xx[:, 0:128], ps3[:, 0:128], mS[z], ALU.mult, R=[ps3, cst])
                    k.tt("dve", xx, xx[:, 128:256], ps1[:, 0:128], M4[z][:, 0:128], ALU.mult, R=[ps1, M4[z]])
                    k.cp("act", zz, zz[:, 0:64], At[:, h * 64:(h + 1) * 64], R=[At])
                    G1[h], XZ[h], XT[h] = g1, xx, zz
                for h in hp:
                    ps4 = k.ps()
                    k.mm(ps4, ps4[:, 0:64], G1[h][:, 256:384], vb[:, h * 64:(h + 1) * 64], R=[G1[h], vb])
                    k.cp("act", XT[h], XT[h][:, 64:128], ps4[:, 0:64], R=[ps4])
            pump(bg, 1)
            for i in range(7):
                pr_ = {}
                for h in heads:
                    b_ = k.ps()
                    pr_[h] = b_
                    xx, zz = XZ[h], XT[h]
                    if i < 6:
                        k.mm(b_, b_[:, 0:128], xx[:, 128:256], xx[:, 0:128], R=[xx])
                        k.mm(b_, b_[:, 128:256], xx[:, 0:128], xx[:, 128:256], R=[xx])
                    k.mm(b_, b_[:, 256:384], xx[:, 128:256], zz.ap, R=[xx, zz])
                for h in heads:
                    b_ = pr_[h]
                    xx, zz = XZ[h], XT[h]
                    nzz = tm["ZZ"].next()
                    k.tt("dve", nzz, nzz.ap, b_[:, 256:384], zz.ap, ALU.add, R=[b_, zz])
                    if i < 6:
                        nxx = tm["XX"].next()
                        k.cp("act", nxx, nxx.ap, b_[:, 0:256], R=[b_])
                        XZ[h] = nxx
                    XT[h] = nzz
                pump(bg, 1)
            Zs, oms = {}, {}
            for h in heads:
                Z = tm["Zb"].next()
                k.cp("act", Z, Z.ap, XT[h].ap, R=[XT[h]])
                Zs[h] = Z
            psOm = k.ps()
            for q_, h in enumerate(heads):
                Z = Zs[h]
                Ah, Gh = Z[:, 0:64], Z[:, 64:128]
                hs_ = slice(h * 64, (h + 1) * 64)
                bank = psPP[h // 4]
                o = (h % 4) * 128
                k.mm(bank, bank[0:64, o:o + 64], Ah, Bt[:, hs_], R=[Z, Bt])
                k.mm(bank, bank[0:64, o + 64:o + 128], Bt[:, hs_], Gh, start=True, stop=False, R=[Z, Bt])
                k.mm(bank, bank[0:64, o + 64:o + 128], Kt[:, hs_], vb[:, hs_], start=False, stop=True, R=[Kt, vb])
                k.mm(psOm, psOm[0:64, q_ * 128:(q_ + 1) * 128], Ah, G1[h][:, 128:256], start=True, stop=False, R=[Z, G1[h]])
                k.mm(psOm, psOm[0:64, q_ * 128:(q_ + 1) * 128], Rt[:, hs_], identb.ap, start=False, stop=True, R=[Rt, identb])
            for q_, h in enumerate(heads):
                om = tm["OmT"].next()
                k.cp("act", om, om.ap, psOm[0:64, q_ * 128:(q_ + 1) * 128], R=[psOm])
                oms[h] = om
            for h in heads:
                Z = Zs[h]
                Gh = Z[:, 64:128]
                hs_ = slice(h * 64, (h + 1) * 64)
                k.mm(psY, psY[:, hs_], G1[h][:, 128:256], Gh, start=True, stop=False, R=[G1[h], Z])
                k.mm(psY, psY[:, hs_], G1[h][:, 384:512], vb[:, hs_], start=False, stop=False, R=[G1[h], vb])
                k.mm(psY, psY[:, hs_], oms[h].ap, S0b[0:64, hs_], start=False, stop=True, R=[oms[h], S0b])
            pump(bg, 1)
        PP = tm["PP"]
        PPv = v3(PP.ap, 8)
        for b in range(2):
            pv = v3(psPP[b][0:64, :], 4)
            k.tt("dve", PP, PPv[:, b * 4:(b + 1) * 4, 0:64], pv[:, :, 0:64], v3(tm["dW"].ap, 8)[:, b * 4:(b + 1) * 4, :], ALU.add,
                 R=[psPP[b], tm["dW"]])
            k.cp("act", PP, PPv[:, b * 4:(b + 1) * 4, 64:128], pv[:, :, 64:128], R=[psPP[b]])
        psS = k.ps()
        for h in range(8):
            k.mm(psS, psS[0:64, h * 64:(h + 1) * 64], PP[:, h * 128:h * 128 + 64], S[0:64, h * 64:(h + 1) * 64], R=[PP, S])
        k.tt("dve", S, v3(S.ap, 8), v3(psS[0:64, :], 8), PPv[:, :, 64:128], ALU.add, R=[psS, PP])
        k.cp("pool", S0b, S0b.ap, S.ap, R=[S])
        k.unpin(pidx[1:])
        return psY, pidx[0:1]

    def phaseF(K2T, Vall, Vv):
        with k.scope():
            Wkv = k.sb([128, 8 * 256], BF16, "Wkv")
            WK2 = k.sb([128, 8 * 256], BF16, "WK2")
            Wr1 = k.sb([128, 8 * 1664], BF16, "Wr1")
            Wr2 = k.sb([128, 8 * 1664], BF16, "Wr2")
            with k.scope():
                stg = Ring(k, 2, [128, 1664], F32, "wstg")
                mu1 = k.sb([128, 1664], F32, "mu1")
                mu2 = k.sb([128, 1664], F32, "mu2")
                load_bc(mu1, I["b_mu"][0:1, :])
                k.ts("dve", mu2, mu2.ap, mu1.ap, 0.5, None, ALU.mult, R=[mu1])
                k.ts("dve", mu1, mu1.ap, mu1.ap, -1.0, 1.0, ALU.mult, ALU.add, R=[mu1])
                d = [(Wkv, lambda kc: Wkv[:, kc * 256:(kc + 1) * 256], None, (0, 256))]
                for g in range(2):
                    for r in range(2):
                        d.append((WK2, (lambda kc, g=g, r=r: WK2[:, kc * 256 + g * 128 + r * 64:kc * 256 + g * 128 + r * 64 + 64]),
                                  None, (g * 64, g * 64 + 64)))
                load_cols2(d, I["w_in0"], 512, 256, stg)
                load_cols2([(Wr1, lambda kc: Wr1[:, kc * 1664:(kc + 1) * 1664], mu1, (0, 1664)),
                            (Wr2, lambda kc: Wr2[:, kc * 1664:(kc + 1) * 1664], mu2, (0, 1664))], I["w_in0"], 1280, 1664, stg)
                modulation_to_dram(0)
                Sh = k.sb([128, D], F32, "ShF")
                A = k.sb([128, D], F32, "AF")
                xr = Ring(k, 2, [128, D], F32, "xr")
                tmpfr = Ring(k, 2, [128, D], F32, "tmpf")
                hbr = Ring(k, 2, [128, D], BF16, "hb")
                junk = k.sb([128, D], BF16, "junk")
                ssr = Ring(k, 2, [128, 8], F32, "ssr")
                hTr = Ring(k, 4, [128, 1024], BF16, "hTr")
                hsr = Ring(k, 2, [128, 1024], BF16, "hsr")
                curg = None
                for sq in seqs:
                    if sq["g"] != curg:
                        curg = sq["g"]
                        k.load(Sh, Sh.ap, mod_s[0, curg, 0], R=[mod_t])
                        k.load(A, A.ap, mod_s[0, curg, 1], R=[mod_t])
                    n = sq["n"]
                    hTs = {}

                    def make_hs(cm):
                        cur = v3(hTs[cm].ap, 8)
                        hs = hsr.next()
                        hv = v3(hs.ap, 8)
                        k.tt("dve", hs, hv[:, :, 1:127], cur[:, :, 0:126], cur[:, :, 2:128], ALU.add, R=[hTs[cm]])
                        if cm > 0:
                            k.tt("pool", hs, hv[:, :, 0:1], cur[:, :, 1:2], v3(hTs[cm - 1].ap, 8)[:, :, 127:128], ALU.add,
                                 R=[hTs[cm], hTs[cm - 1]])
                        else:
                            k.cp("pool", hs, hv[:, :, 0:1], cur[:, :, 1:2], R=[hTs[cm]])
                        if cm < n - 1:
                            k.tt("pool", hs, hv[:, :, 127:128], cur[:, :, 126:127], v3(hTs[cm + 1].ap, 8)[:, :, 0:1], ALU.add,
                                 R=[hTs[cm], hTs[cm + 1]])
                        else:
                            k.cp("pool", hs, hv[:, :, 127:128], cur[:, :, 126:127], R=[hTs[cm]])
                        k.store(hsT_s[sq["t0"] + cm], hs, hs.ap, W=[dt_("hsT", sq["t0"] + cm)])

                    def f1_gen(c, sq=sq, n=n, hTs=hTs, make_hs=make_hs):
                        xt = xr.next()
                        k.load(xt, xt.ap, sq["x"][c * 128:(c + 1) * 128, :])
                        ss_t = ssr.next()
                        rstd = rstd_of(ss_t, junk, [(xt, xt.ap)], D, 1e-6)
                        yield
                        tmpf = tmpfr.next()
                        k.stt("dve", tmpf, tmpf.ap, xt.ap, rstd, A.ap, ALU.mult, ALU.mult, R=[xt, ss_t, A])
                        hb = hbr.next()
                        k.tt("pool", hb, hb.ap, tmpf.ap, Sh.ap, ALU.add, R=[tmpf, Sh])
                        yield
                        ps = k.ps()
                        pb = bank_bf(ps)
                        for kc in range(8):
                            k.tr(ps, pb[:, kc * 128:(kc + 1) * 128], hb[:, kc * 128:(kc + 1) * 128], identb.ap, R=[hb, identb])
                        hT = hTr.next()
                        k.cp("act", hT, hT.ap, pb, R=[ps])
                        hTs[c] = hT
                        k.store(hT_s[sq["t0"] + c], hT, hT.ap, W=[dt_("hT", sq["t0"] + c)])
                        yield
                        if c >= 1:
                            make_hs(c - 1)
                        if c == n - 1:
                            make_hs(n - 1)

                    run2([f1_gen(c) for c in range(n)])
            if stop_after == "F1":
                return
            with k.scope():
                C = rw_consts(["kkv", "kav", "w00", "a00", "w20", "a20"])
                tm = rw_temps()
                hTl = Ring(k, 2, [128, 1024], BF16, "hTl")
                hsTl = Ring(k, 2, [128, 1024], BF16, "hsTl")
                xfr = Ring(k, 2, [128, 1536], F32, "xfr")
                lTr = Ring(k, 2, [128, 128], F32, "lTr")
                kvr = Ring(k, 2, [128, 256], F32, "kvr")
                ropr = Ring(k, 2, [128, 256], F32, "ropr")
                kbr = Ring(k, 2, [128, 128], BF16, "kbr")
                t1r = Ring(k, 2, [128, 128], F32, "t1r")
                yfr = Ring(k, 2, [128, 512], F32, "yfr")
                S = k.sb([64, 512], F32, "S")
                S0b = k.sb([64, 512], BF16, "S0b")
                sto = k.sb([64, 512], F32, "sto")
                def proj_gen(sq, c, outd):
                    gt = sq["t0"] + c
                    hT = hTl.next()
                    k.load(hT, hT.ap, hT_s[gt], R=[dt_("hT", gt)])
                    hs = hsTl.next()
                    k.load(hs, hs.ap, hsT_s[gt], R=[dt_("hsT", gt)])
                    xf = xfr.next()
                    for nb in range(3):
                        ps = k.ps()
                        for kc in range(8):
                            k.mm(ps, ps.ap, hT[:, kc * 128:(kc + 1) * 128], Wr1[:, kc * 1664 + nb * 512:kc * 1664 + (nb + 1) * 512],
                                 start=(kc == 0), stop=False, R=[hT, Wr1])
                            k.mm(ps, ps.ap, hs[:, kc * 128:(kc + 1) * 128], Wr2[:, kc * 1664 + nb * 512:kc * 1664 + (nb + 1) * 512],
                                 start=False, stop=(kc == 7), R=[hs, Wr2])
                        k.cp("act", xf, xf[:, nb * 512:(nb + 1) * 512], ps.ap, R=[ps])
                        yield
                    ps = k.ps()
                    for kc in range(8):
                        k.mm(ps, ps[:, 0:128], Wr1[:, kc * 1664 + 1536:kc * 1664 + 1664], hT[:, kc * 128:(kc + 1) * 128],
                             start=(kc == 0), stop=False, R=[hT, Wr1])
                        k.mm(ps, ps[:, 0:128], Wr2[:, kc * 1664 + 1536:kc * 1664 + 1664], hs[:, kc * 128:(kc + 1) * 128],
                             start=False, stop=(kc == 7), R=[hs, Wr2])
                    lT = lTr.next()
                    k.cp("act", lT, lT.ap, ps[:, 0:128], R=[ps])
                    k.store(xf_s[gt], xf, xf.ap, W=[dt_("xf", gt)])
                    k.store(lora_s[gt], lT, lT.ap, W=[dt_("lora", gt)])
                    outd[c] = (xf, lT)
                    yield
                    ps = k.ps()
                    for kc in range(8):
                        k.mm(ps, ps[:, 0:256], hT[:, kc * 128:(kc + 1) * 128], Wkv[:, kc * 256:(kc + 1) * 256],
                             start=(kc == 0), stop=(kc == 7), R=[hT, Wkv])
                    blk = (sq["kc0"] // 128) + c
                    k.cp("dve", Vall, Vv[:, blk, :, 0:64], v3(ps[:, 128:256], 2), R=[ps])
                    if not sq["sample"]:
                        kv = kvr.next()
                        k.cp("act", kv, kv.ap, ps[:, 0:256], R=[ps])
                        k.store(O["nak"][sq["s"], c * 128:(c + 1) * 128, :], kv, kv[:, 0:128])
                        k.store(O["nav"][sq["s"], c * 128:(c + 1) * 128, :], kv, kv[:, 128:256])
                    yield
                    if sq["sample"]:
                        rp = ropr.next()
                        k.load(rp, v3(rp.ap, 2), I["rope"][:, :, c * 128:(c + 1) * 128])
                    for g in range(2):
                        ps = k.ps()
                        for kc in range(8):
                            k.mm(ps, ps[:, 0:128], WK2[:, kc * 256 + g * 128:kc * 256 + (g + 1) * 128], hT[:, kc * 128:(kc + 1) * 128],
                                 start=(kc == 0), stop=(kc == 7), R=[hT, WK2])
                        dst = K2T[:, g, sq["kc0"] + c * 128:sq["kc0"] + (c + 1) * 128]
                        if sq["sample"]:
                            rope_apply(ps, ps[:, 0:128], K2T, dst, rp, kbr, t1r, 1)
                        else:
                            k.cp("act", K2T, dst, ps[:, 0:128], R=[ps])
                        yield

                for sq in seqs:
                    n = sq["n"]
                    if sq["sample"]:
                        load_state(S, S0b, 0, sto)
                    else:
                        k.ms("dve", S, S.ap, 0.0)
                        k.ms("pool", S0b, S0b.ap, 0.0)
                    outd = {}
                    gens = [proj_gen(sq, c, outd) for c in range(n)]
                    exhaust(gens[0])
                    for c in range(n):
                        gt = sq["t0"] + c
                        xf, lT = outd[c]
                        bg = gens[c + 1] if c + 1 < n else None
                        for _ in rw_feats(C, tm, xf, lT, 0, [0]):
                            pump(bg, 1)
                        exhaust(bg)
                        psY, yidx = rw_chunk(tm, 0, S, S0b)
                        yf = yfr.next()
                        k.cp("act", yf, yf.ap, psY.ap, R=[psY])
                        k.unpin(yidx)
                        k.store(yf_s[gt], yf, yf.ap, W=[dt_("yf", gt)])
                    if not sq["sample"]:
                        store_state(S, sto, O["nst"][sq["s"], 0])

    def rope_apply(ps_t, ps_ap, dst_t, dst_ap, rp, kbr, t1r, npair):
        w = npair * 128
        kb = kbr.next()
        k.cp("act", kb, kb[:, 0:w], ps_ap, R=[ps_t])
        pr = k.ps()
        k.mm(pr, pr[:, 0:w], pmtb.ap, kb[:, 0:w], R=[pmtb, kb])
        t1 = t1r.next()
        cosb = rp[:, 0:128]
        sinb = rp[:, 128:256]
        if npair > 1:
            cosb = cosb.unsqueeze(1).to_broadcast([128, npair, 128])
            sinb = sinb.unsqueeze(1).to_broadcast([128, npair, 128])
            k.tt("dve", t1, v3(t1[:, 0:w], npair), v3(ps_ap, npair), cosb, ALU.mult, R=[ps_t, rp])
            k.tt("dve", kb, v3(kb[:, 0:w], npair), v3(pr[:, 0:w], npair), sinb, ALU.mult, R=[pr, rp])
        else:
            k.tt("dve", t1, t1[:, 0:w], ps_ap, cosb, ALU.mult, R=[ps_t, rp])
            k.tt("dve", kb, kb[:, 0:w], pr[:, 0:w], sinb, ALU.mult, R=[pr, rp])
        k.tt("pool", dst_t, dst_ap, t1[:, 0:w], kb[:, 0:w], ALU.add, R=[t1, kb])

    def load_state(S, S0b, z, raw):
        k.load(raw, v3(raw.ap, 8), I["st0"][z].rearrange("h v k -> v h k"))
        ps = k.ps()
        for h in range(8):
            k.mm(ps, ps[0:64, h * 64:(h + 1) * 64], raw[:, h * 64:(h + 1) * 64], cst[0:64, C_ID:C_ID + 64], R=[raw, cst])
        k.cp("act", S, S.ap, ps[0:64, :], R=[ps])
        k.cp("dve", S0b, S0b.ap, ps[0:64, :], R=[ps])

    def store_state(S, sto, out_ap):
        ps = k.ps()
        for h in range(8):
            k.mm(ps, ps[0:64, h * 64:(h + 1) * 64], S[:, h * 64:(h + 1) * 64], cst[0:64, C_ID:C_ID + 64], R=[S, cst])
        k.cp("act", sto, sto.ap, ps[0:64, :], R=[ps])
        k.store(out_ap.rearrange("h v k -> v h k"), sto, v3(sto.ap, 8))

    def phaseB(K2T, Vall, Vv):
        with k.scope():
            C = rw_consts(["kkv", "kav", "rkv", "lnw", "lnb", "w01", "a00", "a01", "w21", "a20", "a21"])
            tms = rw_temps(dbl=True)
            xfr = Ring(k, 2, [128, 1536], F32, "xfrB")
            lTr = Ring(k, 2, [128, 128], F32, "lTrB")
            yfr = Ring(k, 2, [128, 512], F32, "yfrB")
            ycr = Ring(k, 2, [128, 512], F32, "ycr")
            ysq = k.sb([128, 512], F32, "ysq")
            s8 = k.sb([128, 40], F32, "s8B")
            S = k.sb([64, 512], F32, "SB")
            S0b = k.sb([64, 512], BF16, "S0bB")
            sto = k.sb([64, 512], F32, "stoB")
            work = [(sq, c) for sq in seqs for c in range(sq["n"] - 1, -1, -1)]
            yfs = {}

            def feats_gen(i):
                sq, c = work[i]
                gt = sq["t0"] + c
                xf = xfr.next()
                k.load(xf, xf.ap, xf_s[gt], R=[dt_("xf", gt)])
                lT = lTr.next()
                k.load(lT, lT.ap, lora_s[gt], R=[dt_("lora", gt)])
                yf = yfr.next()
                k.load(yf, yf.ap, yf_s[gt], R=[dt_("yf", gt)])
                yfs[i] = yf
                yield from rw_feats(C, tms[i % 2], xf, lT, 1, [0, 1])

            ysr = Ring(k, 2, [128, 512], F32, "ysr")

            def fin_gen(i, ys_, tm):
                sq, c = work[i]
                gt = sq["t0"] + c
                yc = ycr.next()
                k.red("dve", s8, s8[:, 0:8], v3(ys_.ap, 8), R=[ys_])
                k.ts("dve", s8, s8[:, 8:16], s8[:, 0:8], 1.0 / 64, None, ALU.mult, R=[s8])
                k.tt("dve", yc, v3(yc.ap, 8), v3(ys_.ap, 8), s8[:, 8:16].unsqueeze(2).to_broadcast([128, 8, 64]), ALU.subtract,
                     R=[ys_, s8])
                yield
                k.tt("pool", ysq, ysq.ap, yc.ap, yc.ap, ALU.mult, R=[yc])
                k.red("dve", s8, s8[:, 16:24], v3(ysq.ap, 8), R=[ysq])
                k.ts("dve", s8, s8[:, 16:24], s8[:, 16:24], 1.0 / 64, 64e-5, ALU.mult, ALU.add, R=[s8])
                k.tt("gp", s8, s8[:, 32:40], s8[:, 16:24], mhalf[:, 0:8], ALU.pow, R=[s8, mhalf])
                yield
                k.tt("dve", yc, v3(yc.ap, 8), v3(yc.ap, 8), s8[:, 32:40].unsqueeze(2).to_broadcast([128, 8, 64]), ALU.mult, R=[yc, s8])
                k.tt("pool", yc, yc.ap, yc.ap, C["lnw"].ap, ALU.mult, R=[yc, C["lnw"]])
                yield
                k.tt("pool", yc, yc.ap, yc.ap, C["lnb"].ap, ALU.add, R=[yc, C["lnb"]])
                k.tt("pool", yc, yc.ap, yc.ap, tm["bon"].ap, ALU.add, R=[yc, tm["bon"]])
                k.store(yb_s[gt], yc, yc.ap, W=[dt_("yb", gt)])
                yield

            def chain(*gs):
                for g in gs:
                    if g is not None:
                        yield from g

            gens = [feats_gen(i) for i in range(len(work))]
            exhaust(gens[0])
            fin_prev = None
            for i, (sq, c) in enumerate(work):
                n = sq["n"]
                gt = sq["t0"] + c
                tm = tms[i % 2]
                if c == n - 1:
                    if sq["sample"]:
                        load_state(S, S0b, 1, sto)
                    else:
                        k.ms("dve", S, S.ap, 0.0)
                        k.ms("pool", S0b, S0b.ap, 0.0)
                nxt = gens[i + 1] if i + 1 < len(work) else None
                bg = chain(fin_prev, nxt)
                psY, yidx = rw_chunk(tm, 1, S, S0b, bg)
                yf = yfs.pop(i)
                ys_ = ysr.next()
                k.tt("dve", ys_, ys_.ap, psY.ap, yf.ap, ALU.add, R=[psY, yf])
                k.unpin(yidx)
                exhaust(bg)
                fin_prev = fin_gen(i, ys_, tm)
                if c == 0 and not sq["sample"]:
                    store_state(S, sto, O["nst"][sq["s"], 1])
            exhaust(fin_prev)
        with k.scope():
            Wq = k.sb([128, 8 * 512], BF16, "Wq")
            Wg = k.sb([128, 8 * 1024], BF16, "Wg")
            WO = k.sb([128, 8 * 1024], BF16, "WO")
            with k.scope():
                stg = Ring(k, 2, [128, 1024], F32, "wstgB")
                load_cols2([(Wq, lambda kc: Wq[:, kc * 512:(kc + 1) * 512], None, (0, 512))], I["w_in0"], 0, 512, stg)
                load_cols2([(Wg, lambda kc: Wg[:, kc * 1024:kc * 1024 + 512], None, (0, 512))], I["w_in0"], 768, 512, stg)
                load_cols2([(Wg, lambda kc: Wg[:, kc * 1024 + 512:(kc + 1) * 1024], None, (0, 512))], I["w_in0"], 2944, 512, stg)
                load_cols2([(WO, lambda kc: WO[:, kc * 1024:(kc + 1) * 1024], None, (0, 1024))], I["w_out"][0], 0, 1024, stg)
            esink = k.sb([128, 8], F32, "esink")
            load_bc(esink, I["a_sink"][0:1, :])
            k.act(esink, esink.ap, esink.ap, AF.Exp, R=[esink])
            mLi = k.sb([128, 128], BF16, "mLi")
            mUi = k.sb([128, 128], BF16, "mUi")
            k.cp("dve", mLi, mLi.ap, cst[:, C_LI:C_LI + 128], R=[cst])
            k.cp("dve", mUi, mUi.ap, cst[:, C_UI:C_UI + 128], R=[cst])
            Gms = [k.sb([128, D], F32, f"GmB{g}") for g in range(2)]
            for g in range(2):
                k.load(Gms[g], Gms[g].ap, mod_s[0, g, 2], R=[mod_t])
            hTl = Ring(k, 2, [128, 1024], BF16, "hTlB")
            xr = Ring(k, 2, [128, D], F32, "xrB")
            ybl = Ring(k, 2, [128, 512], F32, "ybl")
            ropr = Ring(k, 2, [128, 256], F32, "roprB")
            kbr = Ring(k, 2, [128, 512], BF16, "kbrB")
            t1r = Ring(k, 2, [128, 512], F32, "t1rB")
            qTr = Ring(k, 2, [128, 512], BF16, "qTr")
            ptr = Ring(k, 3, [128, 640], BF16, "ptr")
            yar = Ring(k, 2, [128, 512], F32, "yar")
            dn = k.sb([128, 16], F32, "dn")
            sgar = Ring(k, 2, [128, 1024], F32, "sgar")
            mbr = Ring(k, 2, [128, 1024], BF16, "mb")
            mTr = Ring(k, 2, [128, 1024], BF16, "mT")
            junk = k.sb([128, 512], BF16, "junkB")
            ssr = Ring(k, 2, [128, 8], F32, "ssrB")
            workC = [(sq, c) for sq in seqs for c in range(sq["n"])]
            resC = {}

            def tileC_gen(i):
                sq, c = workC[i]
                n = sq["n"]
                gt = sq["t0"] + c
                if True:
                    hT = hTl.next()
                    k.load(hT, hT.ap, hT_s[gt], R=[dt_("hT", gt)])
                    xt = xr.next()
                    k.load(xt, xt.ap, sq["x"][c * 128:(c + 1) * 128, :])
                    yc = ybl.next()
                    k.load(yc, yc.ap, yb_s[gt], R=[dt_("yb", gt)])
                    ya = yar.next()
                    sga = sgar.next()
                    resC[i] = (xt, yc, ya, sga)
                    def attn_gen():
                        psq = k.ps()
                        for j in range(4):
                            for kc in range(8):
                                k.mm(psq, psq[:, j * 128:(j + 1) * 128], Wq[:, kc * 512 + j * 128:kc * 512 + (j + 1) * 128],
                                     hT[:, kc * 128:(kc + 1) * 128], start=(kc == 0), stop=(kc == 7), R=[Wq, hT])
                        qT = qTr.next()
                        if sq["sample"]:
                            rp = ropr.next()
                            k.load(rp, v3(rp.ap, 2), I["rope"][:, :, c * 128:(c + 1) * 128])
                            rope_apply(psq, psq.ap, qT, qT.ap, rp, kbr, t1r, 4)
                            blocks = [(0, None), (128, None)]
                            if c > 0:
                                blocks.append((sq["kc0"] + (c - 1) * 128, mLi))
                            blocks.append((sq["kc0"] + c * 128, None))
                            if c < n - 1:
                                blocks.append((sq["kc0"] + (c + 1) * 128, mUi))
                        else:
                            k.cp("act", qT, qT.ap, psq.ap, R=[psq])
                            blocks = [(sq["kc0"] + cc * 128, None) for cc in range(n)]
                        nkb = len(blocks)
                        yield
                        (psO,), oidx = k.pin(1)
                        pssb, PTb = {}, {}

                        def atA(h):
                            j, p, g = h // 2, h % 2, h // 4
                            P_ = slice(64 * p, 64 * p + 64)
                            pss = [k.ps() for _ in range((nkb + 3) // 4)]
                            for i, (col, msk) in enumerate(blocks):
                                b_ = pss[i // 4]
                                k.mm(b_, b_[:, (i % 4) * 128:(i % 4 + 1) * 128], K2T[P_, g, col:col + 128], qT[P_, j * 128:(j + 1) * 128],
                                     R=[K2T, qT])
                            pssb[h] = pss

                        def atB(h):
                            pss = pssb.pop(h)
                            PT_ = ptr.next()
                            for bi, b_ in enumerate(pss):
                                w = min(4, nkb - bi * 4) * 128
                                k.act(PT_, PT_[:, bi * 512:bi * 512 + w], b_[:, 0:w], AF.Exp, R=[b_], scale=0.125)
                            for i, (col, msk) in enumerate(blocks):
                                if msk is not None:
                                    k.tt("pool", PT_, PT_[:, i * 128:(i + 1) * 128], PT_[:, i * 128:(i + 1) * 128], msk.ap, ALU.mult, R=[PT_, msk])
                            PTb[h] = PT_

                        def atC(h):
                            hg, hh, g = h // 4, h % 4, h // 4
                            PT_ = PTb.pop(h)
                            for i, (col, msk) in enumerate(blocks):
                                k.mm(psO, psO[:, hh * 65:(hh + 1) * 65], PT_[:, i * 128:(i + 1) * 128], Vv[:, col // 128, g, :],
                                     start=(i == 0), stop=(i == nkb - 1), R=[PT_, Vall])
                            if hh != 3:
                                return
                            pv = psO[:, 0:260].rearrange("p (a b) -> p a b", a=4)
                            k.tt("dve", dn, dn[:, 0:4], pv[:, :, 64], esink[:, hg * 4:(hg + 1) * 4], ALU.add, R=[psO, esink])
                            k.rcp(dn, dn[:, 4:8], dn[:, 0:4], R=[dn])
                            k.tt("dve", ya, v3(ya.ap, 8)[:, hg * 4:(hg + 1) * 4, :], pv[:, :, 0:64],
                                 dn[:, 4:8].unsqueeze(2).to_broadcast([128, 4, 64]), ALU.mult, R=[psO, dn])

                        for h in range(-1, 8):
                            if h + 1 < 8:
                                atA(h + 1)
                            if h >= 0:
                                atB(h)
                                atC(h)
                            yield
                        k.unpin(oidx)
                        for gi in range(2):
                            psg = k.ps()
                            for kc in range(8):
                                k.mm(psg, psg.ap, hT[:, kc * 128:(kc + 1) * 128], Wg[:, kc * 1024 + gi * 512:kc * 1024 + (gi + 1) * 512],
                                     start=(kc == 0), stop=(kc == 7), R=[hT, Wg])
                            k.act(sga, sga[:, gi * 512:(gi + 1) * 512], psg.ap, AF.Silu, R=[psg])
                            yield
                    yield from attn_gen()

            def fullC_gen(i):
                sq, c = workC[i]
                gt = sq["t0"] + c
                yield from tileC_gen(i)
                xt, yc, ya, sga = resC.pop(i)
                mb, mT = mbr.next(), mTr.next()
                k.tt("dve", mb, mb[:, 0:512], ya.ap, sga[:, 0:512], ALU.mult, R=[ya, sga])
                k.tt("pool", mb, mb[:, 512:1024], yc.ap, sga[:, 512:1024], ALU.mult, R=[yc, sga])
                yield
                yield from out_proj_gen(mb, mT, WO, Gms[sq["g"]], xt, sga, junk, ssr, x1_s[gt * 128:(gt + 1) * 128, :], dt_("x1", gt))

            run2([fullC_gen(i) for i in range(len(workC))])

    def out_proj_residual(*a):
        exhaust(out_proj_gen(*a))

    def out_proj_gen(mb, mT, WO, Gm, xt, tmpo, junk, ssr, out_ap, out_dt):
        ps = k.ps()
        pb = bank_bf(ps)
        for kc in range(8):
            k.tr(ps, pb[:, kc * 128:(kc + 1) * 128], mb[:, kc * 128:(kc + 1) * 128], identb.ap, R=[mb, identb])
        k.cp("act", mT, mT.ap, pb, R=[ps])
        yield
        py = [k.ps(), k.ps()]
        for nb in range(2):
            for kc in range(8):
                k.mm(py[nb], py[nb].ap, mT[:, kc * 128:(kc + 1) * 128], WO[:, kc * 1024 + nb * 512:kc * 1024 + (nb + 1) * 512],
                     start=(kc == 0), stop=(kc == 7), R=[mT, WO])
        ss = ssr.next()
        rstd = rstd_of(ss, junk, [(py[0], py[0].ap), (py[1], py[1].ap)], D, 1e-6)
        for nb in range(2):
            cs = slice(nb * 512, (nb + 1) * 512)
            k.stt("dve", tmpo, tmpo[:, cs], py[nb].ap, rstd, Gm[:, cs], ALU.mult, ALU.mult, R=[py[nb], ss, Gm])
        yield
        k.tt("pool", tmpo, tmpo.ap, tmpo.ap, xt.ap, ALU.add, R=[tmpo, xt])
        k.store(out_ap, tmpo, tmpo.ap, W=[out_dt] if out_dt is not None else [])

    NTOK = NT * 128
    qT_s = dscr("qT_s", (8, 128, NTOK), BF16)
    kT_s = dscr("kT_s", (8, 128, NTOK), BF16)
    v_s = dscr("v_s", (NT, 128, 1024), BF16)
    sg_s = dscr("sg_s", (NT, 128, 1024), BF16)

    def layer1():
        with k.scope():
            lamt = k.sb([128, 4], F32, "lamt")
            with k.scope():
                lq = [k.sb([128, 64], F32, f"lq{i}") for i in range(4)]
                for t, nm in zip(lq, ("c_lq1", "c_lk1", "c_lq2", "c_lk2")):
                    load_bc(t, I[nm][0:1, :])
                k.tt("dve", lq[0], lq[0].ap, lq[0].ap, lq[1].ap, ALU.mult, R=[lq[0], lq[1]])
                k.tt("dve", lq[2], lq[2].ap, lq[2].ap, lq[3].ap, ALU.mult, R=[lq[2], lq[3]])
                k.red("dve", lamt, lamt[:, 0:1], lq[0].ap, R=[lq[0]])
                k.red("dve", lamt, lamt[:, 1:2], lq[2].ap, R=[lq[2]])
                k.act(lamt, lamt[:, 0:2], lamt[:, 0:2], AF.Exp, R=[lamt])
                k.tt("dve", lamt, lamt[:, 2:3], lamt[:, 0:1], lamt[:, 1:2], ALU.subtract, R=[lamt])
                k.ts("dve", lamt, lamt[:, 2:3], lamt[:, 2:3], LAM_INIT, None, ALU.add, R=[lamt])
                k.ts("dve", lamt, lamt[:, 3:4], lamt[:, 2:3], -1.0, None, ALU.mult, R=[lamt])
            subl = k.sb([128, 128], F32, "subl")
            load_bc(subl, I["c_subln"][0:1, :])
            k.ts("dve", subl, subl.ap, subl.ap, 1.0 - LAM_INIT, None, ALU.mult, R=[subl])
            l1_phase1()
            if stop_after == "L1P1":
                return
            l1_phase23(lamt, subl)

    def l1_phase1():
        with k.scope():
            W1 = k.sb([128, 8 * 4096], BF16, "W1L1")
            with k.scope():
                stg = Ring(k, 2, [128, 1024], F32, "wstg1")
                for q4 in range(4):
                    load_cols2([(W1, (lambda kc, q4=q4: W1[:, kc * 4096 + q4 * 1024:kc * 4096 + (q4 + 1) * 1024]), None, (0, 1024))],
                               I["w_in1"], q4 * 1024, 1024, stg)
                modulation_to_dram(1)
            Sh = k.sb([128, D], F32, "Sh1")
            A = k.sb([128, D], F32, "A1")
            xr = Ring(k, 3, [128, D], F32, "xr1")
            tmpf = k.sb([128, D], F32, "tmpf1")
            hbr = Ring(k, 3, [128, D], BF16, "hb1")
            junk = k.sb([128, D], BF16, "junk1")
            ssr = Ring(k, 3, [128, 8], F32, "ssr1")
            hTr = Ring(k, 3, [128, 1024], BF16, "hTr1")
            ropr = Ring(k, 3, [128, 256], F32, "ropr1")
            kbr = Ring(k, 2, [128, 512], BF16, "kbr1")
            t1r = Ring(k, 2, [128, 512], F32, "t1r1")
            qkr = Ring(k, 4, [128, 512], BF16, "qkr1")
            vbr = Ring(k, 3, [128, 1024], BF16, "vbr1")
            vfr = Ring(k, 2, [128, 1024], F32, "vfr1")
            sgr = Ring(k, 3, [128, 1024], BF16, "sgr1")
            sgf = k.sb([128, 512], F32, "sgf1")
            myidx = k.sb([128, 4], mybir.dt.int32, "myidx")
            k.load(myidx, myidx.ap, I["myrows"])
            x1_all = [dt_("x1", NPS * PT + c) for c in range(ST)]

            def p1_tile(sq, c, gt_x, my_i, do_q, do_kv, do_g, rope_src, gt_q):
                xt = xr.next()
                if gt_x is not None:
                    k.load(xt, xt.ap, x1_s[gt_x * 128:(gt_x + 1) * 128, :], R=[dt_("x1", gt_x)])
                else:
                    k.gather(xt, xt.ap, x1_s, myidx, myidx[:, my_i:my_i + 1], R=x1_all)
                ss_t = ssr.next()
                rstd = rstd_of(ss_t, junk, [(xt, xt.ap)], D, 1e-6)
                k.stt("dve", tmpf, tmpf.ap, xt.ap, rstd, A.ap, ALU.mult, ALU.mult, R=[xt, ss_t, A])
                hb = hbr.next()
                k.tt("pool", hb, hb.ap, tmpf.ap, Sh.ap, ALU.add, R=[tmpf, Sh])
                ps = k.ps()
                pb = bank_bf(ps)
                for kc in range(8):
                    k.tr(ps, pb[:, kc * 128:(kc + 1) * 128], hb[:, kc * 128:(kc + 1) * 128], identb.ap, R=[hb, identb])
                hT = hTr.next()
                k.cp("act", hT, hT.ap, pb, R=[ps])
                yield
                rp = None
                if rope_src is not None:
                    rp = ropr.next()
                    k.load(rp, v3(rp.ap, 2), rope_src)
                for do_, coff, dst_s, nm, gdst in ((do_q, 0, qT_s, "qT", gt_q), (do_kv, 1024, kT_s, "kT", gt_x)):
                    if not do_:
                        continue
                    for hg in range(2):
                        ps = k.ps()
                        for hh in range(4):
                            h = hg * 4 + hh
                            for kc in range(8):
                                k.mm(ps, ps[:, hh * 128:(hh + 1) * 128], W1[:, kc * 4096 + coff + h * 128:kc * 4096 + coff + (h + 1) * 128],
                                     hT[:, kc * 128:(kc + 1) * 128], start=(kc == 0), stop=(kc == 7), R=[W1, hT])
                        qk = qkr.next()
                        if rp is not None:
                            rope_apply(ps, ps.ap, qk, qk.ap, rp, kbr, t1r, 4)
                        else:
                            k.cp("act", qk, qk.ap, ps.ap, R=[ps])
                        k.store(dst_s[hg * 4:(hg + 1) * 4, :, gdst * 128:(gdst + 1) * 128].rearrange("h p t -> p h t"), qk, v3(qk.ap, 4),
                                W=[dt_(nm, (gdst, hg))])
                        yield
                if do_kv:
                    vb = vbr.next()
                    vf = vfr.next()
                    for nb in range(2):
                        ps = k.ps()
                        for kc in range(8):
                            k.mm(ps, ps.ap, hT[:, kc * 128:(kc + 1) * 128], W1[:, kc * 4096 + 2048 + nb * 512:kc * 4096 + 2048 + (nb + 1) * 512],
                                 start=(kc == 0), stop=(kc == 7), R=[hT, W1])
                        k.cp("dve", vb, vb[:, nb * 512:(nb + 1) * 512], ps.ap, R=[ps])
                        if not sq["sample"]:
                            k.cp("act", vf, vf[:, nb * 512:(nb + 1) * 512], ps.ap, R=[ps])
                        yield
                    k.store(v_s[gt_x], vb, vb.ap, W=[dt_("v", gt_x)])
                    if not sq["sample"]:
                        k.store(O["ncv"][sq["s"], c * 128:(c + 1) * 128, :], vf, vf.ap)
                        kf = vfr.next()
                        for nb in range(2):
                            ps = k.ps()
                            for kc in range(8):
                                k.mm(ps, ps.ap, hT[:, kc * 128:(kc + 1) * 128], W1[:, kc * 4096 + 1024 + nb * 512:kc * 4096 + 1024 + (nb + 1) * 512],
                                     start=(kc == 0), stop=(kc == 7), R=[hT, W1])
                            k.cp("act", kf, kf[:, nb * 512:(nb + 1) * 512], ps.ap, R=[ps])
                            yield
                        k.store(O["nck"][sq["s"], c * 128:(c + 1) * 128, :], kf, kf.ap)
                if do_g:
                    sg = sgr.next()
                    for nb in range(2):
                        ps = k.ps()
                        for kc in range(8):
                            k.mm(ps, ps.ap, hT[:, kc * 128:(kc + 1) * 128], W1[:, kc * 4096 + 3072 + nb * 512:kc * 4096 + 3072 + (nb + 1) * 512],
                                 start=(kc == 0), stop=(kc == 7), R=[hT, W1])
                        k.act(sg, sg[:, nb * 512:(nb + 1) * 512], ps.ap, AF.Silu, R=[ps])
                        yield
                    k.store(sg_s[gt_q], sg, sg.ap, W=[dt_("sg", gt_q)])

            curg = None
            for sq in seqs:
                if sq["g"] != curg:
                    curg = sq["g"]
                    k.load(Sh, Sh.ap, mod_s[1, curg, 0], R=[mod_t])
                    k.load(A, A.ap, mod_s[1, curg, 1], R=[mod_t])
                tiles = []
                for c in range(sq["n"]):
                    gt = sq["t0"] + c
                    if sq["sample"]:
                        tiles.append(p1_tile(sq, c, gt, None, False, True, False, I["rope"][:, :, c * 128:(c + 1) * 128], None))
                    else:
                        tiles.append(p1_tile(sq, c, gt, None, True, True, True, None, gt))
                if sq["sample"]:
                    for i in range(MYT):
                        tiles.append(p1_tile(sq, i, None, i, True, False, True, I["rope_my"][:, :, i * 128:(i + 1) * 128], sq["t0"] + i))
                run2(tiles, 3)

    def l1_phase23(lamt, subl):
        with k.scope():
            WO = k.sb([128, 8 * 1024], BF16, "WO1")
            with k.scope():
                stg = Ring(k, 2, [128, 1024], F32, "wstg2")
                load_cols2([(WO, lambda kc: WO[:, kc * 1024:(kc + 1) * 1024], None, (0, 1024))], I["w_out"][1], 0, 1024, stg)
            Gm = k.sb([128, D], F32, "Gm1")
            osb = k.sb([128, MYT * 1024], BF16, "osb")
            KTr = Ring(k, 2, [128, 2304], BF16, "KTr")
            Vr = Ring(k, 2, [128, 18 * 129], BF16, "Vr")
            for t in Vr.t:
                k.ms("pool", t, t.ap, 1.0)
            cfr = Ring(k, 2, [128, 128], F32, "cfr")
            cbr = Ring(k, 2, [128, 128], BF16, "cbr")
            qr = Ring(k, 2, [128, 256], BF16, "qr")
            ptr = Ring(k, 4, [128, 512], BF16, "ptr1")
            Osb = [k.sb([128, 2 * 129], F32, f"Osb{z}") for z in range(2)]
            dn = k.sb([128, 8], F32, "dn1")
            ofa = k.sb([128, MYT * 1024], F32, "ofa")
            ofv = ofa.ap.rearrange("p (t h e) -> p t h e", t=MYT, h=8)
            ofs = k.sb([128, MYT * 1024], F32, "ofs")
            of2 = k.sb([128, 256], F32, "of2")
            of2v = v3(of2.ap, 2)
            sgrp = k.sb([128, 96], F32, "sgrp")
            mhalf32 = k.sb([128, 32], F32, "mhalf32")
            k.ms("dve", mhalf32, mhalf32.ap, -0.5)
            junk = k.sb([128, 512], BF16, "junk2")
            ssr = Ring(k, 3, [128, 8], F32, "ssr2")
            sgl = Ring(k, 3, [128, 1024], BF16, "sgl")
            xr = Ring(k, 3, [128, D], F32, "xr2")
            mbr = Ring(k, 3, [128, 1024], BF16, "mb1")
            mTr = Ring(k, 3, [128, 1024], BF16, "mT1")
            tmpor = Ring(k, 3, [128, 1024], F32, "tmpo1")
            (psO0, psO1), oidx = k.pin(2)
            psO = (psO0, psO1)
            myidx = k.sb([128, 4], mybir.dt.int32, "myidx2")
            k.load(myidx, myidx.ap, I["myrows"])
            x1_all = [dt_("x1", NPS * PT + c) for c in range(ST)]
            curg = None
            for sq in seqs:
                n = sq["n"]
                tok0 = sq["t0"] * 128
                nblk = n + (2 if sq["sample"] else 0)
                nq = MYT if sq["sample"] else n
                items = []
                for h in range(8):
                    hd = dict(h=h, KT=None, V=None, Vv=None, qT={})
                    for qg in range(nq // 2):
                        for z in range(2):
                            for kp in range(nblk // 2):
                                items.append((hd, qg, z, kp))
                Sb, PTb = {}, {}

                def stA(i):
                    hd, qg, z, kp = items[i]
                    h = hd["h"]
                    if hd["KT"] is None:
                        KT = KTr.next()
                        V = Vr.next()
                        Vv = V.ap.rearrange("p (b e) -> p b e", b=18)
                        hd["KT"], hd["V"], hd["Vv"] = KT, V, Vv
                        if sq["sample"]:
                            for blk in range(2):
                                cf = cfr.next()
                                k.load(cf, cf.ap, I["cck"][blk * 128:(blk + 1) * 128, h * 128:(h + 1) * 128])
                                cb = cbr.next()
                                k.cp("dve", cb, cb.ap, cf.ap, R=[cf])
                                ps = k.ps()
                                pb = bank_bf(ps)
                                k.tr(ps, pb[:, 0:128], cb.ap, identb.ap, R=[cb, identb])
                                k.cp("act", KT, KT[:, blk * 128:(blk + 1) * 128], pb[:, 0:128], R=[ps])
                                cf = cfr.next()
                                k.load(cf, cf.ap, I["ccv"][blk * 128:(blk + 1) * 128, h * 128:(h + 1) * 128])
                                k.cp("dve", V, Vv[:, blk, 0:128], cf.ap, R=[cf])
                            koff = 256
                        else:
                            koff = 0
                        k.load(KT, KT[:, koff:koff + n * 128], kT_s[h, :, tok0:tok0 + n * 128],
                               R=[dt_("kT", (sq["t0"] + c, h // 4)) for c in range(n)])
                        for c4 in range(0, n, 4):
                            m4 = min(4, n - c4)
                            k.load(V, Vv[:, koff // 128 + c4:koff // 128 + c4 + m4, 0:128],
                                   v_s[sq["t0"] + c4:sq["t0"] + c4 + m4, :, h * 128:(h + 1) * 128].rearrange("c p e -> p c e"),
                                   R=[dt_("v", sq["t0"] + c4 + c) for c in range(m4)])
                    if qg not in hd["qT"]:
                        qT = qr.next()
                        k.load(qT, qT.ap, qT_s[h, :, tok0 + qg * 256:tok0 + (qg + 1) * 256],
                               R=[dt_("qT", (sq["t0"] + qg * 2 + i_, h // 4)) for i_ in range(2)])
                        hd["qT"][qg] = qT
                    qT = hd["qT"][qg]
                    KT = hd["KT"]
                    P_ = slice(64 * z, 64 * z + 64)
                    pS = k.ps()
                    for i_ in range(2):
                        kb = kp * 2 + i_
                        k.mm(pS, pS[:, i_ * 256:(i_ + 1) * 256], KT[P_, kb * 128:(kb + 1) * 128], qT[P_, :], R=[KT, qT])
                    Sb[i] = pS

                def stB(i):
                    pS = Sb.pop(i)
                    PT_ = ptr.next()
                    k.act(PT_, PT_.ap, pS.ap, AF.Exp, R=[pS], scale=0.125)
                    PTb[i] = PT_

                def stC(i):
                    hd, qg, z, kp = items[i]
                    h = hd["h"]
                    PT_ = PTb.pop(i)
                    V, Vv = hd["V"], hd["Vv"]
                    for i_ in range(2):
                        kb = kp * 2 + i_
                        for qs in range(2):
                            k.mm(psO[qs], psO[qs][:, 0:129], PT_[:, i_ * 256 + qs * 128:i_ * 256 + (qs + 1) * 128], Vv[:, kb, :],
                                 start=(kb == 0), stop=(kb == nblk - 1), R=[PT_, V])
                    if kp != nblk // 2 - 1:
                        return
                    for qs in range(2):
                        k.cp("act" if qs == 0 else "dve", Osb[z], Osb[z][:, qs * 129:(qs + 1) * 129], psO[qs][:, 0:129], R=[psO[qs]])
                    if z != 1:
                        return
                    O0 = Osb[0].ap.rearrange("p (q e) -> p q e", q=2)
                    O1 = Osb[1].ap.rearrange("p (q e) -> p q e", q=2)
                    k.rcp(dn, dn[:, 0:2], O0[:, :, 128], R=[Osb[0]])
                    k.rcp(dn, dn[:, 2:4], O1[:, :, 128], R=[Osb[1]])
                    k.ts("dve", dn, dn[:, 4:6], dn[:, 2:4], lamt[:, 3:4], None, ALU.mult, R=[dn, lamt])
                    dst = ofv[:, qg * 2:qg * 2 + 2, h, :]
                    k.tt("dve", ofa, dst, O0[:, :, 0:128], dn[:, 0:2].unsqueeze(2).to_broadcast([128, 2, 128]), ALU.mult, R=[Osb[0], dn])
                    k.tt("dve", of2, of2v, O1[:, :, 0:128], dn[:, 4:6].unsqueeze(2).to_broadcast([128, 2, 128]), ALU.mult, R=[Osb[1], dn])
                    k.tt("dve", ofa, dst, dst, of2v, ALU.add, R=[ofa, of2])

                LA = 2
                for i in range(-LA, len(items)):
                    if i + LA < len(items):
                        stA(i + LA)
                    if i >= 0:
                        stB(i)
                        stC(i)
                ng = nq * 8
                fl = ofa[:, 0:ng * 128]
                k.tt("dve", ofs, ofs[:, 0:ng * 128], fl, fl, ALU.mult, R=[ofa])
                k.red("dve", sgrp, sgrp[:, 0:ng], v3(ofs[:, 0:ng * 128], ng), R=[ofs])
                k.ts("dve", sgrp, sgrp[:, 32:32 + ng], sgrp[:, 0:ng], 1.0 / 128, 1e-5, ALU.mult, ALU.add, R=[sgrp])
                k.tt("gp", sgrp, sgrp[:, 64:64 + ng], sgrp[:, 32:32 + ng], mhalf32[:, 0:ng], ALU.pow, R=[sgrp, mhalf32])
                k.tt("dve", ofs, v3(ofs[:, 0:ng * 128], ng), v3(fl, ng), sgrp[:, 64:64 + ng].unsqueeze(2).to_broadcast([128, ng, 128]),
                     ALU.mult, R=[ofa, sgrp])
                k.tt("dve", osb, v3(osb[:, 0:ng * 128], ng), v3(ofs[:, 0:ng * 128], ng), subl.ap.unsqueeze(1).to_broadcast([128, ng, 128]),
                     ALU.mult, R=[ofs, subl])
                if sq["g"] != curg:
                    curg = sq["g"]
                    k.load(Gm, Gm.ap, mod_s[1, curg, 2], R=[mod_t])
                def p3_gen(c):
                    gt = sq["t0"] + c
                    sg = sgl.next()
                    k.load(sg, sg.ap, sg_s[gt], R=[dt_("sg", gt)])
                    xt = xr.next()
                    if sq["sample"]:
                        k.gather(xt, xt.ap, x1_s, myidx, myidx[:, c:c + 1], R=x1_all)
                    else:
                        k.load(xt, xt.ap, x1_s[gt * 128:(gt + 1) * 128, :], R=[dt_("x1", gt)])
                    mb, mT, tmpo = mbr.next(), mTr.next(), tmpor.next()
                    k.tt("dve", mb, mb.ap, osb[:, c * 1024:(c + 1) * 1024], sg.ap, ALU.mult, R=[osb, sg])
                    yield
                    if sq["sample"]:
                        out_ap = O["ys"][c * 128:(c + 1) * 128, :]
                    else:
                        out_ap = O["yp"][sq["s"] * 256 + c * 128:sq["s"] * 256 + (c + 1) * 128, :]
                    yield from out_proj_gen(mb, mT, WO, Gm, xt, tmpo, junk, ssr, out_ap, None)

                run2([p3_gen(c) for c in range(nq)], 3)
            k.unpin(oidx)

    layer0()
    if stop_after is None:
        layer1()
    k.finish()
    return k


def make_in_maps(inp):
    cst, rope = _consts()
    f = lambda a: np.ascontiguousarray(np.asarray(a, dtype=np.float32))
    maps = []
    for core in range(8):
        b = core % 2
        m = {
            "xp": f(inp["x_prompt"][core * NPS:(core + 1) * NPS]).reshape(NPS * 256, D),
            "xs": f(inp["x_sample"][b]),
            "cak": f(inp["cache_a_k"][b, 0]).reshape(256, 128),
            "cav": f(inp["cache_a_v"][b, 0]).reshape(256, 128),
            "st0": f(inp["state_rwkv"][b, 0]),
            "cck": f(inp["cache_c_k"][b, 0]).reshape(256, 1024),
            "ccv": f(inp["cache_c_v"][b, 0]).reshape(256, 1024),
            "cvec": f(np.stack([np.asarray(inp["c_ctx"]), np.asarray(inp["c"])[b]], 0)),
            "ada_w": f(inp["ada_w"]), "ada_b": f(inp["ada_b"]), "norm_pre": f(inp["norm_pre"]), "norm_post": f(inp["norm_post"]),
            "w_out": f(inp["w_out"]), "w_in0": f(inp["even_w_in"][0]), "a_sink": f(inp["a_sink"]), "b_mu": f(inp["b_mu"]),
            "b_w0": f(inp["b_w0"][0]), "b_w2": f(inp["b_w2"][0]), "b_a0": f(inp["b_a0"][0]), "b_a2": f(inp["b_a2"][0]),
            "b_kk": f(inp["b_kk"]), "b_ka": f(inp["b_ka"]), "b_rk": f(inp["b_rk"]), "b_ln_w": f(inp["b_ln_w"]), "b_ln_b": f(inp["b_ln_b"]),
            "w_in1": f(inp["odd_w_in"][0]), "c_lq1": f(inp["c_lq1"]), "c_lk1": f(inp["c_lk1"]), "c_lq2": f(inp["c_lq2"]),
            "c_lk2": f(inp["c_lk2"]), "c_subln": f(inp["c_subln"]), "cst": cst, "rope": rope,
            "rope_my": np.ascontiguousarray(rope[:, :, (core // 2) * MYT * 128:(core // 2 + 1) * MYT * 128]),
            "myrows": np.ascontiguousarray(
                (NPS * 256 + (core // 2) * MYT * 128 + np.arange(MYT)[None, :] * 128 + np.arange(128)[:, None]).astype(np.int32)),
        }
        maps.append(m)
    return maps


_CACHE = {}


def kernel(**inputs):
    inp = {k_: np.asarray(v) for k_, v in inputs.items()}
    if "k" not in _CACHE:
        k1 = build()
        _CACHE["k"] = build(plan=frozenset(k1.needed))
    kk = _CACHE["k"]
    maps = make_in_maps(inp)
    res = run_bass_kernel_spmd(kk.nc, maps, core_ids=list(range(8)))
    R = res.results
    B = inp["x_prompt"].shape[0]
    y_p = np.concatenate([R[c]["yp"].reshape(NPS, 256, D) for c in range(8)], 0).astype(np.float32)
    y_s = np.zeros((2, 2048, D), np.float32)
    for c in range(8):
        y_s[c % 2, (c // 2) * MYT * 128:(c // 2 + 1) * MYT * 128] = R[c]["ys"]
    nak = np.concatenate([R[c]["nak"] for c in range(8)], 0).reshape(B, 1, 256, 2, 64).astype(np.float32)
    nav = np.concatenate([R[c]["nav"] for c in range(8)], 0).reshape(B, 1, 256, 2, 64).astype(np.float32)
    nst = np.concatenate([R[c]["nst"] for c in range(8)], 0).reshape(B, 1, 2, 8, 64, 64).astype(np.float32)
    nck = np.concatenate([R[c]["nck"] for c in range(8)], 0).reshape(B, 1, 256, 8, 128).astype(np.float32)
    ncv = np.concatenate([R[c]["ncv"] for c in range(8)], 0).reshape(B, 1, 256, 8, 128).astype(np.float32)
    return (y_p, y_s, nak, nav, nst, nck, ncv)
```
